# Optimizing a Trainium2 kernel written in Bass

```python
import math
import jax, jax.numpy as jnp
from jax import lax
import numpy as np

D_MODEL = 2048
BATCH = 4
SEQ = 2048
DEPTH = 2

NORM_EPS = 1e-6
N_BRANCH = 4
BRANCH_WIDTH = D_MODEL // 2

NUM_BUCKETS = 32
MAX_DISTANCE = 128

A_QK_DIM = 64
A_V_DIM = 2 * A_QK_DIM
A_HEADS = BRANCH_WIDTH // A_V_DIM
A_WIDTH = A_HEADS * A_V_DIM
Q_BLOCK = 128

B_HEAD_DIM = 64
B_HEADS = BRANCH_WIDTH // B_HEAD_DIM
B_KV_HEADS = B_HEADS // 4
B_WIDTH = B_HEADS * B_HEAD_DIM
WINDOW = 128

C_HEAD_DIM = 64
C_HEADS = BRANCH_WIDTH // C_HEAD_DIM
C_WIDTH = C_HEADS * C_HEAD_DIM
C_GROUPS = 2
C_STATE = 128
C_CONV = 4
CHUNK = 128
C_CONV_CH = C_WIDTH + 2 * C_GROUPS * C_STATE

D_WIDTH = BRANCH_WIDTH
D_CONV = 31

IN_SPLITS = (
    2 * A_HEADS * A_QK_DIM, 2 * A_HEADS * A_QK_DIM, A_WIDTH, A_WIDTH,
    B_WIDTH, B_KV_HEADS * B_HEAD_DIM, B_KV_HEADS * B_HEAD_DIM, B_WIDTH,
    C_CONV_CH, C_HEADS, C_WIDTH,
    2 * D_WIDTH, D_WIDTH,
    N_BRANCH * D_MODEL,
)
N_IN = sum(IN_SPLITS)

kernel_name = "hybrid_diffattn_swa_ssd_conformer"


def _split_points():
    pts, acc = [], 0
    for n in IN_SPLITS[:-1]:
        acc += n
        pts.append(acc)
    return pts


def rms_norm(x, w):
    xf = x.astype(jnp.float32)
    y = xf * lax.rsqrt(jnp.mean(xf * xf, axis=-1, keepdims=True) + NORM_EPS)
    return (y * w.astype(jnp.float32)).astype(x.dtype)


def layer_norm(x, w, b):
    xf = x.astype(jnp.float32)
    mu = jnp.mean(xf, axis=-1, keepdims=True)
    xc = xf - mu
    var = jnp.mean(xc * xc, axis=-1, keepdims=True)
    y = xc * lax.rsqrt(var + NORM_EPS) * w.astype(jnp.float32) + b.astype(jnp.float32)
    return y.astype(x.dtype)


def causal_depthwise_conv(x, w, b):
    k = w.shape[0]
    y = lax.conv_general_dilated(
        x, w[:, None, :].astype(x.dtype), window_strides=(1,), padding=[(k - 1, 0)],
        dimension_numbers=("NWC", "WIO", "NWC"), feature_group_count=x.shape[-1])
    return y + b.astype(x.dtype)


def t5_bucket(dist):
    n = jnp.maximum(dist, 0)
    max_exact = NUM_BUCKETS // 2
    nf = jnp.maximum(n, 1).astype(jnp.float32)
    large = max_exact + (jnp.log(nf / max_exact) / math.log(MAX_DISTANCE / max_exact)
                         * (NUM_BUCKETS - max_exact)).astype(jnp.int32)
    large = jnp.minimum(large, NUM_BUCKETS - 1)
    return jnp.where(n < max_exact, n, large)


def diff_attention(q, k, v, lam, lam_init, subln_w, bias_table):
    b, s = q.shape[:2]
    nblk = s // Q_BLOCK
    q_blocks = (q * (A_QK_DIM ** -0.5)).reshape(
        b, nblk, Q_BLOCK, A_HEADS, 2, A_QK_DIM).swapaxes(0, 1)
    k_pos = jnp.arange(s)

    def block(args):
        qb, start = args
        q_pos = start + jnp.arange(Q_BLOCK)
        dist = q_pos[:, None] - k_pos[None, :]
        bias = bias_table[t5_bucket(dist)].astype(jnp.float32)
        bias = jnp.where((dist >= 0)[..., None], bias, -jnp.inf).transpose(2, 0, 1)
        logits = jnp.einsum("bqhmd,bkhmd->bhmqk", qb, k).astype(jnp.float32) + bias[:, None]
        p = jax.nn.softmax(logits, axis=-1)
        a = (p[:, :, 0] - lam * p[:, :, 1]).astype(v.dtype)
        return jnp.einsum("bhqk,bkhe->bqhe", a, v)

    starts = jnp.arange(nblk) * Q_BLOCK
    out = lax.map(block, (q_blocks, starts))
    out = out.swapaxes(0, 1).reshape(b, s, A_HEADS, A_V_DIM)
    out = rms_norm(out, subln_w) * (1.0 - lam_init)
    return out.reshape(b, s, A_WIDTH)


def sliding_window_attention(q, k, v, sinks, bias_table):
    b, s = q.shape[:2]
    nb = s // WINDOW
    g = B_HEADS // B_KV_HEADS
    qb = (q * (B_HEAD_DIM ** -0.5)).reshape(b, nb, WINDOW, B_KV_HEADS, g, B_HEAD_DIM)

    def band(t):
        tb = t.reshape(b, nb, WINDOW, B_KV_HEADS, B_HEAD_DIM)
        prev = jnp.concatenate([jnp.zeros_like(tb[:, :1]), tb[:, :-1]], axis=1)
        return jnp.concatenate([prev, tb], axis=2)

    kb, vb = band(k), band(v)
    qi = jnp.arange(WINDOW)[:, None]
    kj = jnp.arange(2 * WINDOW)[None, :]
    dist = WINDOW + qi - kj
    k_pos = jnp.arange(nb)[:, None, None] * WINDOW - WINDOW + kj
    valid = (dist >= 0) & (dist < WINDOW) & (k_pos >= 0)
    bias = bias_table[t5_bucket(dist)].astype(jnp.float32)
    bias = bias.transpose(2, 0, 1).reshape(B_KV_HEADS, g, WINDOW, 2 * WINDOW)
    logits = jnp.einsum("bnqhgd,bnkhd->bnhgqk", qb, kb).astype(jnp.float32) + bias
    logits = jnp.where(valid[:, None, None], logits, -jnp.inf)
    sink = sinks.reshape(B_KV_HEADS, g).astype(jnp.float32)[:, :, None, None]
    sink = jnp.broadcast_to(sink, logits.shape[:-1] + (1,))
    p = jax.nn.softmax(jnp.concatenate([logits, sink], axis=-1), axis=-1)[..., :-1]
    out = jnp.einsum("bnhgqk,bnkhd->bnqhgd", p.astype(vb.dtype), vb)
    return out.reshape(b, s, B_WIDTH)


def ssd_mixer(xbc, dt_raw, z, conv_w, conv_b, dt_bias, a_log, d_skip, norm_w):
    b, s, _ = xbc.shape
    xbc = jax.nn.silu(causal_depthwise_conv(xbc, conv_w, conv_b))
    xs, bm, cm = jnp.split(xbc, [C_WIDTH, C_WIDTH + C_GROUPS * C_STATE], axis=-1)
    nc = s // CHUNK
    e = C_HEADS // C_GROUPS
    x = xs.reshape(b, nc, CHUNK, C_GROUPS, e, C_HEAD_DIM)
    bm = bm.reshape(b, nc, CHUNK, C_GROUPS, C_STATE)
    cm = cm.reshape(b, nc, CHUNK, C_GROUPS, C_STATE)
    dt = jax.nn.softplus((dt_raw + dt_bias).astype(jnp.float32))
    a = -jnp.exp(a_log.astype(jnp.float32))
    dt_c = dt.reshape(b, nc, CHUNK, C_GROUPS, e)
    a_cum = jnp.cumsum((dt_c * a.reshape(C_GROUPS, e)).transpose(0, 3, 4, 1, 2), axis=-1)
    xdt = x * dt_c[..., None].astype(x.dtype)
    seg = a_cum[..., :, None] - a_cum[..., None, :]
    causal = jnp.tril(jnp.ones((CHUNK, CHUNK), dtype=bool))
    decay = jnp.exp(jnp.where(causal, seg, -jnp.inf)).astype(x.dtype)
    cb = jnp.einsum("bclgn,bcsgn->bgcls", cm, bm)
    y_diag = jnp.einsum("bgcls,bgecls,bcsgep->bclgep", cb, decay, xdt)
    decay_states = jnp.exp(a_cum[..., -1:] - a_cum).astype(x.dtype)
    states = jnp.einsum("bclgn,bgecl,bclgep->bcgepn", bm, decay_states, xdt)
    chunk_decay = jnp.exp(a_cum[..., -1]).astype(x.dtype)

    def step(h, inp):
        dec, st = inp
        return dec[..., None, None] * h + st, h

    h0 = jnp.zeros_like(states[:, 0])
    _, prev = lax.scan(step, h0, (jnp.moveaxis(chunk_decay, -1, 0), jnp.moveaxis(states, 1, 0)))
    prev = jnp.moveaxis(prev, 0, 1)
    y_off = jnp.einsum("bclgn,bcgepn,bgecl->bclgep", cm, prev, jnp.exp(a_cum).astype(x.dtype))
    y = (y_diag + y_off + x * d_skip.reshape(C_GROUPS, e)[..., None].astype(x.dtype))
    y = y.reshape(b, s, C_WIDTH)
    yg = (y * jax.nn.silu(z)).reshape(b, s, C_GROUPS, C_WIDTH // C_GROUPS)
    return rms_norm(yg, norm_w.reshape(C_GROUPS, -1)).reshape(b, s, C_WIDTH)


def conformer_conv(glu_in, conv_w, conv_b, ln_w, ln_b):
    val, gate = jnp.split(glu_in, 2, axis=-1)
    h = val * jax.nn.sigmoid(gate)
    h = causal_depthwise_conv(h, conv_w, conv_b)
    h = layer_norm(h, ln_w, ln_b)
    return jax.nn.silu(h)


def setup_inputs(seed: int = 0) -> dict:
    key = jax.random.key(seed)
    ks = jax.random.split(key, 24)
    f32 = jnp.float32
    nrm = lambda k, shape, sc: jax.random.normal(k, shape, f32) * sc
    dt = jnp.exp(jax.random.uniform(ks[8], (DEPTH, C_HEADS), f32,
                                    minval=math.log(1e-3), maxval=math.log(1e-1)))
    return {
        "x": nrm(ks[0], (BATCH, SEQ, D_MODEL), 1.0),
        "norm_w": 1.0 + nrm(ks[1], (DEPTH, D_MODEL), 0.02),
        "w_in": nrm(ks[2], (DEPTH, D_MODEL, N_IN), D_MODEL ** -0.5),
        "diff_lambda": nrm(ks[3], (DEPTH, 4, A_QK_DIM), 0.1),
        "diff_subln_w": 1.0 + nrm(ks[4], (DEPTH, A_V_DIM), 0.02),
        "swa_sinks": nrm(ks[5], (DEPTH, B_HEADS), 0.5),
        "ssd_conv_w": nrm(ks[6], (DEPTH, C_CONV, C_CONV_CH), C_CONV ** -0.5),
        "ssd_conv_b": nrm(ks[7], (DEPTH, C_CONV_CH), 0.02),
        "ssd_dt_bias": dt + jnp.log(-jnp.expm1(-dt)),
        "ssd_a_log": jnp.log(jax.random.uniform(ks[9], (DEPTH, C_HEADS), f32, minval=1.0, maxval=16.0)),
        "ssd_d": 1.0 + nrm(ks[10], (DEPTH, C_HEADS), 0.02),
        "ssd_norm_w": 1.0 + nrm(ks[11], (DEPTH, C_WIDTH), 0.02),
        "conf_conv_w": nrm(ks[12], (DEPTH, D_CONV, D_WIDTH), D_CONV ** -0.5),
        "conf_conv_b": nrm(ks[13], (DEPTH, D_WIDTH), 0.02),
        "conf_ln_w": 1.0 + nrm(ks[14], (DEPTH, D_WIDTH), 0.02),
        "conf_ln_b": nrm(ks[15], (DEPTH, D_WIDTH), 0.02),
        "w_branch": nrm(ks[16], (DEPTH, N_BRANCH, BRANCH_WIDTH, D_MODEL), BRANCH_WIDTH ** -0.5),
        "w_out": nrm(ks[17], (DEPTH, D_MODEL, D_MODEL), D_MODEL ** -0.5),
        "rel_bias": nrm(ks[18], (NUM_BUCKETS, A_HEADS + B_HEADS), 0.5),
        "final_norm_w": 1.0 + nrm(ks[19], (D_MODEL,), 0.02),
    }


def reference(x, norm_w, w_in, diff_lambda, diff_subln_w, swa_sinks, ssd_conv_w, ssd_conv_b,
              ssd_dt_bias, ssd_a_log, ssd_d, ssd_norm_w, conf_conv_w, conf_conv_b, conf_ln_w,
              conf_ln_b, w_branch, w_out, rel_bias, final_norm_w):
    b, s, _ = x.shape
    bias_a = rel_bias[:, :A_HEADS]
    bias_b = rel_bias[:, A_HEADS:]
    for l in range(DEPTH):
        h = rms_norm(x, norm_w[l])
        proj = jnp.einsum("bsd,dn->bsn", h, w_in[l])
        (aq, ak, av, ag, bq, bk, bv, bg, cxbc, cdt, cz, dglu, dg, mg) = jnp.split(
            proj, _split_points(), axis=-1)

        lam_init = 0.8 - 0.6 * math.exp(-0.3 * l)
        lq1, lk1, lq2, lk2 = (diff_lambda[l, i].astype(jnp.float32) for i in range(4))
        lam = jnp.exp(jnp.sum(lq1 * lk1)) - jnp.exp(jnp.sum(lq2 * lk2)) + lam_init
        ya = diff_attention(aq.reshape(b, s, A_HEADS, 2, A_QK_DIM),
                            ak.reshape(b, s, A_HEADS, 2, A_QK_DIM),
                            av.reshape(b, s, A_HEADS, A_V_DIM),
                            lam, lam_init, diff_subln_w[l], bias_a)
        ya = ya * jax.nn.silu(ag)

        g = B_HEADS // B_KV_HEADS
        yb = sliding_window_attention(bq.reshape(b, s, B_KV_HEADS, g, B_HEAD_DIM),
                                      bk.reshape(b, s, B_KV_HEADS, B_HEAD_DIM),
                                      bv.reshape(b, s, B_KV_HEADS, B_HEAD_DIM),
                                      swa_sinks[l], bias_b)
        yb = yb * jax.nn.silu(bg)

        yc = ssd_mixer(cxbc, cdt, cz, ssd_conv_w[l], ssd_conv_b[l], ssd_dt_bias[l],
                       ssd_a_log[l], ssd_d[l], ssd_norm_w[l])

        yd = conformer_conv(dglu, conf_conv_w[l], conf_conv_b[l], conf_ln_w[l], conf_ln_b[l])
        yd = yd * jax.nn.silu(dg)

        branches = jnp.stack([ya, yb, yc, yd], axis=2)
        up = jnp.einsum("bsir,ird->bsid", branches, w_branch[l])
        gates = jax.nn.sigmoid(mg.reshape(b, s, N_BRANCH, D_MODEL))
        merged = jnp.sum(gates * up, axis=2)
        x = x + jnp.einsum("bsd,de->bse", merged, w_out[l])
    return rms_norm(x, final_norm_w)
```

```python
import math
from contextlib import ExitStack
import numpy as np
import concourse.bass as bass
import concourse.mybir as mybir
from concourse.bass_utils import run_bass_kernel_spmd

F32 = mybir.dt.float32
BF16 = mybir.dt.bfloat16
AF = mybir.ActivationFunctionType
ALU = mybir.AluOpType
AX = mybir.AxisListType

T = 2048
D = 2048
NIN = 20496
DEPTH = 2
EPS = 1e-6
NEG = -30000.0
OFF_AQ, OFF_AK, OFF_AV, OFF_AG = 0, 1024, 2048, 3072
OFF_BQ, OFF_BK, OFF_BV, OFF_BG = 4096, 5120, 5376, 5632
OFF_CX, OFF_CDT, OFF_CZ = 6656, 8192, 8208
OFF_DGLU, OFF_DG, OFF_MG = 9232, 11280, 12304

C_ID, C_AJ, C_TRI, C_TRIU, C_OH, C_MROW, C_SELA, C_OH31, C_ONES = 0, 128, 256, 384, 512, 896, 1280, 1281, 1409
NCONST = 1537


def t5_bucket_np(n):
    n = np.maximum(n, 0)
    nf = np.maximum(n, 1).astype(np.float32)
    large = 16 + (np.log(nf / np.float32(16)) / np.float32(math.log(128 / 16)) * np.float32(16)).astype(np.int32)
    large = np.minimum(large, 31)
    return np.where(n < 16, n, large)


def make_consts():
    c = np.zeros((128, NCONST), np.float32)
    p = np.arange(128)
    c[:, C_ID:C_ID + 128] = np.eye(128)
    c[p, C_AJ + 127 - p] = 1.0
    c[:, C_TRI:C_TRI + 128] = (p[:, None] <= p[None, :])
    c[:, C_TRIU:C_TRIU + 128] = (p[:, None] > p[None, :])
    m = np.arange(384)
    dist = m - 128
    bk = t5_bucket_np(dist)
    for j in range(384):
        if dist[j] >= 0:
            c[bk[j], C_OH + j] = 1.0
    c[0:8, C_MROW:C_MROW + 128] = NEG
    c[8:24, C_MROW:C_MROW + 128] = NEG
    c[8:24, C_MROW + 256:C_MROW + 384] = NEG
    c[0:8, C_SELA] = 1.0
    c[31, C_OH31:C_OH31 + 128] = 1.0
    c[:, C_ONES:C_ONES + 128] = 1.0
    blk = np.zeros((16, 2048), np.float32)
    for e in range(16):
        blk[e, e * 128:(e + 1) * 128] = 1.0
    return c, blk


_UNIQ = [0]


def _sbt(nc, name, shape, dt):
    _UNIQ[0] += 1
    return nc.sbuf_tensor(f"{name}_u{_UNIQ[0]}", list(shape), dt)


class Res:
    __slots__ = ("w", "r", "name")

    def __init__(self, name=""):
        self.w = None
        self.r = {}
        self.name = name


class KB:
    def __init__(self, nc):
        self.nc = nc
        self.eng = {"pe": nc.tensor, "act": nc.scalar, "dve": nc.vector, "pool": nc.gpsimd, "sp": nc.sync}
        self.sems = {}
        self.cnt = {}
        self.seen = {}
        for n in self.eng:
            self.sems[n] = nc.semaphore("e_" + n).__enter__()
            self.cnt[n] = 0
            self.seen[n] = {}
        self.dq = {}
        for q, issuer, ns in (("sp", "sp", 28), ("pool", "pool", 12), ("act", "act", 6)):
            sl = [nc.semaphore(f"d_{q}{i}").__enter__() for i in range(ns)]
            for i, s in enumerate(sl):
                self.sems[("d", q, i)] = s
            self.dq[q] = dict(issuer=issuer, n=ns, uses=[0] * ns, nxt=0)
        self.ninst = 0

    def _deps(self, reads, writes):
        d = {}
        for r in reads:
            if r.w is not None and r.w[1] > d.get(r.w[0], 0):
                d[r.w[0]] = r.w[1]
        for w in writes:
            if w.w is not None and w.w[1] > d.get(w.w[0], 0):
                d[w.w[0]] = w.w[1]
            for k, v in w.r.items():
                if v > d.get(k, 0):
                    d[k] = v
        return d

    def _wait(self, issuer, deps, skip_self=False):
        e = self.eng[issuer]
        seen = self.seen[issuer]
        for k, v in deps.items():
            if skip_self and k == issuer:
                continue
            if seen.get(k, 0) >= v:
                continue
            e.wait_ge(self.sems[k], v)
            seen[k] = v
            self.ninst += 1

    def _mark(self, key, val, reads, writes):
        for r in reads:
            if r.r.get(key, 0) < val:
                r.r[key] = val
        for w in writes:
            w.w = (key, val)
            w.r = {}

    def op(self, e, fn, reads=(), writes=()):
        self._wait(e, self._deps(reads, writes), skip_self=(e == "pe"))
        ins = fn()
        self.cnt[e] += 1
        ins.then_inc(self.sems[e], 1)
        self._mark(e, self.cnt[e], reads, writes)
        self.ninst += 1

    def dma(self, q, out, in_, reads=(), writes=(), **kw):
        Q = self.dq[q]
        issuer = Q["issuer"]
        deps = self._deps(reads, writes)
        slot = Q["nxt"]
        Q["nxt"] = (slot + 1) % Q["n"]
        key = ("d", q, slot)
        if Q["uses"][slot] > 0:
            deps[key] = max(deps.get(key, 0), 16 * Q["uses"][slot])
        self._wait(issuer, deps)
        Q["uses"][slot] += 1
        val = 16 * Q["uses"][slot]
        self.eng[issuer].dma_start(out=out, in_=in_, **kw).then_inc(self.sems[key], 16)
        self._mark(key, val, reads, writes)
        self.ninst += 1

    def barrier(self):
        tot = {}
        for n in self.eng:
            if self.cnt[n] > 0:
                tot[n] = self.cnt[n]
        for q, Q in self.dq.items():
            for i in range(Q["n"]):
                if Q["uses"][i] > 0:
                    tot[("d", q, i)] = 16 * Q["uses"][i]
        for n in self.eng:
            self._wait(n, tot)


class Ctx:
    pass


def build(dbg=None):
    nc = bass.Bass("TRN2", target_bir_lowering=False)
    kb = KB(nc)
    g = Ctx()
    g.nc, g.kb, g.dbg = nc, kb, dbg

    def din(name, shape, dt=F32):
        return nc.dram_tensor(name, list(shape), dt, kind="ExternalInput").ap()

    g.x = din("x", [T, D])
    g.norm_w = din("norm_w", [DEPTH, D])
    g.w_in = din("w_in", [DEPTH, D, NIN])
    g.diff_lambda = din("diff_lambda", [DEPTH, 256])
    g.diff_subln_w = din("diff_subln_w", [DEPTH, 128])
    g.swa_sinks = din("swa_sinks", [DEPTH, 16])
    g.ssd_conv_w = din("ssd_conv_w", [DEPTH, 4, 1536])
    g.ssd_conv_b = din("ssd_conv_b", [DEPTH, 1536])
    g.ssd_dt_bias = din("ssd_dt_bias", [DEPTH, 16])
    g.ssd_a_log = din("ssd_a_log", [DEPTH, 16])
    g.ssd_d = din("ssd_d", [DEPTH, 16])
    g.ssd_norm_w = din("ssd_norm_w", [DEPTH, 1024])
    g.conf_conv_w = din("conf_conv_w", [DEPTH, 31, 1024])
    g.conf_conv_b = din("conf_conv_b", [DEPTH, 1024])
    g.conf_ln_w = din("conf_ln_w", [DEPTH, 1024])
    g.conf_ln_b = din("conf_ln_b", [DEPTH, 1024])
    g.w_branch = din("w_branch", [DEPTH, 4, 1024, D])
    g.w_out = din("w_out", [DEPTH, D, D])
    g.rel_bias = din("rel_bias", [32, 24])
    g.final_norm_w = din("final_norm_w", [1, D])
    g.consts = din("consts", [128, NCONST])
    g.cblk = din("cblk", [16, 2048])
    g.y = nc.dram_tensor("y", [T, D], F32, kind="ExternalOutput").ap()
    g.xres = [nc.dram_tensor(f"xres{i}", [T, D], F32).ap() for i in range(2)]
    g.ybr = nc.dram_tensor("ybr", [4, 1024, T], BF16).ap()
    g.sgd = nc.dram_tensor("sgd", [8192, T], BF16).ap()
    g.mgk = [0]
    g.tdram = nc.dram_tensor("tdram", [24, 384], F32).ap()
    g.dbg_out = {}
    if dbg:
        if "hT" in dbg:
            g.dbg_out["hT"] = nc.dram_tensor("dbg_hT", [128, 16, T], BF16, kind="ExternalOutput").ap()
        if "ybr" in dbg:
            g.dbg_out["ybr"] = nc.dram_tensor("dbg_ybr", [4, 1024, T], BF16, kind="ExternalOutput").ap()
        if "x0" in dbg:
            g.dbg_out["x0"] = nc.dram_tensor("dbg_x0", [T, D], F32, kind="ExternalOutput").ap()

    st = ExitStack()

    def sb(name, shape, dt):
        return st.enter_context(_sbt(nc, name, list(shape), dt))

    g.PS = [st.enter_context(nc.psum_tensor(f"ps{i}", [128, 512], F32)) for i in range(8)]
    g.PSr = [Res(f"ps{i}") for i in range(8)]

    g.cf = sb("cf", [128, NCONST], F32)
    g.cf_r = Res()
    kb.dma("sp", g.cf[:], g.consts, writes=[g.cf_r])
    g.idb = sb("idb", [128, 128], BF16)
    g.onesb = sb("onesb", [128, 128], BF16)
    g.trib = sb("trib", [128, 128], BF16)
    g.cb_r = Res()
    kb.op("dve", lambda: nc.vector.tensor_copy(out=g.idb[:], in_=g.cf[:, C_ID:C_ID + 128]), reads=[g.cf_r], writes=[g.cb_r])
    kb.op("dve", lambda: nc.vector.tensor_copy(out=g.onesb[:], in_=g.cf[:, C_ONES:C_ONES + 128]), reads=[g.cf_r], writes=[g.cb_r])
    kb.op("dve", lambda: nc.vector.tensor_copy(out=g.trib[:], in_=g.cf[:, C_TRI:C_TRI + 128]), reads=[g.cf_r], writes=[g.cb_r])
    g.kc = sb("kc", [128, 8], F32)
    g.kc_r = Res()
    for i, v in enumerate([0.0, 8.0, EPS, 1.0 / 1024, -1.0, 1.0 / 512]):
        kb.op("dve", lambda i=i, v=v: nc.vector.memset(g.kc[:, i:i + 1], v), writes=[g.kc_r])

    srcs = [g.x, g.xres[0], g.xres[1]]
    for l in range(DEPTH):
        with ExitStack() as ls:
            g.ls = ls
            g.hT = ls.enter_context(_sbt(nc, f"hT{l}", [128, 16, T], BF16))
            g.hT_r = [Res(f"hT{tg}") for tg in range(4)]
            phase_norm(g, l, srcs[l])
            g.sgd_r = Res()
            g.mgjobs = mg_job_list(g, l)
            if dbg and "hT" in dbg and l == 0:
                kb.dma("sp", g.dbg_out["hT"], g.hT[:], reads=g.hT_r)
            kb.barrier()
            with ExitStack() as bs:
                g.bt = bs.enter_context(_sbt(nc, f"bt{l}", [128, 2, 24, 128], BF16))
                g.bt_r = Res()
                g.c31 = bs.enter_context(_sbt(nc, f"c31{l}", [128, 24], F32))
                g.c31_r = Res()
                setup_bias(g)
                if not (dbg and "skipA" in dbg):
                    mixer_A(g, l)
                    kb.barrier()
                if not (dbg and "skipB" in dbg):
                    mixer_B(g, l)
                    kb.barrier()
            if not (dbg and "skipC" in dbg):
                mixer_C(g, l)
                kb.barrier()
            if not (dbg and "skipD" in dbg):
                mixer_D(g, l)
                kb.barrier()
        if dbg and "ybr" in dbg and l == 0:
            kb.dma("sp", g.dbg_out["ybr"], g.ybr, reads=[g.ybr_r])
            kb.barrier()
        if dbg and "stop_mix" in dbg:
            break
        phase3(g, l, srcs[l], srcs[l + 1])
        kb.barrier()
        if dbg and "x0" in dbg and l == 0:
            kb.dma("sp", g.dbg_out["x0"], g.xres[0], reads=[g.xres_r])
            kb.barrier()
    if not (dbg and "stop_mix" in dbg):
        phase_final(g, srcs[DEPTH])
    kb.barrier()
    return nc, g


def setup_bias(g):
    nc, kb = g.nc, g.kb
    with ExitStack() as s:
        tab = s.enter_context(_sbt(nc, "tab", [32, 24], F32))
        tt = s.enter_context(_sbt(nc, "ttab", [24, 384], F32))
        csh = s.enter_context(_sbt(nc, "csh", [24, 1], F32))
        hk = s.enter_context(_sbt(nc, "hk", [128, 2, 24, 128], F32))
        tab_r, tt_r, csh_r, hk_r, td_r = Res(), Res(), Res(), Res(), Res()
        kb.dma("sp", tab[:], g.rel_bias, writes=[tab_r])
        ps = g.PS[0]
        kb.op("pe", lambda: nc.tensor.matmul(ps[0:24, 0:384], tab[0:32, 0:24], g.cf[0:32, C_OH:C_OH + 384], start=True, stop=True),
              reads=[tab_r, g.cf_r], writes=[g.PSr[0]])
        kb.op("dve", lambda: nc.vector.tensor_tensor(out=csh[:], in0=ps[0:24, 383:384], in1=g.cf[0:24, C_SELA:C_SELA + 1], op=ALU.mult),
              reads=[g.PSr[0], g.cf_r], writes=[csh_r])
        kb.op("dve", lambda: nc.vector.tensor_scalar(out=tt[:], in0=ps[0:24, 0:384], scalar1=csh[:, 0:1], scalar2=g.kc[0:24, 1:2],
                                                     op0=ALU.subtract, op1=ALU.mult), reads=[g.PSr[0], csh_r, g.kc_r], writes=[tt_r])
        kb.op("dve", lambda: nc.vector.tensor_tensor(out=tt[:], in0=tt[:], in1=g.cf[0:24, C_MROW:C_MROW + 384], op=ALU.add),
              reads=[tt_r, g.cf_r], writes=[tt_r])
        kb.dma("sp", g.tdram, tt[:], reads=[tt_r], writes=[td_r])
        for kind, off in ((0, 1), (1, 129)):
            src = bass.AP(g.tdram.tensor, off, [[1, 128], [384, 24], [1, 128]])
            kb.dma("sp", hk[:, kind], src, reads=[td_r], writes=[hk_r])
        for kind in range(2):
            for hg in range(6):
                pi = 1 + (kind * 6 + hg) % 4
                p = g.PS[pi]
                kb.op("pe", lambda p=p, kind=kind, hg=hg: nc.tensor.matmul(
                    p[:, :], g.cf[:, C_AJ:C_AJ + 128], hk[:, kind, hg * 4:(hg + 1) * 4, :].rearrange("p h q -> p (h q)"),
                    start=True, stop=True), reads=[hk_r, g.cf_r], writes=[g.PSr[pi]])
                kb.op("dve", lambda p=p, kind=kind, hg=hg: nc.vector.tensor_copy(
                    out=g.bt[:, kind, hg * 4:(hg + 1) * 4, :].rearrange("p h q -> p (h q)"), in_=p[:, :]),
                    reads=[g.PSr[pi]], writes=[g.bt_r])
        p = g.PS[5]
        kb.op("pe", lambda: nc.tensor.matmul(p[:, 0:24], g.cf[0:32, C_OH31:C_OH31 + 128], tab[0:32, 0:24], start=True, stop=True),
              reads=[tab_r, g.cf_r], writes=[g.PSr[5]])
        kb.op("dve", lambda: nc.vector.tensor_copy(out=g.c31[:], in_=p[:, 0:24]), reads=[g.PSr[5]], writes=[g.c31_r])
        kb.barrier()


def phase_norm(g, l, src):
    nc, kb = g.nc, g.kb
    with ExitStack() as s:
        wbc = s.enter_context(_sbt(nc, "wbc", [128, D], F32))
        xt = [s.enter_context(_sbt(nc, f"xt{i}", [128, D], F32)) for i in range(2)]
        hb = [s.enter_context(_sbt(nc, f"hb{i}", [128, D], BF16)) for i in range(2)]
        junk = s.enter_context(_sbt(nc, "junk", [128, D], BF16))
        st = [s.enter_context(_sbt(nc, f"st{i}", [128, 4], F32)) for i in range(2)]
        wbc_r, junk_r = Res(), Res()
        xt_r = [Res(), Res()]
        hb_r = [Res(), Res()]
        st_r = [Res(), Res()]
        kb.dma("sp", wbc[:], g.norm_w[l:l + 1, :].partition_broadcast(128), writes=[wbc_r])
        for t in range(16):
            b = t % 2
            kb.dma("sp", xt[b][:], src[t * 128:(t + 1) * 128, :], reads=[getattr(g, "xres_r", Res())], writes=[xt_r[b]])
            kb.op("act", lambda b=b: nc.scalar.activation(out=junk[:], in_=xt[b][:], func=AF.Square, accum_out=st[b][:, 0:1]),
                  reads=[xt_r[b]], writes=[junk_r, st_r[b]])
            kb.op("act", lambda b=b: nc.scalar.activation(out=st[b][:, 1:2], in_=st[b][:, 0:1], func=AF.Sqrt, scale=1.0 / D, bias=g.kc[:, 2:3]),
                  reads=[st_r[b], g.kc_r], writes=[st_r[b]])
            kb.op("dve", lambda b=b: nc.vector.reciprocal(out=st[b][:, 2:3], in_=st[b][:, 1:2]), reads=[st_r[b]], writes=[st_r[b]])
            kb.op("dve", lambda b=b: nc.vector.scalar_tensor_tensor(out=hb[b][:], in0=xt[b][:], scalar=st[b][:, 2:3], in1=wbc[:],
                                                                    op0=ALU.mult, op1=ALU.mult),
                  reads=[xt_r[b], st_r[b], wbc_r], writes=[hb_r[b]])
            for q4 in range(4):
                pi = (t * 4 + q4) % 8
                pb = g.PS[pi][:].bitcast(BF16)

                def tr(pb=pb, b=b, q4=q4):
                    ins = None
                    for j in range(4):
                        kt = q4 * 4 + j
                        ins = nc.tensor.transpose(pb[:, j * 128:(j + 1) * 128], hb[b][:, kt * 128:(kt + 1) * 128], g.idb[:])
                    return ins
                kb.op("pe", tr, reads=[hb_r[b], g.cb_r], writes=[g.PSr[pi]])
                dst = g.hT[:, q4 * 4:(q4 + 1) * 4, t * 128:(t + 1) * 128]
                srcp = pb[:, 0:512].rearrange("p (j q) -> p j q", j=4)
                if q4 % 2 == 0:
                    kb.op("act", lambda dst=dst, srcp=srcp: nc.scalar.copy(out=dst, in_=srcp), reads=[g.PSr[pi]], writes=[g.hT_r[t // 4]])
                else:
                    kb.op("dve", lambda dst=dst, srcp=srcp: nc.vector.tensor_copy(out=dst, in_=srcp), reads=[g.PSr[pi]], writes=[g.hT_r[t // 4]])


class WPool:
    def __init__(self, g, s, name, n, kt, cols):
        self.g = g
        self.bufs = [s.enter_context(_sbt(g.nc, f"{name}{i}", [128, kt, cols], BF16)) for i in range(n)]
        self.res = [Res(f"{name}{i}") for i in range(n)]
        self.i = 0
        self.kt = kt

    def load(self, pieces):
        i = self.i
        self.i = (i + 1) % len(self.bufs)
        c = 0
        for ap, ncols in pieces:
            self.g.kb.dma("pool", self.bufs[i][:, :, c:c + ncols], ap.rearrange("(kt p) n -> p kt n", p=128), writes=[self.res[i]])
            c += ncols
        return self.bufs[i], self.res[i]


def win(g, l, c0, n):
    return g.w_in[l, :, c0:c0 + n]


def proj_fm(g, wbuf, wres, coff, tg, pi, M=128):
    nc = g.nc
    ps = g.PS[pi]

    def f():
        ins = None
        for kt in range(16):
            ins = nc.tensor.matmul(ps[0:M, :], wbuf[:, kt, coff:coff + M], g.hT[:, kt, tg * 512:(tg + 1) * 512],
                                   start=(kt == 0), stop=(kt == 15))
        return ins
    g.kb.op("pe", f, reads=[wres, g.hT_r[tg]], writes=[g.PSr[pi]])


class Rot:
    def __init__(self, items):
        self.items = list(items)
        self.i = 0

    def next(self):
        v = self.items[self.i]
        self.i = (self.i + 1) % len(self.items)
        return v


def mg_job_list(g, l):
    nc, kb = g.nc, g.kb
    jobs = []
    for i in range(4):
        for dtp in range(8):
            stt = {}
            for dj in range(2):
                for tg in range(4):
                    def job(i=i, dtp=dtp, dj=dj, tg=tg, stt=stt, first=(dj == 0 and tg == 0)):
                        if first:
                            stt["w"] = g.mgpool.load([(win(g, l, OFF_MG + i * 2048 + dtp * 256, 256), 256)])
                        wbuf, wres = stt["w"]
                        pi = g.mgrot.next()
                        proj_fm(g, wbuf, wres, dj * 128, tg, pi)
                        k = g.mgk[0]
                        g.mgk[0] += 1
                        sgi = k % len(g.mgst)
                        kb.op("act", lambda: nc.scalar.activation(out=g.mgst[sgi][:], in_=g.PS[pi][:, :], func=AF.Sigmoid), reads=[g.PSr[pi]], writes=[g.mgst_r[sgi]])
                        r0 = i * 2048 + dtp * 256 + dj * 128
                        kb.dma("sp", g.sgd[r0:r0 + 128, tg * 512:(tg + 1) * 512], g.mgst[sgi][:], reads=[g.mgst_r[sgi]], writes=[g.sgd_r])
                    jobs.append(job)
    return jobs


def mg_host(g, s, pool, rot):
    g.mgpool = pool
    g.mgrot = rot
    g.mgst = [s.enter_context(_sbt(g.nc, f"mgst{i}", [128, 512], BF16)) for i in range(2)]
    g.mgst_r = [Res(), Res()]


def mg_run(g, n):
    for _ in range(n):
        if g.mgjobs:
            g.mgjobs.pop(0)()


def mixer_A(g, l):
    nc, kb = g.nc, g.kb
    lam_init = 0.8 - 0.6 * math.exp(-0.3 * l)
    if not hasattr(g, "ybr_r"):
        g.ybr_r = Res()
    with ExitStack() as s:
        def sb(name, shape, dt):
            return s.enter_context(_sbt(nc, name, list(shape), dt))
        wp = WPool(g, s, "wA", 2, 16, 512)
        dl = sb("dl", [128, 256], F32)
        sc = sb("scA", [128, 8], F32)
        dl_r, sc_r = Res(), Res()
        kb.dma("sp", dl[:], g.diff_lambda[l:l + 1, :].partition_broadcast(128), writes=[dl_r])
        kb.op("dve", lambda: nc.vector.tensor_tensor(out=dl[:, 0:64], in0=dl[:, 0:64], in1=dl[:, 64:128], op=ALU.mult), reads=[dl_r], writes=[dl_r])
        kb.op("dve", lambda: nc.vector.tensor_tensor(out=dl[:, 128:192], in0=dl[:, 128:192], in1=dl[:, 192:256], op=ALU.mult), reads=[dl_r], writes=[dl_r])
        kb.op("dve", lambda: nc.vector.reduce_sum(out=sc[:, 0:1], in_=dl[:, 0:64], axis=AX.X), reads=[dl_r], writes=[sc_r])
        kb.op("dve", lambda: nc.vector.reduce_sum(out=sc[:, 1:2], in_=dl[:, 128:192], axis=AX.X), reads=[dl_r], writes=[sc_r])
        kb.op("act", lambda: nc.scalar.activation(out=sc[:, 2:4], in_=sc[:, 0:2], func=AF.Exp), reads=[sc_r], writes=[sc_r])
        kb.op("dve", lambda: nc.vector.tensor_tensor(out=sc[:, 4:5], in0=sc[:, 3:4], in1=sc[:, 2:3], op=ALU.subtract), reads=[sc_r], writes=[sc_r])
        kb.op("dve", lambda: nc.vector.tensor_scalar_add(out=sc[:, 5:6], in0=sc[:, 4:5], scalar1=-lam_init), reads=[sc_r], writes=[sc_r])
        kb.dma("sp", sc[:, 6:7], g.diff_subln_w[l:l + 1, :].rearrange("o e -> e o"), writes=[sc_r], allow_slow_non_contiguous=True)
        kb.op("dve", lambda: nc.vector.tensor_scalar_mul(out=sc[:, 7:8], in0=sc[:, 6:7], scalar1=(1.0 - lam_init)), reads=[sc_r], writes=[sc_r])
        neglam = sc[:, 5:6]
        swcol = sc[:, 7:8]

        NB = 2
        qT = [sb(f"qT{i}", [128, T], BF16) for i in range(NB)]
        kT = [sb(f"kT{i}", [128, T], BF16) for i in range(NB)]
        vT = [sb(f"vT{i}", [128, T], BF16) for i in range(NB)]
        gT = [sb(f"gT{i}", [128, T], BF16) for i in range(NB)]
        Vt = [sb(f"Vt{i}", [128, 16, 128], BF16) for i in range(NB)]
        qT_r = [Res() for _ in range(NB)]
        kT_r = [Res() for _ in range(NB)]
        vT_r = [Res() for _ in range(NB)]
        gT_r = [Res() for _ in range(NB)]
        Vt_r = [Res() for _ in range(NB)]
        NE = 5
        Eb = [sb(f"Eb{i}", [128, 512], BF16) for i in range(NE)]
        Eb_r = [Res() for _ in range(NE)]
        erot = Rot(range(NE))
        f1 = [sb(f"fA{i}", [128, 512], F32) for i in range(6)]
        f1_r = [Res() for _ in range(6)]
        sqb = sb("sqb", [128, 512], BF16)
        sqb_r = Res()
        yb = [sb(f"ybA{i}", [128, 512], BF16) for i in range(2)]
        yb_r = [Res(), Res()]
        prot = Rot([0, 1, 2, 3])
        srot = prot
        PO = [4, 6]
        PSUMS = [5, 7]
        def inproj_closures(h):
            hb = h % NB
            stt = {}
            out = []

            def ld():
                stt["w"] = wp.load([(win(g, l, OFF_AQ + h * 128, 128), 128), (win(g, l, OFF_AK + h * 128, 128), 128),
                                    (win(g, l, OFF_AV + h * 128, 128), 128), (win(g, l, OFF_AG + h * 128, 128), 128)])
            out.append(ld)
            dsts = [(qT[hb], qT_r[hb]), (kT[hb], kT_r[hb]), (vT[hb], vT_r[hb]), (gT[hb], gT_r[hb])]
            for ti in (2, 1, 0, 3):
                for tg in range(4):
                    def grp(ti=ti, tg=tg):
                        wbuf, wres = stt["w"]
                        pi = prot.next()
                        proj_fm(g, wbuf, wres, ti * 128, tg, pi)
                        dst, dres = dsts[ti]
                        o = dst[:, tg * 512:(tg + 1) * 512]
                        if ti == 3:
                            kb.op("act", lambda: nc.scalar.activation(out=o, in_=g.PS[pi][:, :], func=AF.Silu), reads=[g.PSr[pi]], writes=[dres])
                        else:
                            kb.op("dve", lambda: nc.vector.tensor_copy(out=o, in_=g.PS[pi][:, :]), reads=[g.PSr[pi]], writes=[dres])
                    out.append(grp)
                if ti == 2:
                    for q4 in range(4):
                        def trv(q4=q4):
                            pi = prot.next()
                            pb = g.PS[pi][:].bitcast(BF16)

                            def tr():
                                ins = None
                                for j in range(4):
                                    tt = q4 * 4 + j
                                    ins = nc.tensor.transpose(pb[:, j * 128:(j + 1) * 128], vT[hb][:, tt * 128:(tt + 1) * 128], g.idb[:])
                                return ins
                            kb.op("pe", tr, reads=[vT_r[hb], g.cb_r], writes=[g.PSr[pi]])
                            kb.op("dve", lambda: nc.vector.tensor_copy(out=Vt[hb][:, q4 * 4:(q4 + 1) * 4, :], in_=pb[:, 0:512].rearrange("p (j q) -> p j q", j=4)),
                                  reads=[g.PSr[pi]], writes=[Vt_r[hb]])
                        out.append(trv)
            return out

        for c_ in inproj_closures(0):
            c_()
        for h in range(8):
            hb = h % NB
            nxt = inproj_closures(h + 1) if h + 1 < 8 else []
            def make_step(G, m, j, hb=hb, h=h):
                po, psm = PO[m], PSUMS[m]
                last = 4 * G + 3
                c0 = max(j - 4 * G, 0) * 128
                st = {}

                def emit_S():
                    si = srot.next()
                    pS = g.PS[si]

                    def smm():
                        nb = []
                        if j >= 4 * G:
                            nb.append((0, (j - 4 * G) * 128))
                        if 4 * G <= j + 1 <= 4 * G + 3:
                            nb.append((1, (j + 1 - 4 * G) * 128))
                        ins = nc.tensor.matmul(pS[:, c0:512], kT[hb][m * 64:(m + 1) * 64, j * 128:(j + 1) * 128],
                                               qT[hb][m * 64:(m + 1) * 64, G * 512 + c0:(G + 1) * 512], start=True, stop=(len(nb) == 0))
                        for bi, (kind, cc) in enumerate(nb):
                            ins = nc.tensor.matmul(pS[:, cc:cc + 128], g.idb[:], g.bt[:, kind, h, :], start=False, stop=(bi == len(nb) - 1))
                        return ins
                    kb.op("pe", smm, reads=[kT_r[hb], qT_r[hb], g.bt_r, g.cb_r], writes=[g.PSr[si]])
                    ei = erot.next()
                    st["ei"] = ei
                    kb.op("act", lambda: nc.scalar.activation(out=Eb[ei][:, c0:512], in_=pS[:, c0:512], func=AF.Exp, scale=0.125, bias=g.c31[:, h:h + 1]),
                          reads=[g.PSr[si], g.c31_r], writes=[Eb_r[ei]])

                def emit_PV():
                    ei = st["ei"]

                    def pv():
                        nc.tensor.matmul(g.PS[po][:, c0:512], Vt[hb][:, j, :], Eb[ei][:, c0:512], start=(j == 0), stop=(j == last))
                        return nc.tensor.matmul(g.PS[psm][:, c0:512], g.onesb[:], Eb[ei][:, c0:512], start=(j == 0), stop=(j == last))
                    kb.op("pe", pv, reads=[Vt_r[hb], Eb_r[ei], g.cb_r], writes=[g.PSr[po], g.PSr[psm]])
                    if m == 1 and j == last:
                        norm_G(G)
                return emit_S, emit_PV

            def norm_G(G, hb=hb, h=h):
                r1, o1, o2, o, rs, y1 = f1
                kb.op("dve", lambda: nc.vector.reciprocal(out=r1[:], in_=g.PS[PSUMS[0]][:, :]), reads=[g.PSr[PSUMS[0]]], writes=[f1_r[0]])
                kb.op("dve", lambda: nc.vector.tensor_tensor(out=o1[:], in0=g.PS[PO[0]][:, :], in1=r1[:], op=ALU.mult), reads=[g.PSr[PO[0]], f1_r[0]], writes=[f1_r[1]])
                kb.op("dve", lambda: nc.vector.reciprocal(out=r1[:], in_=g.PS[PSUMS[1]][:, :]), reads=[g.PSr[PSUMS[1]], f1_r[0]], writes=[f1_r[0]])
                kb.op("dve", lambda: nc.vector.tensor_tensor(out=o2[:], in0=g.PS[PO[1]][:, :], in1=r1[:], op=ALU.mult), reads=[g.PSr[PO[1]], f1_r[0]], writes=[f1_r[2]])
                kb.op("dve", lambda: nc.vector.scalar_tensor_tensor(out=o[:], in0=o2[:], scalar=neglam, in1=o1[:], op0=ALU.mult, op1=ALU.add),
                      reads=[f1_r[1], f1_r[2], sc_r], writes=[f1_r[3]])
                kb.op("act", lambda: nc.scalar.activation(out=sqb[:], in_=o[:], func=AF.Square), reads=[f1_r[3]], writes=[sqb_r])
                pi = prot.next()
                kb.op("pe", lambda pi=pi: nc.tensor.matmul(g.PS[pi][:, :], g.onesb[:], sqb[:], start=True, stop=True), reads=[sqb_r, g.cb_r], writes=[g.PSr[pi]])
                kb.op("act", lambda pi=pi: nc.scalar.activation(out=rs[:], in_=g.PS[pi][:, :], func=AF.Sqrt, scale=1.0 / 128, bias=g.kc[:, 2:3]),
                      reads=[g.PSr[pi], g.kc_r], writes=[f1_r[4]])
                kb.op("dve", lambda: nc.vector.reciprocal(out=rs[:], in_=rs[:]), reads=[f1_r[4]], writes=[f1_r[4]])
                kb.op("dve", lambda: nc.vector.scalar_tensor_tensor(out=y1[:], in0=o[:], scalar=swcol, in1=gT[hb][:, G * 512:(G + 1) * 512],
                                                                    op0=ALU.mult, op1=ALU.mult), reads=[f1_r[3], sc_r, gT_r[hb]], writes=[f1_r[5]])
                yi = (h * 4 + G) % 2
                kb.op("dve", lambda: nc.vector.tensor_tensor(out=yb[yi][:], in0=y1[:], in1=rs[:], op=ALU.mult), reads=[f1_r[5], f1_r[4]], writes=[yb_r[yi]])
                kb.dma("sp", g.ybr[0, h * 128:(h + 1) * 128, G * 512:(G + 1) * 512], yb[yi][:], reads=[yb_r[yi]], writes=[g.ybr_r])

            steps = [make_step(G, m, j) for G in range(4) for m in range(2) for j in range(0, 4 * G + 4)]
            SKEW = 2
            pend = []
            stride = 3
            for si_, (eS, ePV) in enumerate(steps):
                eS()
                pend.append(ePV)
                if len(pend) > SKEW:
                    pend.pop(0)()
                if nxt and si_ % stride == stride - 1:
                    nxt.pop(0)()
            while pend:
                pend.pop(0)()
            while nxt:
                nxt.pop(0)()


def mixer_B(g, l):
    nc, kb = g.nc, g.kb
    if not hasattr(g, "ybr_r"):
        g.ybr_r = Res()
    with ExitStack() as s:
        def sb(name, shape, dt):
            return s.enter_context(_sbt(nc, name, list(shape), dt))
        wp = WPool(g, s, "wB", 2, 16, 512)
        mgp = WPool(g, s, "wBm", 2, 16, 256)
        sk = sb("skB", [128, 16], F32)
        sk_r = Res()
        for par in range(2):
            src = bass.AP(g.swa_sinks.tensor, l * 16 + par, [[0, 64], [2, 8]])
            kb.dma("sp", sk[par * 64:(par + 1) * 64, 0:8], src, writes=[sk_r], allow_slow_non_contiguous=True)
        kb.op("act", lambda: nc.scalar.activation(out=sk[:, 8:16], in_=sk[:, 0:8], func=AF.Exp), reads=[sk_r], writes=[sk_r])
        kd = [sb(f"kd{i}", [128, T], BF16) for i in range(4)]
        kd_r = [Res() for _ in range(4)]
        Vb = sb("Vb", [128, 16, 256], BF16)
        Vb_r = Res()
        prot = Rot([0, 1])
        srot = Rot([2, 3, 4, 5])
        PO, PSM = 6, 7
        wbuf, wres = wp.load([(win(g, l, OFF_BK + (i // 2) * 64, 64), 64) for i in range(8)])
        ev = 0
        for kv in range(4):
            for tg in range(4):
                pi = prot.next()
                proj_fm(g, wbuf, wres, kv * 128, tg, pi)
                o = kd[kv][:, tg * 512:(tg + 1) * 512]
                ev += 1
                if ev % 2 == 0:
                    kb.op("act", lambda o=o, pi=pi: nc.scalar.copy(out=o, in_=g.PS[pi][:, :]), reads=[g.PSr[pi]], writes=[kd_r[kv]])
                else:
                    kb.op("dve", lambda o=o, pi=pi: nc.vector.tensor_copy(out=o, in_=g.PS[pi][:, :]), reads=[g.PSr[pi]], writes=[kd_r[kv]])
        wbuf, wres = wp.load([(win(g, l, OFF_BV, 256), 256)])
        for tt in range(16):
            pi = prot.next()

            def vmm(pi=pi, tt=tt, wbuf=wbuf):
                ins = None
                for kt in range(16):
                    ins = nc.tensor.matmul(g.PS[pi][:, 0:256], g.hT[:, kt, tt * 128:(tt + 1) * 128], wbuf[:, kt, 0:256], start=(kt == 0), stop=(kt == 15))
                return ins
            kb.op("pe", vmm, reads=[wres, g.hT_r[tt // 4]], writes=[g.PSr[pi]])
            kb.op("dve", lambda pi=pi, tt=tt: nc.vector.tensor_copy(out=Vb[:, tt, :], in_=g.PS[pi][:, 0:256]), reads=[g.PSr[pi]], writes=[Vb_r])
        NB = 2
        qT = [sb(f"qB{i}", [128, T], BF16) for i in range(NB)]
        gT = [sb(f"gB{i}", [128, T], BF16) for i in range(NB)]
        qT_r = [Res() for _ in range(NB)]
        gT_r = [Res() for _ in range(NB)]
        NE = 5
        Eb = [sb(f"EbB{i}", [128, 256], BF16) for i in range(NE)]
        Eb_r = [Res() for _ in range(NE)]
        erot = Rot(range(NE))
        f1 = [sb(f"fB{i}", [128, 512], F32) for i in range(2)]
        f1_r = [Res() for _ in range(2)]
        yb = [sb(f"ybB{i}", [128, 512], BF16) for i in range(2)]
        yb_r = [Res(), Res()]
        def inprojB(t):
            hb = t % NB
            stt = {}
            out = []

            def ld():
                stt["w"] = wp.load([(win(g, l, OFF_BQ + t * 128, 128), 128), (win(g, l, OFF_BG + t * 128, 128), 128)])
            out.append(ld)
            for ti in range(2):
                for tg in range(4):
                    def grp(ti=ti, tg=tg):
                        wbuf, wres = stt["w"]
                        pi = prot.next()
                        proj_fm(g, wbuf, wres, ti * 128, tg, pi)
                        if ti == 0:
                            o = qT[hb][:, tg * 512:(tg + 1) * 512]
                            kb.op("dve", lambda: nc.vector.tensor_copy(out=o, in_=g.PS[pi][:, :]), reads=[g.PSr[pi]], writes=[qT_r[hb]])
                        else:
                            o = gT[hb][:, tg * 512:(tg + 1) * 512]
                            kb.op("act", lambda: nc.scalar.activation(out=o, in_=g.PS[pi][:, :], func=AF.Silu), reads=[g.PSr[pi]], writes=[gT_r[hb]])
                    out.append(grp)
            return out

        mg_host(g, s, mgp, prot)
        for c_ in inprojB(0):
            c_()
        for t in range(8):
            hb = t % NB
            kv = t // 2
            nxt = inprojB(t + 1) if t + 1 < 8 else []

            def make_stepB(G, par, j, t=t, hb=hb, kv=kv):
                hq = 2 * t + par
                lo, hi = par * 64, (par + 1) * 64
                blocks = [i for i in (j, j + 1) if 4 * G <= i <= 4 * G + 3]
                cA = (blocks[0] - 4 * G) * 128
                ncol = 128 * len(blocks)
                st = {}

                def emit_S():
                    si = srot.next()
                    pS = g.PS[si]

                    def smm():
                        nc.tensor.matmul(pS[:, 0:ncol], kd[kv][lo:hi, j * 128:(j + 1) * 128],
                                         qT[hb][lo:hi, G * 512 + cA:G * 512 + cA + ncol], start=True, stop=False)
                        ins = None
                        for bi, i in enumerate(blocks):
                            kind = 0 if i == j else 1
                            ins = nc.tensor.matmul(pS[:, bi * 128:(bi + 1) * 128], g.idb[:], g.bt[:, kind, 8 + hq, :], start=False, stop=(bi == len(blocks) - 1))
                        return ins
                    kb.op("pe", smm, reads=[kd_r[kv], qT_r[hb], g.bt_r, g.cb_r], writes=[g.PSr[si]])
                    ei = erot.next()
                    st["ei"] = ei
                    kb.op("act", lambda: nc.scalar.activation(out=Eb[ei][:, 0:ncol], in_=pS[:, 0:ncol], func=AF.Exp, scale=0.125),
                          reads=[g.PSr[si]], writes=[Eb_r[ei]])

                def emit_PV():
                    ei = st["ei"]

                    def pv():
                        ins = None
                        for bi, i in enumerate(blocks):
                            cc = (i - 4 * G) * 128
                            first = (j == i - 1) or (i == 0)
                            lastk = (j == i)
                            nc.tensor.matmul(g.PS[PO][lo:hi, cc:cc + 128], Vb[:, j, kv * 64:(kv + 1) * 64], Eb[ei][:, bi * 128:(bi + 1) * 128], start=first, stop=lastk)
                            ins = nc.tensor.matmul(g.PS[PSM][lo:hi, cc:cc + 128], g.onesb[:, 0:64], Eb[ei][:, bi * 128:(bi + 1) * 128], start=first, stop=lastk)
                        return ins
                    kb.op("pe", pv, reads=[Vb_r, Eb_r[ei], g.cb_r], writes=[g.PSr[PO], g.PSr[PSM]])
                    if par == 1 and j == 4 * G + 3:
                        fin_G(G)
                return emit_S, emit_PV

            def fin_G(G, t=t, hb=hb):
                den, y1 = f1
                kb.op("dve", lambda: nc.vector.tensor_scalar_add(out=den[:], in0=g.PS[PSM][:, :], scalar1=sk[:, 8 + t:9 + t]), reads=[g.PSr[PSM], sk_r], writes=[f1_r[0]])
                kb.op("dve", lambda: nc.vector.reciprocal(out=den[:], in_=den[:]), reads=[f1_r[0]], writes=[f1_r[0]])
                kb.op("dve", lambda: nc.vector.tensor_tensor(out=y1[:], in0=g.PS[PO][:, :], in1=den[:], op=ALU.mult), reads=[g.PSr[PO], f1_r[0]], writes=[f1_r[1]])
                yi = (t * 4 + G) % 2
                kb.op("dve", lambda: nc.vector.tensor_tensor(out=yb[yi][:], in0=y1[:], in1=gT[hb][:, G * 512:(G + 1) * 512], op=ALU.mult),
                      reads=[f1_r[1], gT_r[hb]], writes=[yb_r[yi]])
                kb.dma("sp", g.ybr[1, t * 128:(t + 1) * 128, G * 512:(G + 1) * 512], yb[yi][:], reads=[yb_r[yi]], writes=[g.ybr_r])

            steps = [make_stepB(G, par, j) for G in range(4) for par in range(2) for j in range(max(4 * G - 1, 0), 4 * G + 4)]
            SKEW = 2
            pend = []
            stride = 4
            for si_, (eS, ePV) in enumerate(steps):
                eS()
                pend.append(ePV)
                if len(pend) > SKEW:
                    pend.pop(0)()
                if nxt and si_ % stride == stride - 1:
                    nxt.pop(0)()
            while pend:
                pend.pop(0)()
            while nxt:
                nxt.pop(0)()


def load_rows_T(g, s, name, rows_aps, nt):
    nc, kb = g.nc, g.kb
    R = len(rows_aps)
    rows = s.enter_context(_sbt(nc, name + "_rows", [R, nt * 128], F32))
    outT = s.enter_context(_sbt(nc, name + "_T", [128, nt, R], F32))
    rows_r, out_r = Res(), Res()
    for r, ap in enumerate(rows_aps):
        kb.dma("sp", rows[r:r + 1, :], ap, writes=[rows_r])
    pi = 0
    ps = g.PS[pi]

    def f():
        ins = None
        for t in range(nt):
            ins = nc.tensor.transpose(ps[:, t * R:(t + 1) * R], rows[0:R, t * 128:(t + 1) * 128], g.cf[0:R, C_ID:C_ID + R])
        return ins
    kb.op("pe", f, reads=[rows_r, g.cf_r], writes=[g.PSr[pi]])
    kb.op("dve", lambda: nc.vector.tensor_copy(out=outT[:].rearrange("p t r -> p (t r)"), in_=ps[:, 0:nt * R]), reads=[g.PSr[pi]], writes=[out_r])
    return outT, out_r


def mixer_D(g, l):
    nc, kb = g.nc, g.kb
    if not hasattr(g, "ybr_r"):
        g.ybr_r = Res()
    with ExitStack() as s:
        def sb(name, shape, dt):
            return s.enter_context(_sbt(nc, name, list(shape), dt))
        rows = [g.conf_conv_w[l, j:j + 1, :] for j in range(31)] + [g.conf_conv_b[l:l + 1, :], g.conf_ln_w[l:l + 1, :], g.conf_ln_b[l:l + 1, :]]
        dp = sb("dp", [128, 8, 34], F32)
        dp_r = Res()
        with ExitStack() as s2:
            dpT, dp_r0 = load_rows_T(g, s2, "dpar", rows, 8)
            kb.op("dve", lambda: nc.vector.tensor_copy(out=dp[:], in_=dpT[:]), reads=[dp_r0], writes=[dp_r])
            kb.barrier()
        wp = WPool(g, s, "wD", 2, 16, 256)
        conv = sb("convD", [128, 8, T], F32)
        conv_r = [Res() for _ in range(4)]
        s3 = ExitStack()

        def sb3(name, shape, dt):
            return s3.enter_context(_sbt(nc, name, list(shape), dt))
        hpad = [sb3(f"hpad{i}", [128, 30 + T], BF16) for i in range(2)]
        hpad_r = [Res(), Res()]
        for i in range(2):
            kb.op("dve", lambda i=i: nc.vector.memset(hpad[i][:, 0:30], 0.0), writes=[hpad_r[i]])
        dgl = [sb3(f"dgl{i}", [128, 31, 128], BF16) for i in range(2)]
        dgl_r = [Res(), Res()]
        sg = [sb3(f"sgD{i}", [128, 512], F32) for i in range(2)]
        sg_r = [Res(), Res()]
        prot = Rot([0, 1, 2, 3])
        crot = Rot([4, 5, 6, 7])
        for t in range(8):
            b = t % 2
            wbuf, wres = wp.load([(win(g, l, OFF_DGLU + t * 128, 128), 128), (win(g, l, OFF_DGLU + 1024 + t * 128, 128), 128)])
            for j in range(31):
                kb.op("dve", lambda j=j, t=t, b=b: nc.vector.tensor_scalar_mul(out=dgl[b][:, j, :], in0=g.cf[:, C_ID:C_ID + 128], scalar1=dp[:, t, j:j + 1]),
                      reads=[g.cf_r, dp_r], writes=[dgl_r[b]])
            for tg in range(4):
                pv_, pg_ = prot.next(), prot.next()
                proj_fm(g, wbuf, wres, 0, tg, pv_)
                proj_fm(g, wbuf, wres, 128, tg, pg_)
                si = (t * 4 + tg) % 2
                kb.op("act", lambda si=si, pg_=pg_: nc.scalar.activation(out=sg[si][:], in_=g.PS[pg_][:, :], func=AF.Sigmoid), reads=[g.PSr[pg_]], writes=[sg_r[si]])
                kb.op("dve", lambda si=si, pv_=pv_, b=b, tg=tg: nc.vector.tensor_tensor(
                    out=hpad[b][:, 30 + tg * 512:30 + (tg + 1) * 512], in0=g.PS[pv_][:, :], in1=sg[si][:], op=ALU.mult),
                    reads=[g.PSr[pv_], sg_r[si]], writes=[hpad_r[b]])
            for tg in range(4):
                ci = crot.next()

                def cmm(ci=ci, b=b, tg=tg):
                    ins = None
                    for j in range(31):
                        ins = nc.tensor.matmul(g.PS[ci][:, :], dgl[b][:, j, :], hpad[b][:, tg * 512 + j:tg * 512 + j + 512], start=(j == 0), stop=(j == 30))
                    return ins
                kb.op("pe", cmm, reads=[dgl_r[b], hpad_r[b]], writes=[g.PSr[ci]])
                kb.op("act", lambda ci=ci, t=t, tg=tg: nc.scalar.activation(out=conv[:, t, tg * 512:(tg + 1) * 512], in_=g.PS[ci][:, :], func=AF.Identity,
                                                                           bias=dp[:, t, 31:32]), reads=[g.PSr[ci], dp_r], writes=[conv_r[tg]])
        kb.barrier()
        s3.close()
        sq = [sb(f"sqD{i}", [128, 512], F32) for i in range(2)]
        sq_r = [Res(), Res()]
        mu = sb("muD", [128, 512], F32)
        rs = sb("rsD", [128, 512], F32)
        tmp = [sb(f"tmD{i}", [128, 512], F32) for i in range(2)]
        tmp_r = [Res(), Res()]
        gD = [sb(f"gD{i}", [128, 512], F32) for i in range(2)]
        gD_r = [Res(), Res()]
        mu_r, rs_r = Res(), Res()
        yb = [sb(f"ybD{i}", [128, 512], BF16) for i in range(2)]
        yb_r = [Res(), Res()]
        onesf = g.cf[:, C_ONES:C_ONES + 128]
        wg = [None] * 8
        for tg in range(4):
            pA, pB = 0, 1

            def amm(tg=tg):
                ins = None
                for t in range(8):
                    ins = nc.tensor.matmul(g.PS[pA][:, :], onesf, conv[:, t, tg * 512:(tg + 1) * 512], start=(t == 0), stop=(t == 7))
                return ins
            kb.op("pe", amm, reads=[conv_r[tg], g.cf_r], writes=[g.PSr[pA]])
            for t in range(8):
                si = t % 2
                kb.op("act", lambda si=si, t=t, tg=tg: nc.scalar.activation(out=sq[si][:], in_=conv[:, t, tg * 512:(tg + 1) * 512], func=AF.Square),
                      reads=[conv_r[tg]], writes=[sq_r[si]])
                kb.op("pe", lambda si=si, t=t: nc.tensor.matmul(g.PS[pB][:, :], onesf, sq[si][:], start=(t == 0), stop=(t == 7)),
                      reads=[sq_r[si], g.cf_r], writes=[g.PSr[pB]])
            kb.op("act", lambda: nc.scalar.mul(out=mu[:], in_=g.PS[pA][:, :], mul=1.0 / 1024), reads=[g.PSr[pA]], writes=[mu_r])
            kb.op("dve", lambda: nc.vector.tensor_tensor(out=rs[:], in0=mu[:], in1=mu[:], op=ALU.mult), reads=[mu_r], writes=[rs_r])
            kb.op("dve", lambda: nc.vector.scalar_tensor_tensor(out=rs[:], in0=g.PS[pB][:, :], scalar=g.kc[:, 3:4], in1=rs[:], op0=ALU.mult, op1=ALU.subtract),
                  reads=[g.PSr[pB], g.kc_r, rs_r], writes=[rs_r])
            kb.op("act", lambda: nc.scalar.activation(out=rs[:], in_=rs[:], func=AF.Sqrt, bias=g.kc[:, 2:3]), reads=[rs_r, g.kc_r], writes=[rs_r])
            kb.op("dve", lambda: nc.vector.reciprocal(out=rs[:], in_=rs[:]), reads=[rs_r], writes=[rs_r])
            for t in range(8):
                if t % 2 == 0:
                    wbuf, wres = wp.load([(win(g, l, OFF_DG + t * 128, 256), 256)])
                pg_ = 2 + (t % 4)
                proj_fm(g, wbuf, wres, (t % 2) * 128, tg, pg_)
                si = t % 2
                kb.op("act", lambda si=si, pg_=pg_: nc.scalar.activation(out=gD[si][:], in_=g.PS[pg_][:, :], func=AF.Silu), reads=[g.PSr[pg_]], writes=[gD_r[si]])
                kb.op("dve", lambda si=si, t=t, tg=tg: nc.vector.tensor_tensor(out=tmp[si][:], in0=conv[:, t, tg * 512:(tg + 1) * 512], in1=mu[:], op=ALU.subtract),
                      reads=[conv_r[tg], mu_r], writes=[tmp_r[si]])
                kb.op("dve", lambda si=si: nc.vector.tensor_tensor(out=tmp[si][:], in0=tmp[si][:], in1=rs[:], op=ALU.mult), reads=[tmp_r[si], rs_r], writes=[tmp_r[si]])
                kb.op("act", lambda si=si, t=t: nc.scalar.activation(out=tmp[si][:], in_=tmp[si][:], func=AF.Silu, scale=dp[:, t, 32:33], bias=dp[:, t, 33:34]),
                      reads=[tmp_r[si], dp_r], writes=[tmp_r[si]])
                kb.op("dve", lambda si=si: nc.vector.tensor_tensor(out=yb[si][:], in0=tmp[si][:], in1=gD[si][:], op=ALU.mult), reads=[tmp_r[si], gD_r[si]], writes=[yb_r[si]])
                kb.dma("sp", g.ybr[3, t * 128:(t + 1) * 128, tg * 512:(tg + 1) * 512], yb[si][:], reads=[yb_r[si]], writes=[g.ybr_r])
        if g.mgjobs:
            mg_host(g, s, wp, Rot([6, 7]))
            mg_run(g, 10 ** 6)


def mixer_C(g, l):
    nc, kb = g.nc, g.kb
    if not hasattr(g, "ybr_r"):
        g.ybr_r = Res()
    with ExitStack() as s:
        def sb(name, shape, dt):
            return s.enter_context(_sbt(nc, name, list(shape), dt))
        cp = sb("cp", [128, 12, 5], F32)
        cp_r = Res()
        with ExitStack() as s2:
            rows = [g.ssd_conv_w[l, j:j + 1, :] for j in range(4)] + [g.ssd_conv_b[l:l + 1, :]]
            cpT, cp_r0 = load_rows_T(g, s2, "cpar", rows, 12)
            kb.op("dve", lambda: nc.vector.tensor_copy(out=cp[:], in_=cpT[:]), reads=[cp_r0], writes=[cp_r])
            kb.barrier()
        pc = sb("pcC", [128, 64], F32)
        pc_r = Res()
        kb.dma("sp", pc[:, 0:16], g.ssd_dt_bias[l:l + 1, :].partition_broadcast(128), writes=[pc_r])
        kb.dma("sp", pc[:, 16:32], g.ssd_a_log[l:l + 1, :].partition_broadcast(128), writes=[pc_r])
        for par in range(2):
            src = bass.AP(g.ssd_d.tensor, l * 16 + par, [[0, 64], [2, 8]])
            kb.dma("sp", pc[par * 64:(par + 1) * 64, 32:40], src, writes=[pc_r], allow_slow_non_contiguous=True)
        kb.dma("sp", pc[:, 40:48], g.ssd_norm_w[l:l + 1, :].rearrange("o (t p) -> p (o t)", p=128), writes=[pc_r], allow_slow_non_contiguous=True)
        kb.op("act", lambda: nc.scalar.activation(out=pc[:, 16:32], in_=pc[:, 16:32], func=AF.Exp), reads=[pc_r], writes=[pc_r])
        kb.op("dve", lambda: nc.vector.tensor_scalar_mul(out=pc[:, 16:32], in0=pc[:, 16:32], scalar1=-1.0), reads=[pc_r], writes=[pc_r])
        blk = sb("blkC", [16, 2048], F32)
        blk_r = Res()
        kb.dma("sp", blk[:], g.cblk, writes=[blk_r])
        wp = WPool(g, s, "wC", 2, 16, 256)
        prot = Rot([0, 1])
        xbc = sb("xbc", [128, 12, T], BF16)
        xbc_r = [Res() for _ in range(12)]
        dtm = sb("dtm", [128, 256], F32)
        acol = sb("acol", [128, 256], F32)
        dst_ = sb("dstm", [128, 256], F32)
        cdb = sb("cdb", [128, 256], F32)
        dt_r, acol_r, dst_r, cdb_r = Res(), Res(), Res(), Res()
        with ExitStack() as s3:
            def sb3(name, shape, dt):
                return s3.enter_context(_sbt(nc, name, list(shape), dt))
            rawp = [sb3(f"rawp{i}", [128, 3 + T], BF16) for i in range(2)]
            rawp_r = [Res(), Res()]
            for i in range(2):
                kb.op("dve", lambda i=i: nc.vector.memset(rawp[i][:, 0:3], 0.0), writes=[rawp_r[i]])
            dg4 = [sb3(f"dg4{i}", [128, 4, 128], BF16) for i in range(2)]
            dg4_r = [Res(), Res()]
            dAm = sb3("dAm", [128, 256], F32)
            dA_r = Res()
            for ti in range(12):
                b = ti % 2
                if ti % 2 == 0:
                    wbuf, wres = wp.load([(win(g, l, OFF_CX + ti * 128, 256), 256)])
                for j in range(4):
                    kb.op("dve", lambda j=j, ti=ti, b=b: nc.vector.tensor_scalar_mul(out=dg4[b][:, j, :], in0=g.cf[:, C_ID:C_ID + 128], scalar1=cp[:, ti, j:j + 1]),
                          reads=[g.cf_r, cp_r], writes=[dg4_r[b]])
                for tg in range(4):
                    pi = prot.next()
                    proj_fm(g, wbuf, wres, (ti % 2) * 128, tg, pi)
                    kb.op("act", lambda pi=pi, b=b, tg=tg: nc.scalar.copy(out=rawp[b][:, 3 + tg * 512:3 + (tg + 1) * 512], in_=g.PS[pi][:, :]),
                          reads=[g.PSr[pi]], writes=[rawp_r[b]])
                for tg in range(4):
                    ci = 2 + (ti * 4 + tg) % 2

                    def cmm(ci=ci, b=b, tg=tg):
                        ins = None
                        for j in range(4):
                            ins = nc.tensor.matmul(g.PS[ci][:, :], dg4[b][:, j, :], rawp[b][:, tg * 512 + j:tg * 512 + j + 512], start=(j == 0), stop=(j == 3))
                        return ins
                    kb.op("pe", cmm, reads=[dg4_r[b], rawp_r[b]], writes=[g.PSr[ci]])
                    kb.op("act", lambda ci=ci, ti=ti, tg=tg: nc.scalar.activation(out=xbc[:, ti, tg * 512:(tg + 1) * 512], in_=g.PS[ci][:, :], func=AF.Silu,
                                                                                bias=cp[:, ti, 4:5]), reads=[g.PSr[ci], cp_r], writes=[xbc_r[ti]])
            wbuf, wres = wp.load([(win(g, l, OFF_CDT, 16), 16)])
            pdt = 4

            def dtmm():
                ins = None
                for tt in range(16):
                    for kt in range(16):
                        ins = nc.tensor.matmul(g.PS[pdt][:, tt * 16:(tt + 1) * 16], g.hT[:, kt, tt * 128:(tt + 1) * 128], wbuf[:, kt, 0:16], start=(kt == 0), stop=(kt == 15))
                return ins
            kb.op("pe", dtmm, reads=[wres] + g.hT_r, writes=[g.PSr[pdt]])
            kb.op("dve", lambda: nc.vector.tensor_tensor(out=dtm[:].rearrange("p (c e) -> p c e", e=16), in0=g.PS[pdt][:, 0:256].rearrange("p (c e) -> p c e", e=16),
                                                         in1=pc[:, 0:16].unsqueeze(1).to_broadcast([128, 16, 16]), op=ALU.add), reads=[g.PSr[pdt], pc_r], writes=[dt_r])
            kb.op("act", lambda: nc.scalar.activation(out=dtm[:], in_=dtm[:], func=AF.Exp), reads=[dt_r], writes=[dt_r])
            kb.op("act", lambda: nc.scalar.activation(out=dtm[:], in_=dtm[:], func=AF.Ln, bias=1.0), reads=[dt_r], writes=[dt_r])
            kb.op("dve", lambda: nc.vector.tensor_tensor(out=dAm[:].rearrange("p (c e) -> p c e", e=16), in0=dtm[:].rearrange("p (c e) -> p c e", e=16),
                                                         in1=pc[:, 16:32].unsqueeze(1).to_broadcast([128, 16, 16]), op=ALU.mult), reads=[dt_r, pc_r], writes=[dA_r])
            for (cst, dstt, dres, doexp) in ((C_TRI, acol, acol_r, False), (C_TRIU, dst_, dst_r, True), (C_ONES, cdb, cdb_r, True)):
                pi = prot.next()
                kb.op("pe", lambda pi=pi, cst=cst: nc.tensor.matmul(g.PS[pi][:, 0:256], g.cf[:, cst:cst + 128], dAm[:], start=True, stop=True),
                      reads=[dA_r, g.cf_r], writes=[g.PSr[pi]])
                if doexp:
                    kb.op("act", lambda pi=pi, dstt=dstt: nc.scalar.activation(out=dstt[:], in_=g.PS[pi][:, 0:256], func=AF.Exp), reads=[g.PSr[pi]], writes=[dres])
                else:
                    kb.op("dve", lambda pi=pi, dstt=dstt: nc.vector.tensor_copy(out=dstt[:], in_=g.PS[pi][:, 0:256]), reads=[g.PSr[pi]], writes=[dres])
            kb.barrier()
        hst = sb("hst", [128, 1024], F32)
        prevb = sb("prevb", [128, 1024], BF16)
        hst_r, prevb_r = Res(), Res()
        acTc = [sb(f"acTc{i}", [16, 128], F32) for i in range(2)]
        acTc_r = [Res(), Res()]
        tsegs = [sb(f"tseg{i}", [128, 512], F32) for i in range(2)]
        tsegs_r = [Res(), Res()]
        Ed = [sb(f"Ed{i}", [128, 512], BF16) for i in range(2)]
        Ed_r = [Res(), Res()]
        Ea = [sb(f"Ea{i}", [128, 512], BF16) for i in range(2)]
        Ea_r = [Res(), Res()]
        MT = [sb(f"MT{i}", [128, 4, 128], BF16) for i in range(2)]
        MT_r = [Res(), Res()]
        CsT = [sb(f"CsT{i}", [128, 4, 128], BF16) for i in range(2)]
        CsT_r = [Res(), Res()]
        cbm = [sb(f"cbm{i}", [128, 2, 128], BF16) for i in range(2)]
        cbm_r = [Res(), Res()]
        xdt = sb("xdt", [128, 1024], BF16)
        xdt_r = Res()
        xdd = sb("xdd", [128, 1024], BF16)
        xdd_r = Res()
        Btm = sb("Btm", [128, 256], BF16)
        Btm_r = Res()
        siz = sb("siz", [128, 8, 512], BF16)
        siz_r = Res()
        yg = sb("ygC", [128, 8, 128], F32)
        yg_r = Res()
        sqc = sb("sqC", [128, 8, 128], BF16)
        sqc_r = Res()
        rsc = sb("rsC", [128, 2, 128], F32)
        rsc_r = Res()
        ycb = [sb(f"ycb{i}", [128, 8, 128], BF16) for i in range(2)]
        ycb_r = [Res(), Res()]
        tmpc = sb("tmpC", [128, 1024], F32)
        tmpc_r = Res()
        mg_host(g, s, wp, prot)
        for c in range(16):
            cb2 = c % 2
            cs = slice(c * 128, (c + 1) * 128)
            if c % 4 == 0:
                tg = c // 4
                for t in range(8):
                    if t % 2 == 0:
                        wbuf, wres = wp.load([(win(g, l, OFF_CZ + t * 128, 256), 256)])
                    pi = prot.next()
                    proj_fm(g, wbuf, wres, (t % 2) * 128, tg, pi)
                    kb.op("act", lambda pi=pi, t=t: nc.scalar.activation(out=siz[:, t, :], in_=g.PS[pi][:, :], func=AF.Silu), reads=[g.PSr[pi]], writes=[siz_r])
            pT = prot.next()
            kb.op("pe", lambda pT=pT, c=c: nc.tensor.transpose(g.PS[pT][0:16, 0:128], acol[:, c * 16:(c + 1) * 16], g.cf[:, C_ID:C_ID + 128]),
                  reads=[acol_r, g.cf_r], writes=[g.PSr[pT]])
            kb.op("dve", lambda pT=pT, cb2=cb2: nc.vector.tensor_copy(out=acTc[cb2][:], in_=g.PS[pT][0:16, 0:128]), reads=[g.PSr[pT]], writes=[acTc_r[cb2]])
            pX = 2
            pXb = g.PS[pX][:].bitcast(BF16)

            def trx(cs=cs, pXb=pXb):
                ins = None
                for t in range(8):
                    ins = nc.tensor.transpose(pXb[:, t * 128:(t + 1) * 128], xbc[:, t, cs], g.idb[:])
                return ins
            kb.op("pe", trx, reads=xbc_r[0:8] + [g.cb_r], writes=[g.PSr[pX]])
            kb.op("dve", lambda c=c, pXb=pXb: nc.vector.tensor_tensor(
                out=xdt[:].rearrange("p (e d) -> p e d", d=64), in0=pXb[:, 0:1024].rearrange("p (e d) -> p e d", d=64),
                in1=dtm[:, c * 16:(c + 1) * 16].unsqueeze(2).to_broadcast([128, 16, 64]), op=ALU.mult), reads=[g.PSr[pX], dt_r], writes=[xdt_r])
            kb.op("dve", lambda c=c: nc.vector.tensor_tensor(
                out=xdd[:].rearrange("p (e d) -> p e d", d=64), in0=xdt[:].rearrange("p (e d) -> p e d", d=64),
                in1=dst_[:, c * 16:(c + 1) * 16].unsqueeze(2).to_broadcast([128, 16, 64]), op=ALU.mult), reads=[xdt_r, dst_r], writes=[xdd_r])
            pB = 3
            pBb = g.PS[pB][:].bitcast(BF16)

            def trb(cs=cs, pBb=pBb):
                nc.tensor.transpose(pBb[:, 0:128], xbc[:, 8, cs], g.idb[:])
                return nc.tensor.transpose(pBb[:, 128:256], xbc[:, 9, cs], g.idb[:])
            kb.op("pe", trb, reads=[xbc_r[8], xbc_r[9], g.cb_r], writes=[g.PSr[pB]])
            kb.op("act", lambda pBb=pBb: nc.scalar.copy(out=Btm[:], in_=pBb[:, 0:256]), reads=[g.PSr[pB]], writes=[Btm_r])
            pCB = 3

            def cbmm(cs=cs):
                nc.tensor.matmul(g.PS[pCB][:, 256:384], xbc[:, 8, cs], xbc[:, 10, cs], start=True, stop=True)
                return nc.tensor.matmul(g.PS[pCB][:, 384:512], xbc[:, 9, cs], xbc[:, 11, cs], start=True, stop=True)
            kb.op("pe", cbmm, reads=xbc_r[8:12], writes=[g.PSr[pCB]])
            kb.op("dve", lambda cb2=cb2: nc.vector.tensor_tensor(out=cbm[cb2][:], in0=g.PS[pCB][:, 256:512].rearrange("p (g l) -> p g l", g=2),
                                                                in1=g.trib[:].unsqueeze(1).to_broadcast([128, 2, 128]), op=ALU.mult),
                  reads=[g.PSr[pCB], g.cb_r], writes=[cbm_r[cb2]])
            pY = [6, 7]
            def emit_bcm(hq, cb2=cb2):
                pa = 4 + hq % 2

                def bcm():
                    ins = None
                    for e4 in range(4):
                        e = hq * 4 + e4
                        ins = nc.tensor.matmul(g.PS[pa][:, e4 * 128:(e4 + 1) * 128], blk[0:16, e * 128:(e + 1) * 128], acTc[cb2][:, :], start=True, stop=True)
                    return ins
                kb.op("pe", bcm, reads=[acTc_r[cb2], blk_r], writes=[g.PSr[pa]])
            emit_bcm(0)
            for hq in range(4):
                pa = 4 + hq % 2
                if hq < 3:
                    emit_bcm(hq + 1)
                mg_run(g, 4)
                b2 = hq % 2
                tseg, tseg_r = tsegs[b2], tsegs_r[b2]
                for e4 in range(4):
                    e = hq * 4 + e4
                    kb.op("dve", lambda pa=pa, e4=e4, e=e, c=c, tseg=tseg: nc.vector.tensor_scalar(
                        out=tseg[:, e4 * 128:(e4 + 1) * 128], in0=g.PS[pa][:, e4 * 128:(e4 + 1) * 128], scalar1=acol[:, c * 16 + e:c * 16 + e + 1],
                        scalar2=g.kc[:, 0:1], op0=ALU.subtract, op1=ALU.min), reads=[g.PSr[pa], acol_r, g.kc_r], writes=[tseg_r])
                kb.op("act", lambda b2=b2, tseg=tseg: nc.scalar.activation(out=Ed[b2][:], in_=tseg[:], func=AF.Exp), reads=[tseg_r], writes=[Ed_r[b2]])
                kb.op("act", lambda b2=b2, pa=pa: nc.scalar.activation(out=Ea[b2][:], in_=g.PS[pa][:, :], func=AF.Exp), reads=[g.PSr[pa]], writes=[Ea_r[b2]])
                grp = hq // 2
                kb.op("dve", lambda b2=b2, cb2=cb2, grp=grp: nc.vector.tensor_tensor(out=MT[b2][:], in0=Ed[b2][:].rearrange("p (e l) -> p e l", e=4),
                                                                                    in1=cbm[cb2][:, grp, :].unsqueeze(1).to_broadcast([128, 4, 128]), op=ALU.mult),
                      reads=[Ed_r[b2], cbm_r[cb2]], writes=[MT_r[b2]])
                kb.op("dve", lambda b2=b2, grp=grp, cs=cs: nc.vector.tensor_tensor(out=CsT[b2][:], in0=Ea[b2][:].rearrange("p (e l) -> p e l", e=4),
                                                                                  in1=xbc[:, 10 + grp, cs].unsqueeze(1).to_broadcast([128, 4, 128]), op=ALU.mult),
                      reads=[Ea_r[b2], xbc_r[10 + grp]], writes=[CsT_r[b2]])
                py = pY[hq // 2]

                def ymm(py=py, b2=b2, hq=hq, c=c):
                    ins = None
                    for e4 in range(4):
                        e = hq * 4 + e4
                        lo = (e % 2) * 64
                        cc = ((e // 2) % 4) * 128
                        ins = nc.tensor.matmul(g.PS[py][lo:lo + 64, cc:cc + 128], xdt[:, e * 64:(e + 1) * 64], MT[b2][:, e4, :], start=True, stop=(c == 0))
                        if c > 0:
                            ins = nc.tensor.matmul(g.PS[py][lo:lo + 64, cc:cc + 128], prevb[:, e * 64:(e + 1) * 64], CsT[b2][:, e4, :], start=False, stop=True)
                    return ins
                kb.op("pe", ymm, reads=[xdt_r, MT_r[b2], prevb_r, CsT_r[b2]], writes=[g.PSr[py]])
            kb.op("dve", lambda cs=cs: nc.vector.tensor_tensor(out=tmpc[:].rearrange("p (t l) -> p t l", t=8), in0=xbc[:, 0:8, cs],
                                                              in1=pc[:, 32:40].unsqueeze(2).to_broadcast([128, 8, 128]), op=ALU.mult),
                  reads=xbc_r[0:8] + [pc_r], writes=[tmpc_r])
            for gi in range(2):
                kb.op("dve", lambda gi=gi: nc.vector.tensor_tensor(out=yg[:, gi * 4:(gi + 1) * 4, :].rearrange("p t l -> p (t l)"), in0=g.PS[pY[gi]][:, :],
                                                                  in1=tmpc[:, gi * 512:(gi + 1) * 512], op=ALU.add), reads=[g.PSr[pY[gi]], tmpc_r], writes=[yg_r])
            zc = slice((c % 4) * 128, (c % 4 + 1) * 128)
            kb.op("dve", lambda zc=zc: nc.vector.tensor_tensor(out=yg[:], in0=yg[:], in1=siz[:, :, zc], op=ALU.mult), reads=[yg_r, siz_r], writes=[yg_r])
            kb.op("act", lambda: nc.scalar.activation(out=sqc[:], in_=yg[:], func=AF.Square), reads=[yg_r], writes=[sqc_r])
            pR = 2

            def rmm():
                ins = None
                for gi in range(2):
                    for t4 in range(4):
                        ins = nc.tensor.matmul(g.PS[pR][:, gi * 128:(gi + 1) * 128], g.onesb[:], sqc[:, gi * 4 + t4, :], start=(t4 == 0), stop=(t4 == 3))
                return ins
            kb.op("pe", rmm, reads=[sqc_r, g.cb_r], writes=[g.PSr[pR]])
            kb.op("act", lambda: nc.scalar.activation(out=rsc[:].rearrange("p g l -> p (g l)"), in_=g.PS[pR][:, 0:256], func=AF.Sqrt, scale=1.0 / 512, bias=g.kc[:, 2:3]),
                  reads=[g.PSr[pR], g.kc_r], writes=[rsc_r])
            kb.op("dve", lambda: nc.vector.reciprocal(out=rsc[:], in_=rsc[:]), reads=[rsc_r], writes=[rsc_r])
            for gi in range(2):
                kb.op("dve", lambda gi=gi: nc.vector.tensor_tensor(out=yg[:, gi * 4:(gi + 1) * 4, :], in0=yg[:, gi * 4:(gi + 1) * 4, :],
                                                                  in1=rsc[:, gi, :].unsqueeze(1).to_broadcast([128, 4, 128]), op=ALU.mult), reads=[yg_r, rsc_r], writes=[yg_r])
            kb.op("dve", lambda cb2=cb2: nc.vector.tensor_tensor(out=ycb[cb2][:], in0=yg[:], in1=pc[:, 40:48].unsqueeze(2).to_broadcast([128, 8, 128]), op=ALU.mult),
                  reads=[yg_r, pc_r], writes=[ycb_r[cb2]])
            kb.dma("sp", g.ybr[2].rearrange("(t p) l -> p t l", p=128)[:, :, cs], ycb[cb2][:], reads=[ycb_r[cb2]], writes=[g.ybr_r])
            if c < 15:
                pS2 = [4, 5]
                for gi in range(2):
                    kb.op("pe", lambda gi=gi: nc.tensor.matmul(g.PS[pS2[gi]][:, :], Btm[:, gi * 128:(gi + 1) * 128], xdd[:, gi * 512:(gi + 1) * 512], start=True, stop=True),
                          reads=[Btm_r, xdd_r], writes=[g.PSr[pS2[gi]]])
                if c == 0:
                    for gi in range(2):
                        kb.op("dve", lambda gi=gi: nc.vector.tensor_copy(out=hst[:, gi * 512:(gi + 1) * 512], in_=g.PS[pS2[gi]][:, :]), reads=[g.PSr[pS2[gi]]], writes=[hst_r])
                else:
                    kb.op("dve", lambda c=c: nc.vector.tensor_tensor(out=hst[:].rearrange("p (e d) -> p e d", d=64), in0=hst[:].rearrange("p (e d) -> p e d", d=64),
                                                                    in1=cdb[:, c * 16:(c + 1) * 16].unsqueeze(2).to_broadcast([128, 16, 64]), op=ALU.mult),
                          reads=[hst_r, cdb_r], writes=[hst_r])
                    for gi in range(2):
                        kb.op("dve", lambda gi=gi: nc.vector.tensor_tensor(out=hst[:, gi * 512:(gi + 1) * 512], in0=hst[:, gi * 512:(gi + 1) * 512], in1=g.PS[pS2[gi]][:, :], op=ALU.add),
                              reads=[hst_r, g.PSr[pS2[gi]]], writes=[hst_r])
                kb.op("act", lambda: nc.scalar.copy(out=prevb[:], in_=hst[:]), reads=[hst_r], writes=[prevb_r])


def phase3(g, l, src, dst):
    nc, kb = g.nc, g.kb
    g.xres_r_new = Res()
    with ExitStack() as s:
        def sb(name, shape, dt):
            return s.enter_context(_sbt(nc, name, list(shape), dt))
        yh = sb("yh", [128, 32, 1024], BF16)
        mT = sb("mT", [128, 16, 1024], BF16)
        yh_r = Res()
        mT_r = [Res(), Res()]
        wbr = WPool(g, s, "wbr", 3, 8, 128)
        wo = WPool(g, s, "wo", 2, 16, 256)
        acc = [sb(f"acc{i}", [128, 512], F32) for i in range(2)]
        acc_r = [Res(), Res()]
        NG = 6
        gt = [sb(f"gt3{i}", [128, 512], BF16) for i in range(NG)]
        gt_r = [Res() for _ in range(NG)]
        pr = [sb(f"pr3{i}", [128, 512], F32) for i in range(2)]
        pr_r = [Res(), Res()]
        xt = [sb(f"xt3{i}", [128, 256], F32) for i in range(4)]
        xt_r = [Res() for _ in range(4)]
        brot = Rot([0, 1, 2, 3, 4, 5])
        orot = Rot([6, 7])
        k = 0
        for hf in range(2):
            ts = slice(hf * 1024, (hf + 1) * 1024)
            for i in range(4):
                kb.dma("sp", yh[:, i * 8:(i + 1) * 8, :], g.ybr[i].rearrange("(t p) l -> p t l", p=128)[:, :, ts], reads=[g.ybr_r], writes=[yh_r])
            for dt_ in range(16):
                for i in range(4):
                    wb_, wb_r = wbr.load([(g.w_branch[l, i, :, dt_ * 128:(dt_ + 1) * 128], 128)])
                    for tg in range(2):
                        pb = brot.next()
                        k += 1
                        gi = k % NG
                        si = k % 2
                        r0 = i * 2048 + dt_ * 128
                        c0 = hf * 1024 + tg * 512
                        kb.dma("sp", gt[gi][:], g.sgd[r0:r0 + 128, c0:c0 + 512], reads=[g.sgd_r], writes=[gt_r[gi]])

                        def bmm(pb=pb, wb_=wb_, tg=tg, i=i):
                            ins = None
                            for rt in range(8):
                                ins = nc.tensor.matmul(g.PS[pb][:, :], wb_[:, rt, :], yh[:, i * 8 + rt, tg * 512:(tg + 1) * 512], start=(rt == 0), stop=(rt == 7))
                            return ins
                        kb.op("pe", bmm, reads=[wb_r, yh_r], writes=[g.PSr[pb]])
                        if i == 0:
                            kb.op("dve", lambda gi=gi, pb=pb, tg=tg: nc.vector.tensor_tensor(out=acc[tg][:], in0=g.PS[pb][:, :], in1=gt[gi][:], op=ALU.mult),
                                  reads=[g.PSr[pb], gt_r[gi]], writes=[acc_r[tg]])
                        else:
                            kb.op("dve", lambda gi=gi, si=si, pb=pb: nc.vector.tensor_tensor(out=pr[si][:], in0=g.PS[pb][:, :], in1=gt[gi][:], op=ALU.mult),
                                  reads=[g.PSr[pb], gt_r[gi]], writes=[pr_r[si]])
                            if i < 3:
                                kb.op("dve", lambda si=si, tg=tg: nc.vector.tensor_tensor(out=acc[tg][:], in0=acc[tg][:], in1=pr[si][:], op=ALU.add),
                                      reads=[acc_r[tg], pr_r[si]], writes=[acc_r[tg]])
                            else:
                                kb.op("dve", lambda si=si, tg=tg, dt_=dt_: nc.vector.tensor_tensor(out=mT[:, dt_, tg * 512:(tg + 1) * 512], in0=acc[tg][:], in1=pr[si][:], op=ALU.add),
                                      reads=[acc_r[tg], pr_r[si]], writes=[mT_r[tg]])
            iters = [(eg, tt) for eg in range(8) for tt in range(8)]

            def load_x(k):
                eg, tt = iters[k]
                xi = k % 4
                tok0 = hf * 1024 + tt * 128
                kb.dma("sp", xt[xi][:], src[tok0:tok0 + 128, eg * 256:(eg + 1) * 256], reads=[getattr(g, "xres_r", Res())], writes=[xt_r[xi]])
            load_x(0)
            load_x(1)
            for k, (eg, tt) in enumerate(iters):
                if tt == 0:
                    wo_, wo_r = wo.load([(g.w_out[l, :, eg * 256:(eg + 1) * 256], 256)])
                if k + 2 < len(iters):
                    load_x(k + 2)
                po = orot.next()
                tok0 = hf * 1024 + tt * 128
                xi = k % 4

                def omm(po=po, wo_=wo_, tt=tt):
                    ins = None
                    for dt_ in range(16):
                        ins = nc.tensor.matmul(g.PS[po][:, 0:256], mT[:, dt_, tt * 128:(tt + 1) * 128], wo_[:, dt_, :], start=(dt_ == 0), stop=(dt_ == 15))
                    return ins
                kb.op("pe", omm, reads=[wo_r, mT_r[tt // 4]], writes=[g.PSr[po]])
                kb.op("dve", lambda xi=xi, po=po: nc.vector.tensor_tensor(out=xt[xi][:], in0=xt[xi][:], in1=g.PS[po][:, 0:256], op=ALU.add),
                      reads=[xt_r[xi], g.PSr[po]], writes=[xt_r[xi]])
                kb.dma("sp", dst[tok0:tok0 + 128, eg * 256:(eg + 1) * 256], xt[xi][:], reads=[xt_r[xi]], writes=[g.xres_r_new])
    g.xres_r = g.xres_r_new


def phase_final(g, src):
    nc, kb = g.nc, g.kb
    with ExitStack() as s:
        wbc = s.enter_context(_sbt(nc, "wbcF", [128, D], F32))
        xt = [s.enter_context(_sbt(nc, f"xtF{i}", [128, D], F32)) for i in range(2)]
        ot = [s.enter_context(_sbt(nc, f"otF{i}", [128, D], F32)) for i in range(2)]
        junk = s.enter_context(_sbt(nc, "junkF", [128, D], BF16))
        st = [s.enter_context(_sbt(nc, f"stF{i}", [128, 4], F32)) for i in range(2)]
        wbc_r, junk_r = Res(), Res()
        xt_r, ot_r, st_r = [Res(), Res()], [Res(), Res()], [Res(), Res()]
        y_r = Res()
        kb.dma("sp", wbc[:], g.final_norm_w[0:1, :].partition_broadcast(128), writes=[wbc_r])
        for t in range(16):
            b = t % 2
            kb.dma("sp", xt[b][:], src[t * 128:(t + 1) * 128, :], reads=[g.xres_r], writes=[xt_r[b]])
            kb.op("act", lambda b=b: nc.scalar.activation(out=junk[:], in_=xt[b][:], func=AF.Square, accum_out=st[b][:, 0:1]), reads=[xt_r[b]], writes=[junk_r, st_r[b]])
            kb.op("act", lambda b=b: nc.scalar.activation(out=st[b][:, 1:2], in_=st[b][:, 0:1], func=AF.Sqrt, scale=1.0 / D, bias=g.kc[:, 2:3]),
                  reads=[st_r[b], g.kc_r], writes=[st_r[b]])
            kb.op("dve", lambda b=b: nc.vector.reciprocal(out=st[b][:, 2:3], in_=st[b][:, 1:2]), reads=[st_r[b]], writes=[st_r[b]])
            kb.op("dve", lambda b=b: nc.vector.scalar_tensor_tensor(out=ot[b][:], in0=xt[b][:], scalar=st[b][:, 2:3], in1=wbc[:], op0=ALU.mult, op1=ALU.mult),
                  reads=[xt_r[b], st_r[b], wbc_r], writes=[ot_r[b]])
            kb.dma("sp", g.y[t * 128:(t + 1) * 128, :], ot[b][:], reads=[ot_r[b]], writes=[y_r])


_CACHE = {}


def kernel(**inputs):
    dbg = inputs.pop("_dbg", None)
    key = tuple(sorted(dbg)) if dbg else None
    if key not in _CACHE:
        _CACHE[key] = build(dbg)
    nc, g = _CACHE[key]
    consts, cblk = make_consts()
    x = np.ascontiguousarray(inputs["x"], dtype=np.float32)
    shared = {
        "norm_w": inputs["norm_w"], "w_in": inputs["w_in"],
        "diff_lambda": np.asarray(inputs["diff_lambda"]).reshape(DEPTH, 256),
        "diff_subln_w": inputs["diff_subln_w"], "swa_sinks": inputs["swa_sinks"],
        "ssd_conv_w": inputs["ssd_conv_w"], "ssd_conv_b": inputs["ssd_conv_b"], "ssd_dt_bias": inputs["ssd_dt_bias"],
        "ssd_a_log": inputs["ssd_a_log"], "ssd_d": inputs["ssd_d"], "ssd_norm_w": inputs["ssd_norm_w"],
        "conf_conv_w": inputs["conf_conv_w"], "conf_conv_b": inputs["conf_conv_b"], "conf_ln_w": inputs["conf_ln_w"],
        "conf_ln_b": inputs["conf_ln_b"], "w_branch": inputs["w_branch"], "w_out": inputs["w_out"],
        "rel_bias": inputs["rel_bias"], "final_norm_w": np.asarray(inputs["final_norm_w"]).reshape(1, D),
        "consts": consts, "cblk": cblk,
    }
    shared = {k: np.ascontiguousarray(v, dtype=np.float32) for k, v in shared.items()}
    ncores = 1 if dbg else 4
    in_maps = []
    for c in range(ncores):
        m = dict(shared)
        m["x"] = x[c % 4]
        in_maps.append(m)
    res = run_bass_kernel_spmd(nc, in_maps, core_ids=list(range(ncores)))
    if dbg:
        return res
    return np.stack([np.asarray(res.results[b]["y"]) for b in range(4)], axis=0).astype(np.float32)
```

```python
import math
from contextlib import ExitStack
import numpy as np
import concourse.bass as bass
import concourse.mybir as mybir
from concourse.bass_utils import run_bass_kernel_spmd

F32 = mybir.dt.float32
BF16 = mybir.dt.bfloat16
AF = mybir.ActivationFunctionType
ALU = mybir.AluOpType
AX = mybir.AxisListType

T = 2048
D = 2048
NIN = 20496
DEPTH = 2
EPS = 1e-6
NEG = -30000.0
OFF_AQ, OFF_AK, OFF_AV, OFF_AG = 0, 1024, 2048, 3072
OFF_BQ, OFF_BK, OFF_BV, OFF_BG = 4096, 5120, 5376, 5632
OFF_CX, OFF_CDT, OFF_CZ = 6656, 8192, 8208
OFF_DGLU, OFF_DG, OFF_MG = 9232, 11280, 12304

C_ID, C_AJ, C_TRI, C_TRIU, C_OH, C_MROW, C_SELA, C_OH31, C_ONES = 0, 128, 256, 384, 512, 896, 1280, 1281, 1409
NCONST = 1537


def t5_bucket_np(n):
    n = np.maximum(n, 0)
    nf = np.maximum(n, 1).astype(np.float32)
    large = 16 + (np.log(nf / np.float32(16)) / np.float32(math.log(128 / 16)) * np.float32(16)).astype(np.int32)
    large = np.minimum(large, 31)
    return np.where(n < 16, n, large)


def make_consts():
    c = np.zeros((128, NCONST), np.float32)
    p = np.arange(128)
    c[:, C_ID:C_ID + 128] = np.eye(128)
    c[p, C_AJ + 127 - p] = 1.0
    c[:, C_TRI:C_TRI + 128] = (p[:, None] <= p[None, :])
    c[:, C_TRIU:C_TRIU + 128] = (p[:, None] > p[None, :])
    m = np.arange(384)
    dist = m - 128
    bk = t5_bucket_np(dist)
    for j in range(384):
        if dist[j] >= 0:
            c[bk[j], C_OH + j] = 1.0
    c[0:8, C_MROW:C_MROW + 128] = NEG
    c[8:24, C_MROW:C_MROW + 128] = NEG
    c[8:24, C_MROW + 256:C_MROW + 384] = NEG
    c[0:8, C_SELA] = 1.0
    c[31, C_OH31:C_OH31 + 128] = 1.0
    c[:, C_ONES:C_ONES + 128] = 1.0
    blk = np.zeros((16, 2048), np.float32)
    for e in range(16):
        blk[e, e * 128:(e + 1) * 128] = 1.0
    return c, blk


_UNIQ = [0]


def _sbt(nc, name, shape, dt):
    _UNIQ[0] += 1
    return nc.sbuf_tensor(f"{name}_u{_UNIQ[0]}", list(shape), dt)


class Res:
    __slots__ = ("w", "r", "name")

    def __init__(self, name=""):
        self.w = None
        self.r = {}
        self.name = name


class KB:
    def __init__(self, nc):
        self.nc = nc
        self.eng = {"pe": nc.tensor, "act": nc.scalar, "dve": nc.vector, "pool": nc.gpsimd, "sp": nc.sync}
        self.sems = {}
        self.cnt = {}
        self.seen = {}
        for n in self.eng:
            self.sems[n] = nc.semaphore("e_" + n).__enter__()
            self.cnt[n] = 0
            self.seen[n] = {}
        self.dq = {}
        for q, issuer, ns in (("sp", "sp", 28), ("pool", "pool", 12), ("act", "act", 6)):
            sl = [nc.semaphore(f"d_{q}{i}").__enter__() for i in range(ns)]
            for i, s in enumerate(sl):
                self.sems[("d", q, i)] = s
            self.dq[q] = dict(issuer=issuer, n=ns, uses=[0] * ns, nxt=0)
        self.ninst = 0

    def _deps(self, reads, writes):
        d = {}
        for r in reads:
            if r.w is not None and r.w[1] > d.get(r.w[0], 0):
                d[r.w[0]] = r.w[1]
        for w in writes:
            if w.w is not None and w.w[1] > d.get(w.w[0], 0):
                d[w.w[0]] = w.w[1]
            for k, v in w.r.items():
                if v > d.get(k, 0):
                    d[k] = v
        return d

    def _wait(self, issuer, deps, skip_self=False):
        e = self.eng[issuer]
        seen = self.seen[issuer]
        for k, v in deps.items():
            if skip_self and k == issuer:
                continue
            if seen.get(k, 0) >= v:
                continue
            e.wait_ge(self.sems[k], v)
            seen[k] = v
            self.ninst += 1

    def _mark(self, key, val, reads, writes):
        for r in reads:
            if r.r.get(key, 0) < val:
                r.r[key] = val
        for w in writes:
            w.w = (key, val)
            w.r = {}

    def op(self, e, fn, reads=(), writes=()):
        self._wait(e, self._deps(reads, writes), skip_self=(e == "pe"))
        ins = fn()
        self.cnt[e] += 1
        ins.then_inc(self.sems[e], 1)
        self._mark(e, self.cnt[e], reads, writes)
        self.ninst += 1

    def dma(self, q, out, in_, reads=(), writes=(), **kw):
        Q = self.dq[q]
        issuer = Q["issuer"]
        deps = self._deps(reads, writes)
        slot = Q["nxt"]
        Q["nxt"] = (slot + 1) % Q["n"]
        key = ("d", q, slot)
        if Q["uses"][slot] > 0:
            deps[key] = max(deps.get(key, 0), 16 * Q["uses"][slot])
        self._wait(issuer, deps)
        Q["uses"][slot] += 1
        val = 16 * Q["uses"][slot]
        self.eng[issuer].dma_start(out=out, in_=in_, **kw).then_inc(self.sems[key], 16)
        self._mark(key, val, reads, writes)
        self.ninst += 1

    def barrier(self):
        tot = {}
        for n in self.eng:
            if self.cnt[n] > 0:
                tot[n] = self.cnt[n]
        for q, Q in self.dq.items():
            for i in range(Q["n"]):
                if Q["uses"][i] > 0:
                    tot[("d", q, i)] = 16 * Q["uses"][i]
        for n in self.eng:
            self._wait(n, tot)


class Ctx:
    pass


def build(dbg=None):
    nc = bass.Bass("TRN2", target_bir_lowering=False)
    kb = KB(nc)
    g = Ctx()
    g.nc, g.kb, g.dbg = nc, kb, dbg

    def din(name, shape, dt=F32):
        return nc.dram_tensor(name, list(shape), dt, kind="ExternalInput").ap()

    g.x = din("x", [T, D])
    g.norm_w = din("norm_w", [DEPTH, D])
    g.w_in = din("w_in", [DEPTH, D, NIN])
    g.diff_lambda = din("diff_lambda", [DEPTH, 256])
    g.diff_subln_w = din("diff_subln_w", [DEPTH, 128])
    g.swa_sinks = din("swa_sinks", [DEPTH, 16])
    g.ssd_conv_w = din("ssd_conv_w", [DEPTH, 4, 1536])
    g.ssd_conv_b = din("ssd_conv_b", [DEPTH, 1536])
    g.ssd_dt_bias = din("ssd_dt_bias", [DEPTH, 16])
    g.ssd_a_log = din("ssd_a_log", [DEPTH, 16])
    g.ssd_d = din("ssd_d", [DEPTH, 16])
    g.ssd_norm_w = din("ssd_norm_w", [DEPTH, 1024])
    g.conf_conv_w = din("conf_conv_w", [DEPTH, 31, 1024])
    g.conf_conv_b = din("conf_conv_b", [DEPTH, 1024])
    g.conf_ln_w = din("conf_ln_w", [DEPTH, 1024])
    g.conf_ln_b = din("conf_ln_b", [DEPTH, 1024])
    g.w_branch = din("w_branch", [DEPTH, 4, 1024, D])
    g.w_out = din("w_out", [DEPTH, D, D])
    g.rel_bias = din("rel_bias", [32, 24])
    g.final_norm_w = din("final_norm_w", [1, D])
    g.consts = din("consts", [128, NCONST])
    g.cblk = din("cblk", [16, 2048])
    g.y = nc.dram_tensor("y", [T, D], F32, kind="ExternalOutput").ap()
    g.xres = [nc.dram_tensor(f"xres{i}", [T, D], F32).ap() for i in range(2)]
    g.ybr = nc.dram_tensor("ybr", [4, 1024, T], BF16).ap()
    g.sgd = nc.dram_tensor("sgd", [8192, T], BF16).ap()
    g.mgk = [0]
    g.tdram = nc.dram_tensor("tdram", [24, 384], F32).ap()
    g.dbg_out = {}
    if dbg:
        if "hT" in dbg:
            g.dbg_out["hT"] = nc.dram_tensor("dbg_hT", [128, 16, T], BF16, kind="ExternalOutput").ap()
        if "ybr" in dbg:
            g.dbg_out["ybr"] = nc.dram_tensor("dbg_ybr", [4, 1024, T], BF16, kind="ExternalOutput").ap()
        if "x0" in dbg:
            g.dbg_out["x0"] = nc.dram_tensor("dbg_x0", [T, D], F32, kind="ExternalOutput").ap()

    st = ExitStack()

    def sb(name, shape, dt):
        return st.enter_context(_sbt(nc, name, list(shape), dt))

    g.PS = [st.enter_context(nc.psum_tensor(f"ps{i}", [128, 512], F32)) for i in range(8)]
    g.PSr = [Res(f"ps{i}") for i in range(8)]

    g.cf = sb("cf", [128, NCONST], F32)
    g.cf_r = Res()
    kb.dma("sp", g.cf[:], g.consts, writes=[g.cf_r])
    g.idb = sb("idb", [128, 128], BF16)
    g.onesb = sb("onesb", [128, 128], BF16)
    g.trib = sb("trib", [128, 128], BF16)
    g.cb_r = Res()
    kb.op("dve", lambda: nc.vector.tensor_copy(out=g.idb[:], in_=g.cf[:, C_ID:C_ID + 128]), reads=[g.cf_r], writes=[g.cb_r])
    kb.op("dve", lambda: nc.vector.tensor_copy(out=g.onesb[:], in_=g.cf[:, C_ONES:C_ONES + 128]), reads=[g.cf_r], writes=[g.cb_r])
    kb.op("dve", lambda: nc.vector.tensor_copy(out=g.trib[:], in_=g.cf[:, C_TRI:C_TRI + 128]), reads=[g.cf_r], writes=[g.cb_r])
    g.kc = sb("kc", [128, 8], F32)
    g.kc_r = Res()
    for i, v in enumerate([0.0, 8.0, EPS, 1.0 / 1024, -1.0, 1.0 / 512]):
        kb.op("dve", lambda i=i, v=v: nc.vector.memset(g.kc[:, i:i + 1], v), writes=[g.kc_r])

    srcs = [g.x, g.xres[0], g.xres[1]]
    for l in range(DEPTH):
        with ExitStack() as ls:
            g.ls = ls
            g.hT = ls.enter_context(_sbt(nc, f"hT{l}", [128, 16, T], BF16))
            g.hT_r = [Res(f"hT{tg}") for tg in range(4)]
            phase_norm(g, l, srcs[l])
            g.sgd_r = Res()
            g.mgjobs = mg_job_list(g, l)
            if dbg and "hT" in dbg and l == 0:
                kb.dma("sp", g.dbg_out["hT"], g.hT[:], reads=g.hT_r)
            kb.barrier()
            with ExitStack() as bs:
                g.bt = bs.enter_context(_sbt(nc, f"bt{l}", [128, 2, 24, 128], BF16))
                g.bt_r = Res()
                g.c31 = bs.enter_context(_sbt(nc, f"c31{l}", [128, 24], F32))
                g.c31_r = Res()
                setup_bias(g)
                if not (dbg and "skipA" in dbg):
                    mixer_A(g, l)
                    kb.barrier()
                if not (dbg and "skipB" in dbg):
                    mixer_B(g, l)
                    kb.barrier()
            if not (dbg and "skipC" in dbg):
                mixer_C(g, l)
                kb.barrier()
            if not (dbg and "skipD" in dbg):
                mixer_D(g, l)
                kb.barrier()
        if dbg and "ybr" in dbg and l == 0:
            kb.dma("sp", g.dbg_out["ybr"], g.ybr, reads=[g.ybr_r])
            kb.barrier()
        if dbg and "stop_mix" in dbg:
            break
        phase3(g, l, srcs[l], srcs[l + 1])
        kb.barrier()
        if dbg and "x0" in dbg and l == 0:
            kb.dma("sp", g.dbg_out["x0"], g.xres[0], reads=[g.xres_r])
            kb.barrier()
    if not (dbg and "stop_mix" in dbg):
        phase_final(g, srcs[DEPTH])
    kb.barrier()
    return nc, g


def setup_bias(g):
    nc, kb = g.nc, g.kb
    with ExitStack() as s:
        tab = s.enter_context(_sbt(nc, "tab", [32, 24], F32))
        tt = s.enter_context(_sbt(nc, "ttab", [24, 384], F32))
        csh = s.enter_context(_sbt(nc, "csh", [24, 1], F32))
        hk = s.enter_context(_sbt(nc, "hk", [128, 2, 24, 128], F32))
        tab_r, tt_r, csh_r, hk_r, td_r = Res(), Res(), Res(), Res(), Res()
        kb.dma("sp", tab[:], g.rel_bias, writes=[tab_r])
        ps = g.PS[0]
        kb.op("pe", lambda: nc.tensor.matmul(ps[0:24, 0:384], tab[0:32, 0:24], g.cf[0:32, C_OH:C_OH + 384], start=True, stop=True),
              reads=[tab_r, g.cf_r], writes=[g.PSr[0]])
        kb.op("dve", lambda: nc.vector.tensor_tensor(out=csh[:], in0=ps[0:24, 383:384], in1=g.cf[0:24, C_SELA:C_SELA + 1], op=ALU.mult),
              reads=[g.PSr[0], g.cf_r], writes=[csh_r])
        kb.op("dve", lambda: nc.vector.tensor_scalar(out=tt[:], in0=ps[0:24, 0:384], scalar1=csh[:, 0:1], scalar2=g.kc[0:24, 1:2],
                                                     op0=ALU.subtract, op1=ALU.mult), reads=[g.PSr[0], csh_r, g.kc_r], writes=[tt_r])
        kb.op("dve", lambda: nc.vector.tensor_tensor(out=tt[:], in0=tt[:], in1=g.cf[0:24, C_MROW:C_MROW + 384], op=ALU.add),
              reads=[tt_r, g.cf_r], writes=[tt_r])
        kb.dma("sp", g.tdram, tt[:], reads=[tt_r], writes=[td_r])
        for kind, off in ((0, 1), (1, 129)):
            src = bass.AP(g.tdram.tensor, off, [[1, 128], [384, 24], [1, 128]])
            kb.dma("sp", hk[:, kind], src, reads=[td_r], writes=[hk_r])
        for kind in range(2):
            for hg in range(6):
                pi = 1 + (kind * 6 + hg) % 4
                p = g.PS[pi]
                kb.op("pe", lambda p=p, kind=kind, hg=hg: nc.tensor.matmul(
                    p[:, :], g.cf[:, C_AJ:C_AJ + 128], hk[:, kind, hg * 4:(hg + 1) * 4, :].rearrange("p h q -> p (h q)"),
                    start=True, stop=True), reads=[hk_r, g.cf_r], writes=[g.PSr[pi]])
                kb.op("dve", lambda p=p, kind=kind, hg=hg: nc.vector.tensor_copy(
                    out=g.bt[:, kind, hg * 4:(hg + 1) * 4, :].rearrange("p h q -> p (h q)"), in_=p[:, :]),
                    reads=[g.PSr[pi]], writes=[g.bt_r])
        p = g.PS[5]
        kb.op("pe", lambda: nc.tensor.matmul(p[:, 0:24], g.cf[0:32, C_OH31:C_OH31 + 128], tab[0:32, 0:24], start=True, stop=True),
              reads=[tab_r, g.cf_r], writes=[g.PSr[5]])
        kb.op("dve", lambda: nc.vector.tensor_copy(out=g.c31[:], in_=p[:, 0:24]), reads=[g.PSr[5]], writes=[g.c31_r])
        kb.barrier()


def phase_norm(g, l, src):
    nc, kb = g.nc, g.kb
    with ExitStack() as s:
        wbc = s.enter_context(_sbt(nc, "wbc", [128, D], F32))
        xt = [s.enter_context(_sbt(nc, f"xt{i}", [128, D], F32)) for i in range(2)]
        hb = [s.enter_context(_sbt(nc, f"hb{i}", [128, D], BF16)) for i in range(2)]
        junk = s.enter_context(_sbt(nc, "junk", [128, D], BF16))
        st = [s.enter_context(_sbt(nc, f"st{i}", [128, 4], F32)) for i in range(2)]
        wbc_r, junk_r = Res(), Res()
        xt_r = [Res(), Res()]
        hb_r = [Res(), Res()]
        st_r = [Res(), Res()]
        kb.dma("sp", wbc[:], g.norm_w[l:l + 1, :].partition_broadcast(128), writes=[wbc_r])
        for t in range(16):
            b = t % 2
            kb.dma("sp", xt[b][:], src[t * 128:(t + 1) * 128, :], reads=[getattr(g, "xres_r", Res())], writes=[xt_r[b]])
            kb.op("act", lambda b=b: nc.scalar.activation(out=junk[:], in_=xt[b][:], func=AF.Square, accum_out=st[b][:, 0:1]),
                  reads=[xt_r[b]], writes=[junk_r, st_r[b]])
            kb.op("act", lambda b=b: nc.scalar.activation(out=st[b][:, 1:2], in_=st[b][:, 0:1], func=AF.Sqrt, scale=1.0 / D, bias=g.kc[:, 2:3]),
                  reads=[st_r[b], g.kc_r], writes=[st_r[b]])
            kb.op("dve", lambda b=b: nc.vector.reciprocal(out=st[b][:, 2:3], in_=st[b][:, 1:2]), reads=[st_r[b]], writes=[st_r[b]])
            kb.op("dve", lambda b=b: nc.vector.scalar_tensor_tensor(out=hb[b][:], in0=xt[b][:], scalar=st[b][:, 2:3], in1=wbc[:],
                                                                    op0=ALU.mult, op1=ALU.mult),
                  reads=[xt_r[b], st_r[b], wbc_r], writes=[hb_r[b]])
            for q4 in range(4):
                pi = (t * 4 + q4) % 8
                pb = g.PS[pi][:].bitcast(BF16)

                def tr(pb=pb, b=b, q4=q4):
                    ins = None
                    for j in range(4):
                        kt = q4 * 4 + j
                        ins = nc.tensor.transpose(pb[:, j * 128:(j + 1) * 128], hb[b][:, kt * 128:(kt + 1) * 128], g.idb[:])
                    return ins
                kb.op("pe", tr, reads=[hb_r[b], g.cb_r], writes=[g.PSr[pi]])
                dst = g.hT[:, q4 * 4:(q4 + 1) * 4, t * 128:(t + 1) * 128]
                srcp = pb[:, 0:512].rearrange("p (j q) -> p j q", j=4)
                if q4 % 2 == 0:
                    kb.op("act", lambda dst=dst, srcp=srcp: nc.scalar.copy(out=dst, in_=srcp), reads=[g.PSr[pi]], writes=[g.hT_r[t // 4]])
                else:
                    kb.op("dve", lambda dst=dst, srcp=srcp: nc.vector.tensor_copy(out=dst, in_=srcp), reads=[g.PSr[pi]], writes=[g.hT_r[t // 4]])


class WPool:
    def __init__(self, g, s, name, n, kt, cols):
        self.g = g
        self.bufs = [s.enter_context(_sbt(g.nc, f"{name}{i}", [128, kt, cols], BF16)) for i in range(n)]
        self.res = [Res(f"{name}{i}") for i in range(n)]
        self.i = 0
        self.kt = kt

    def load(self, pieces):
        i = self.i
        self.i = (i + 1) % len(self.bufs)
        c = 0
        for ap, ncols in pieces:
            self.g.kb.dma("pool", self.bufs[i][:, :, c:c + ncols], ap.rearrange("(kt p) n -> p kt n", p=128), writes=[self.res[i]])
            c += ncols
        return self.bufs[i], self.res[i]


def win(g, l, c0, n):
    return g.w_in[l, :, c0:c0 + n]


def proj_fm(g, wbuf, wres, coff, tg, pi, M=128):
    nc = g.nc
    ps = g.PS[pi]

    def f():
        ins = None
        for kt in range(16):
            ins = nc.tensor.matmul(ps[0:M, :], wbuf[:, kt, coff:coff + M], g.hT[:, kt, tg * 512:(tg + 1) * 512],
                                   start=(kt == 0), stop=(kt == 15))
        return ins
    g.kb.op("pe", f, reads=[wres, g.hT_r[tg]], writes=[g.PSr[pi]])


class Rot:
    def __init__(self, items):
        self.items = list(items)
        self.i = 0

    def next(self):
        v = self.items[self.i]
        self.i = (self.i + 1) % len(self.items)
        return v


def mg_job_list(g, l):
    nc, kb = g.nc, g.kb
    jobs = []
    for i in range(4):
        for dtp in range(8):
            stt = {}
            for dj in range(2):
                for tg in range(4):
                    def job(i=i, dtp=dtp, dj=dj, tg=tg, stt=stt, first=(dj == 0 and tg == 0)):
                        if first:
                            stt["w"] = g.mgpool.load([(win(g, l, OFF_MG + i * 2048 + dtp * 256, 256), 256)])
                        wbuf, wres = stt["w"]
                        pi = g.mgrot.next()
                        proj_fm(g, wbuf, wres, dj * 128, tg, pi)
                        k = g.mgk[0]
                        g.mgk[0] += 1
                        sgi = k % len(g.mgst)
                        kb.op("act", lambda: nc.scalar.activation(out=g.mgst[sgi][:], in_=g.PS[pi][:, :], func=AF.Sigmoid), reads=[g.PSr[pi]], writes=[g.mgst_r[sgi]])
                        r0 = i * 2048 + dtp * 256 + dj * 128
                        kb.dma("sp", g.sgd[r0:r0 + 128, tg * 512:(tg + 1) * 512], g.mgst[sgi][:], reads=[g.mgst_r[sgi]], writes=[g.sgd_r])
                    jobs.append(job)
    return jobs


def mg_host(g, s, pool, rot):
    g.mgpool = pool
    g.mgrot = rot
    g.mgst = [s.enter_context(_sbt(g.nc, f"mgst{i}", [128, 512], BF16)) for i in range(2)]
    g.mgst_r = [Res(), Res()]


def mg_run(g, n):
    for _ in range(n):
        if g.mgjobs:
            g.mgjobs.pop(0)()


def mixer_A(g, l):
    nc, kb = g.nc, g.kb
    lam_init = 0.8 - 0.6 * math.exp(-0.3 * l)
    if not hasattr(g, "ybr_r"):
        g.ybr_r = Res()
    with ExitStack() as s:
        def sb(name, shape, dt):
            return s.enter_context(_sbt(nc, name, list(shape), dt))
        wp = WPool(g, s, "wA", 2, 16, 512)
        dl = sb("dl", [128, 256], F32)
        sc = sb("scA", [128, 8], F32)
        dl_r, sc_r = Res(), Res()
        kb.dma("sp", dl[:], g.diff_lambda[l:l + 1, :].partition_broadcast(128), writes=[dl_r])
        kb.op("dve", lambda: nc.vector.tensor_tensor(out=dl[:, 0:64], in0=dl[:, 0:64], in1=dl[:, 64:128], op=ALU.mult), reads=[dl_r], writes=[dl_r])
        kb.op("dve", lambda: nc.vector.tensor_tensor(out=dl[:, 128:192], in0=dl[:, 128:192], in1=dl[:, 192:256], op=ALU.mult), reads=[dl_r], writes=[dl_r])
        kb.op("dve", lambda: nc.vector.reduce_sum(out=sc[:, 0:1], in_=dl[:, 0:64], axis=AX.X), reads=[dl_r], writes=[sc_r])
        kb.op("dve", lambda: nc.vector.reduce_sum(out=sc[:, 1:2], in_=dl[:, 128:192], axis=AX.X), reads=[dl_r], writes=[sc_r])
        kb.op("act", lambda: nc.scalar.activation(out=sc[:, 2:4], in_=sc[:, 0:2], func=AF.Exp), reads=[sc_r], writes=[sc_r])
        kb.op("dve", lambda: nc.vector.tensor_tensor(out=sc[:, 4:5], in0=sc[:, 3:4], in1=sc[:, 2:3], op=ALU.subtract), reads=[sc_r], writes=[sc_r])
        kb.op("dve", lambda: nc.vector.tensor_scalar_add(out=sc[:, 5:6], in0=sc[:, 4:5], scalar1=-lam_init), reads=[sc_r], writes=[sc_r])
        kb.dma("sp", sc[:, 6:7], g.diff_subln_w[l:l + 1, :].rearrange("o e -> e o"), writes=[sc_r], allow_slow_non_contiguous=True)
        kb.op("dve", lambda: nc.vector.tensor_scalar_mul(out=sc[:, 7:8], in0=sc[:, 6:7], scalar1=(1.0 - lam_init)), reads=[sc_r], writes=[sc_r])
        neglam = sc[:, 5:6]
        swcol = sc[:, 7:8]

        NB = 2
        qT = [sb(f"qT{i}", [128, T], BF16) for i in range(NB)]
        kT = [sb(f"kT{i}", [128, T], BF16) for i in range(NB)]
        vT = [sb(f"vT{i}", [128, T], BF16) for i in range(NB)]
        gT = [sb(f"gT{i}", [128, T], BF16) for i in range(NB)]
        Vt = [sb(f"Vt{i}", [128, 16, 128], BF16) for i in range(NB)]
        qT_r = [Res() for _ in range(NB)]
        kT_r = [Res() for _ in range(NB)]
        vT_r = [Res() for _ in range(NB)]
        gT_r = [Res() for _ in range(NB)]
        Vt_r = [Res() for _ in range(NB)]
        NE = 5
        Eb = [sb(f"Eb{i}", [128, 512], BF16) for i in range(NE)]
        Eb_r = [Res() for _ in range(NE)]
        erot = Rot(range(NE))
        f1 = [sb(f"fA{i}", [128, 512], F32) for i in range(6)]
        f1_r = [Res() for _ in range(6)]
        sqb = sb("sqb", [128, 512], BF16)
        sqb_r = Res()
        yb = [sb(f"ybA{i}", [128, 512], BF16) for i in range(2)]
        yb_r = [Res(), Res()]
        prot = Rot([0, 1, 2, 3])
        srot = prot
        PO = [4, 6]
        PSUMS = [5, 7]
        def inproj_closures(h):
            hb = h % NB
            stt = {}
            out = []

            def ld():
                stt["w"] = wp.load([(win(g, l, OFF_AQ + h * 128, 128), 128), (win(g, l, OFF_AK + h * 128, 128), 128),
                                    (win(g, l, OFF_AV + h * 128, 128), 128), (win(g, l, OFF_AG + h * 128, 128), 128)])
            out.append(ld)
            dsts = [(qT[hb], qT_r[hb]), (kT[hb], kT_r[hb]), (vT[hb], vT_r[hb]), (gT[hb], gT_r[hb])]
            for ti in (2, 1, 0, 3):
                for tg in range(4):
                    def grp(ti=ti, tg=tg):
                        wbuf, wres = stt["w"]
                        pi = prot.next()
                        proj_fm(g, wbuf, wres, ti * 128, tg, pi)
                        dst, dres = dsts[ti]
                        o = dst[:, tg * 512:(tg + 1) * 512]
                        if ti == 3:
                            kb.op("act", lambda: nc.scalar.activation(out=o, in_=g.PS[pi][:, :], func=AF.Silu), reads=[g.PSr[pi]], writes=[dres])
                        else:
                            kb.op("dve", lambda: nc.vector.tensor_copy(out=o, in_=g.PS[pi][:, :]), reads=[g.PSr[pi]], writes=[dres])
                    out.append(grp)
                if ti == 2:
                    for q4 in range(4):
                        def trv(q4=q4):
                            pi = prot.next()
                            pb = g.PS[pi][:].bitcast(BF16)

                            def tr():
                                ins = None
                                for j in range(4):
                                    tt = q4 * 4 + j
                                    ins = nc.tensor.transpose(pb[:, j * 128:(j + 1) * 128], vT[hb][:, tt * 128:(tt + 1) * 128], g.idb[:])
                                return ins
                            kb.op("pe", tr, reads=[vT_r[hb], g.cb_r], writes=[g.PSr[pi]])
                            kb.op("dve", lambda: nc.vector.tensor_copy(out=Vt[hb][:, q4 * 4:(q4 + 1) * 4, :], in_=pb[:, 0:512].rearrange("p (j q) -> p j q", j=4)),
                                  reads=[g.PSr[pi]], writes=[Vt_r[hb]])
                        out.append(trv)
            return out

        for c_ in inproj_closures(0):
            c_()
        for h in range(8):
            hb = h % NB
            nxt = inproj_closures(h + 1) if h + 1 < 8 else []
            def make_step(G, m, j, hb=hb, h=h):
                po, psm = PO[m], PSUMS[m]
                last = 4 * G + 3
                c0 = max(j - 4 * G, 0) * 128
                st = {}

                def emit_S():
                    si = srot.next()
                    pS = g.PS[si]

                    def smm():
                        nb = []
                        if j >= 4 * G:
                            nb.append((0, (j - 4 * G) * 128))
                        if 4 * G <= j + 1 <= 4 * G + 3:
                            nb.append((1, (j + 1 - 4 * G) * 128))
                        ins = nc.tensor.matmul(pS[:, c0:512], kT[hb][m * 64:(m + 1) * 64, j * 128:(j + 1) * 128],
                                               qT[hb][m * 64:(m + 1) * 64, G * 512 + c0:(G + 1) * 512], start=True, stop=(len(nb) == 0))
                        for bi, (kind, cc) in enumerate(nb):
                            ins = nc.tensor.matmul(pS[:, cc:cc + 128], g.idb[:], g.bt[:, kind, h, :], start=False, stop=(bi == len(nb) - 1))
                        return ins
                    kb.op("pe", smm, reads=[kT_r[hb], qT_r[hb], g.bt_r, g.cb_r], writes=[g.PSr[si]])
                    ei = erot.next()
                    st["ei"] = ei
                    kb.op("act", lambda: nc.scalar.activation(out=Eb[ei][:, c0:512], in_=pS[:, c0:512], func=AF.Exp, scale=0.125, bias=g.c31[:, h:h + 1]),
                          reads=[g.PSr[si], g.c31_r], writes=[Eb_r[ei]])

                def emit_PV():
                    ei = st["ei"]

                    def pv():
                        nc.tensor.matmul(g.PS[po][:, c0:512], Vt[hb][:, j, :], Eb[ei][:, c0:512], start=(j == 0), stop=(j == last))
                        return nc.tensor.matmul(g.PS[psm][:, c0:512], g.onesb[:], Eb[ei][:, c0:512], start=(j == 0), stop=(j == last))
                    kb.op("pe", pv, reads=[Vt_r[hb], Eb_r[ei], g.cb_r], writes=[g.PSr[po], g.PSr[psm]])
                    if m == 1 and j == last:
                        norm_G(G)
                return emit_S, emit_PV

            def norm_G(G, hb=hb, h=h):
                r1, o1, o2, o, rs, y1 = f1
                kb.op("dve", lambda: nc.vector.reciprocal(out=r1[:], in_=g.PS[PSUMS[0]][:, :]), reads=[g.PSr[PSUMS[0]]], writes=[f1_r[0]])
                kb.op("dve", lambda: nc.vector.tensor_tensor(out=o1[:], in0=g.PS[PO[0]][:, :], in1=r1[:], op=ALU.mult), reads=[g.PSr[PO[0]], f1_r[0]], writes=[f1_r[1]])
                kb.op("dve", lambda: nc.vector.reciprocal(out=r1[:], in_=g.PS[PSUMS[1]][:, :]), reads=[g.PSr[PSUMS[1]], f1_r[0]], writes=[f1_r[0]])
                kb.op("dve", lambda: nc.vector.tensor_tensor(out=o2[:], in0=g.PS[PO[1]][:, :], in1=r1[:], op=ALU.mult), reads=[g.PSr[PO[1]], f1_r[0]], writes=[f1_r[2]])
                kb.op("dve", lambda: nc.vector.scalar_tensor_tensor(out=o[:], in0=o2[:], scalar=neglam, in1=o1[:], op0=ALU.mult, op1=ALU.add),
                      reads=[f1_r[1], f1_r[2], sc_r], writes=[f1_r[3]])
                kb.op("act", lambda: nc.scalar.activation(out=sqb[:], in_=o[:], func=AF.Square), reads=[f1_r[3]], writes=[sqb_r])
                pi = prot.next()
                kb.op("pe", lambda pi=pi: nc.tensor.matmul(g.PS[pi][:, :], g.onesb[:], sqb[:], start=True, stop=True), reads=[sqb_r, g.cb_r], writes=[g.PSr[pi]])
                kb.op("act", lambda pi=pi: nc.scalar.activation(out=rs[:], in_=g.PS[pi][:, :], func=AF.Sqrt, scale=1.0 / 128, bias=g.kc[:, 2:3]),
                      reads=[g.PSr[pi], g.kc_r], writes=[f1_r[4]])
                kb.op("dve", lambda: nc.vector.reciprocal(out=rs[:], in_=rs[:]), reads=[f1_r[4]], writes=[f1_r[4]])
                kb.op("dve", lambda: nc.vector.scalar_tensor_tensor(out=y1[:], in0=o[:], scalar=swcol, in1=gT[hb][:, G * 512:(G + 1) * 512],
                                                                    op0=ALU.mult, op1=ALU.mult), reads=[f1_r[3], sc_r, gT_r[hb]], writes=[f1_r[5]])
                yi = (h * 4 + G) % 2
                kb.op("dve", lambda: nc.vector.tensor_tensor(out=yb[yi][:], in0=y1[:], in1=rs[:], op=ALU.mult), reads=[f1_r[5], f1_r[4]], writes=[yb_r[yi]])
                kb.dma("sp", g.ybr[0, h * 128:(h + 1) * 128, G * 512:(G + 1) * 512], yb[yi][:], reads=[yb_r[yi]], writes=[g.ybr_r])

            steps = [make_step(G, m, j) for G in range(4) for m in range(2) for j in range(0, 4 * G + 4)]
            SKEW = 2
            pend = []
            stride = 3
            for si_, (eS, ePV) in enumerate(steps):
                eS()
                pend.append(ePV)
                if len(pend) > SKEW:
                    pend.pop(0)()
                if nxt and si_ % stride == stride - 1:
                    nxt.pop(0)()
            while pend:
                pend.pop(0)()
            while nxt:
                nxt.pop(0)()


def mixer_B(g, l):
    nc, kb = g.nc, g.kb
    if not hasattr(g, "ybr_r"):
        g.ybr_r = Res()
    with ExitStack() as s:
        def sb(name, shape, dt):
            return s.enter_context(_sbt(nc, name, list(shape), dt))
        wp = WPool(g, s, "wB", 2, 16, 512)
        mgp = WPool(g, s, "wBm", 2, 16, 256)
        sk = sb("skB", [128, 16], F32)
        sk_r = Res()
        for par in range(2):
            src = bass.AP(g.swa_sinks.tensor, l * 16 + par, [[0, 64], [2, 8]])
            kb.dma("sp", sk[par * 64:(par + 1) * 64, 0:8], src, writes=[sk_r], allow_slow_non_contiguous=True)
        kb.op("act", lambda: nc.scalar.activation(out=sk[:, 8:16], in_=sk[:, 0:8], func=AF.Exp), reads=[sk_r], writes=[sk_r])
        kd = [sb(f"kd{i}", [128, T], BF16) for i in range(4)]
        kd_r = [Res() for _ in range(4)]
        Vb = sb("Vb", [128, 16, 256], BF16)
        Vb_r = Res()
        prot = Rot([0, 1])
        srot = Rot([2, 3, 4, 5])
        PO, PSM = 6, 7
        wbuf, wres = wp.load([(win(g, l, OFF_BK + (i // 2) * 64, 64), 64) for i in range(8)])
        ev = 0
        for kv in range(4):
            for tg in range(4):
                pi = prot.next()
                proj_fm(g, wbuf, wres, kv * 128, tg, pi)
                o = kd[kv][:, tg * 512:(tg + 1) * 512]
                ev += 1
                if ev % 2 == 0:
                    kb.op("act", lambda o=o, pi=pi: nc.scalar.copy(out=o, in_=g.PS[pi][:, :]), reads=[g.PSr[pi]], writes=[kd_r[kv]])
                else:
                    kb.op("dve", lambda o=o, pi=pi: nc.vector.tensor_copy(out=o, in_=g.PS[pi][:, :]), reads=[g.PSr[pi]], writes=[kd_r[kv]])
        wbuf, wres = wp.load([(win(g, l, OFF_BV, 256), 256)])
        for tt in range(16):
            pi = prot.next()

            def vmm(pi=pi, tt=tt, wbuf=wbuf):
                ins = None
                for kt in range(16):
                    ins = nc.tensor.matmul(g.PS[pi][:, 0:256], g.hT[:, kt, tt * 128:(tt + 1) * 128], wbuf[:, kt, 0:256], start=(kt == 0), stop=(kt == 15))
                return ins
            kb.op("pe", vmm, reads=[wres, g.hT_r[tt // 4]], writes=[g.PSr[pi]])
            kb.op("dve", lambda pi=pi, tt=tt: nc.vector.tensor_copy(out=Vb[:, tt, :], in_=g.PS[pi][:, 0:256]), reads=[g.PSr[pi]], writes=[Vb_r])
        NB = 2
        qT = [sb(f"qB{i}", [128, T], BF16) for i in range(NB)]
        gT = [sb(f"gB{i}", [128, T], BF16) for i in range(NB)]
        qT_r = [Res() for _ in range(NB)]
        gT_r = [Res() for _ in range(NB)]
        NE = 5
        Eb = [sb(f"EbB{i}", [128, 256], BF16) for i in range(NE)]
        Eb_r = [Res() for _ in range(NE)]
        erot = Rot(range(NE))
        f1 = [sb(f"fB{i}", [128, 512], F32) for i in range(2)]
        f1_r = [Res() for _ in range(2)]
        yb = [sb(f"ybB{i}", [128, 512], BF16) for i in range(2)]
        yb_r = [Res(), Res()]
        def inprojB(t):
            hb = t % NB
            stt = {}
            out = []

            def ld():
                stt["w"] = wp.load([(win(g, l, OFF_BQ + t * 128, 128), 128), (win(g, l, OFF_BG + t * 128, 128), 128)])
            out.append(ld)
            for ti in range(2):
                for tg in range(4):
                    def grp(ti=ti, tg=tg):
                        wbuf, wres = stt["w"]
                        pi = prot.next()
                        proj_fm(g, wbuf, wres, ti * 128, tg, pi)
                        if ti == 0:
                            o = qT[hb][:, tg * 512:(tg + 1) * 512]
                            kb.op("dve", lambda: nc.vector.tensor_copy(out=o, in_=g.PS[pi][:, :]), reads=[g.PSr[pi]], writes=[qT_r[hb]])
                        else:
                            o = gT[hb][:, tg * 512:(tg + 1) * 512]
                            kb.op("act", lambda: nc.scalar.activation(out=o, in_=g.PS[pi][:, :], func=AF.Silu), reads=[g.PSr[pi]], writes=[gT_r[hb]])
                    out.append(grp)
            return out

        mg_host(g, s, mgp, prot)
        for c_ in inprojB(0):
            c_()
        for t in range(8):
            hb = t % NB
            kv = t // 2
            nxt = inprojB(t + 1) if t + 1 < 8 else []

            def make_stepB(G, par, j, t=t, hb=hb, kv=kv):
                hq = 2 * t + par
                lo, hi = par * 64, (par + 1) * 64
                blocks = [i for i in (j, j + 1) if 4 * G <= i <= 4 * G + 3]
                cA = (blocks[0] - 4 * G) * 128
                ncol = 128 * len(blocks)
                st = {}

                def emit_S():
                    si = srot.next()
                    pS = g.PS[si]

                    def smm():
                        nc.tensor.matmul(pS[:, 0:ncol], kd[kv][lo:hi, j * 128:(j + 1) * 128],
                                         qT[hb][lo:hi, G * 512 + cA:G * 512 + cA + ncol], start=True, stop=False)
                        ins = None
                        for bi, i in enumerate(blocks):
                            kind = 0 if i == j else 1
                            ins = nc.tensor.matmul(pS[:, bi * 128:(bi + 1) * 128], g.idb[:], g.bt[:, kind, 8 + hq, :], start=False, stop=(bi == len(blocks) - 1))
                        return ins
                    kb.op("pe", smm, reads=[kd_r[kv], qT_r[hb], g.bt_r, g.cb_r], writes=[g.PSr[si]])
                    ei = erot.next()
                    st["ei"] = ei
                    kb.op("act", lambda: nc.scalar.activation(out=Eb[ei][:, 0:ncol], in_=pS[:, 0:ncol], func=AF.Exp, scale=0.125),
                          reads=[g.PSr[si]], writes=[Eb_r[ei]])

                def emit_PV():
                    ei = st["ei"]

                    def pv():
                        ins = None
                        for bi, i in enumerate(blocks):
                            cc = (i - 4 * G) * 128
                            first = (j == i - 1) or (i == 0)
                            lastk = (j == i)
                            nc.tensor.matmul(g.PS[PO][lo:hi, cc:cc + 128], Vb[:, j, kv * 64:(kv + 1) * 64], Eb[ei][:, bi * 128:(bi + 1) * 128], start=first, stop=lastk)
                            ins = nc.tensor.matmul(g.PS[PSM][lo:hi, cc:cc + 128], g.onesb[:, 0:64], Eb[ei][:, bi * 128:(bi + 1) * 128], start=first, stop=lastk)
                        return ins
                    kb.op("pe", pv, reads=[Vb_r, Eb_r[ei], g.cb_r], writes=[g.PSr[PO], g.PSr[PSM]])
                    if par == 1 and j == 4 * G + 3:
                        fin_G(G)
                return emit_S, emit_PV

            def fin_G(G, t=t, hb=hb):
                den, y1 = f1
                kb.op("dve", lambda: nc.vector.tensor_scalar_add(out=den[:], in0=g.PS[PSM][:, :], scalar1=sk[:, 8 + t:9 + t]), reads=[g.PSr[PSM], sk_r], writes=[f1_r[0]])
                kb.op("dve", lambda: nc.vector.reciprocal(out=den[:], in_=den[:]), reads=[f1_r[0]], writes=[f1_r[0]])
                kb.op("dve", lambda: nc.vector.tensor_tensor(out=y1[:], in0=g.PS[PO][:, :], in1=den[:], op=ALU.mult), reads=[g.PSr[PO], f1_r[0]], writes=[f1_r[1]])
                yi = (t * 4 + G) % 2
                kb.op("dve", lambda: nc.vector.tensor_tensor(out=yb[yi][:], in0=y1[:], in1=gT[hb][:, G * 512:(G + 1) * 512], op=ALU.mult),
                      reads=[f1_r[1], gT_r[hb]], writes=[yb_r[yi]])
                kb.dma("sp", g.ybr[1, t * 128:(t + 1) * 128, G * 512:(G + 1) * 512], yb[yi][:], reads=[yb_r[yi]], writes=[g.ybr_r])

            steps = [make_stepB(G, par, j) for G in range(4) for par in range(2) for j in range(max(4 * G - 1, 0), 4 * G + 4)]
            SKEW = 2
            pend = []
            stride = 4
            for si_, (eS, ePV) in enumerate(steps):
                eS()
                pend.append(ePV)
                if len(pend) > SKEW:
                    pend.pop(0)()
                if nxt and si_ % stride == stride - 1:
                    nxt.pop(0)()
            while pend:
                pend.pop(0)()
            while nxt:
                nxt.pop(0)()


def load_rows_T(g, s, name, rows_aps, nt):
    nc, kb = g.nc, g.kb
    R = len(rows_aps)
    rows = s.enter_context(_sbt(nc, name + "_rows", [R, nt * 128], F32))
    outT = s.enter_context(_sbt(nc, name + "_T", [128, nt, R], F32))
    rows_r, out_r = Res(), Res()
    for r, ap in enumerate(rows_aps):
        kb.dma("sp", rows[r:r + 1, :], ap, writes=[rows_r])
    pi = 0
    ps = g.PS[pi]

    def f():
        ins = None
        for t in range(nt):
            ins = nc.tensor.transpose(ps[:, t * R:(t + 1) * R], rows[0:R, t * 128:(t + 1) * 128], g.cf[0:R, C_ID:C_ID + R])
        return ins
    kb.op("pe", f, reads=[rows_r, g.cf_r], writes=[g.PSr[pi]])
    kb.op("dve", lambda: nc.vector.tensor_copy(out=outT[:].rearrange("p t r -> p (t r)"), in_=ps[:, 0:nt * R]), reads=[g.PSr[pi]], writes=[out_r])
    return outT, out_r


def mixer_D(g, l):
    nc, kb = g.nc, g.kb
    if not hasattr(g, "ybr_r"):
        g.ybr_r = Res()
    with ExitStack() as s:
        def sb(name, shape, dt):
            return s.enter_context(_sbt(nc, name, list(shape), dt))
        rows = [g.conf_conv_w[l, j:j + 1, :] for j in range(31)] + [g.conf_conv_b[l:l + 1, :], g.conf_ln_w[l:l + 1, :], g.conf_ln_b[l:l + 1, :]]
        dp = sb("dp", [128, 8, 34], F32)
        dp_r = Res()
        with ExitStack() as s2:
            dpT, dp_r0 = load_rows_T(g, s2, "dpar", rows, 8)
            kb.op("dve", lambda: nc.vector.tensor_copy(out=dp[:], in_=dpT[:]), reads=[dp_r0], writes=[dp_r])
            kb.barrier()
        wp = WPool(g, s, "wD", 2, 16, 256)
        conv = sb("convD", [128, 8, T], F32)
        conv_r = [Res() for _ in range(4)]
        s3 = ExitStack()

        def sb3(name, shape, dt):
            return s3.enter_context(_sbt(nc, name, list(shape), dt))
        hpad = [sb3(f"hpad{i}", [128, 30 + T], BF16) for i in range(2)]
        hpad_r = [Res(), Res()]
        for i in range(2):
            kb.op("dve", lambda i=i: nc.vector.memset(hpad[i][:, 0:30], 0.0), writes=[hpad_r[i]])
        dgl = [sb3(f"dgl{i}", [128, 31, 128], BF16) for i in range(2)]
        dgl_r = [Res(), Res()]
        sg = [sb3(f"sgD{i}", [128, 512], F32) for i in range(2)]
        sg_r = [Res(), Res()]
        prot = Rot([0, 1, 2, 3])
        crot = Rot([4, 5, 6, 7])
        for t in range(8):
            b = t % 2
            wbuf, wres = wp.load([(win(g, l, OFF_DGLU + t * 128, 128), 128), (win(g, l, OFF_DGLU + 1024 + t * 128, 128), 128)])
            for j in range(31):
                kb.op("dve", lambda j=j, t=t, b=b: nc.vector.tensor_scalar_mul(out=dgl[b][:, j, :], in0=g.cf[:, C_ID:C_ID + 128], scalar1=dp[:, t, j:j + 1]),
                      reads=[g.cf_r, dp_r], writes=[dgl_r[b]])
            for tg in range(4):
                pv_, pg_ = prot.next(), prot.next()
                proj_fm(g, wbuf, wres, 0, tg, pv_)
                proj_fm(g, wbuf, wres, 128, tg, pg_)
                si = (t * 4 + tg) % 2
                kb.op("act", lambda si=si, pg_=pg_: nc.scalar.activation(out=sg[si][:], in_=g.PS[pg_][:, :], func=AF.Sigmoid), reads=[g.PSr[pg_]], writes=[sg_r[si]])
                kb.op("dve", lambda si=si, pv_=pv_, b=b, tg=tg: nc.vector.tensor_tensor(
                    out=hpad[b][:, 30 + tg * 512:30 + (tg + 1) * 512], in0=g.PS[pv_][:, :], in1=sg[si][:], op=ALU.mult),
                    reads=[g.PSr[pv_], sg_r[si]], writes=[hpad_r[b]])
            for tg in range(4):
                ci = crot.next()

                def cmm(ci=ci, b=b, tg=tg):
                    ins = None
                    for j in range(31):
                        ins = nc.tensor.matmul(g.PS[ci][:, :], dgl[b][:, j, :], hpad[b][:, tg * 512 + j:tg * 512 + j + 512], start=(j == 0), stop=(j == 30))
                    return ins
                kb.op("pe", cmm, reads=[dgl_r[b], hpad_r[b]], writes=[g.PSr[ci]])
                kb.op("act", lambda ci=ci, t=t, tg=tg: nc.scalar.activation(out=conv[:, t, tg * 512:(tg + 1) * 512], in_=g.PS[ci][:, :], func=AF.Identity,
                                                                           bias=dp[:, t, 31:32]), reads=[g.PSr[ci], dp_r], writes=[conv_r[tg]])
        kb.barrier()
        s3.close()
        sq = [sb(f"sqD{i}", [128, 512], F32) for i in range(2)]
        sq_r = [Res(), Res()]
        mu = sb("muD", [128, 512], F32)
        rs = sb("rsD", [128, 512], F32)
        tmp = [sb(f"tmD{i}", [128, 512], F32) for i in range(2)]
        tmp_r = [Res(), Res()]
        gD = [sb(f"gD{i}", [128, 512], F32) for i in range(2)]
        gD_r = [Res(), Res()]
        mu_r, rs_r = Res(), Res()
        yb = [sb(f"ybD{i}", [128, 512], BF16) for i in range(2)]
        yb_r = [Res(), Res()]
        onesf = g.cf[:, C_ONES:C_ONES + 128]
        wg = [None] * 8
        for tg in range(4):
            pA, pB = 0, 1

            def amm(tg=tg):
                ins = None
                for t in range(8):
                    ins = nc.tensor.matmul(g.PS[pA][:, :], onesf, conv[:, t, tg * 512:(tg + 1) * 512], start=(t == 0), stop=(t == 7))
                return ins
            kb.op("pe", amm, reads=[conv_r[tg], g.cf_r], writes=[g.PSr[pA]])
            for t in range(8):
                si = t % 2
                kb.op("act", lambda si=si, t=t, tg=tg: nc.scalar.activation(out=sq[si][:], in_=conv[:, t, tg * 512:(tg + 1) * 512], func=AF.Square),
                      reads=[conv_r[tg]], writes=[sq_r[si]])
                kb.op("pe", lambda si=si, t=t: nc.tensor.matmul(g.PS[pB][:, :], onesf, sq[si][:], start=(t == 0), stop=(t == 7)),
                      reads=[sq_r[si], g.cf_r], writes=[g.PSr[pB]])
            kb.op("act", lambda: nc.scalar.mul(out=mu[:], in_=g.PS[pA][:, :], mul=1.0 / 1024), reads=[g.PSr[pA]], writes=[mu_r])
            kb.op("dve", lambda: nc.vector.tensor_tensor(out=rs[:], in0=mu[:], in1=mu[:], op=ALU.mult), reads=[mu_r], writes=[rs_r])
            kb.op("dve", lambda: nc.vector.scalar_tensor_tensor(out=rs[:], in0=g.PS[pB][:, :], scalar=g.kc[:, 3:4], in1=rs[:], op0=ALU.mult, op1=ALU.subtract),
                  reads=[g.PSr[pB], g.kc_r, rs_r], writes=[rs_r])
            kb.op("act", lambda: nc.scalar.activation(out=rs[:], in_=rs[:], func=AF.Sqrt, bias=g.kc[:, 2:3]), reads=[rs_r, g.kc_r], writes=[rs_r])
            kb.op("dve", lambda: nc.vector.reciprocal(out=rs[:], in_=rs[:]), reads=[rs_r], writes=[rs_r])
            for t in range(8):
                if t % 2 == 0:
                    wbuf, wres = wp.load([(win(g, l, OFF_DG + t * 128, 256), 256)])
                pg_ = 2 + (t % 4)
                proj_fm(g, wbuf, wres, (t % 2) * 128, tg, pg_)
                si = t % 2
                kb.op("act", lambda si=si, pg_=pg_: nc.scalar.activation(out=gD[si][:], in_=g.PS[pg_][:, :], func=AF.Silu), reads=[g.PSr[pg_]], writes=[gD_r[si]])
                kb.op("dve", lambda si=si, t=t, tg=tg: nc.vector.tensor_tensor(out=tmp[si][:], in0=conv[:, t, tg * 512:(tg + 1) * 512], in1=mu[:], op=ALU.subtract),
                      reads=[conv_r[tg], mu_r], writes=[tmp_r[si]])
                kb.op("dve", lambda si=si: nc.vector.tensor_tensor(out=tmp[si][:], in0=tmp[si][:], in1=rs[:], op=ALU.mult), reads=[tmp_r[si], rs_r], writes=[tmp_r[si]])
                kb.op("act", lambda si=si, t=t: nc.scalar.activation(out=tmp[si][:], in_=tmp[si][:], func=AF.Silu, scale=dp[:, t, 32:33], bias=dp[:, t, 33:34]),
                      reads=[tmp_r[si], dp_r], writes=[tmp_r[si]])
                kb.op("dve", lambda si=si: nc.vector.tensor_tensor(out=yb[si][:], in0=tmp[si][:], in1=gD[si][:], op=ALU.mult), reads=[tmp_r[si], gD_r[si]], writes=[yb_r[si]])
                kb.dma("sp", g.ybr[3, t * 128:(t + 1) * 128, tg * 512:(tg + 1) * 512], yb[si][:], reads=[yb_r[si]], writes=[g.ybr_r])
        if g.mgjobs:
            mg_host(g, s, wp, Rot([6, 7]))
            mg_run(g, 10 ** 6)


def mixer_C(g, l):
    nc, kb = g.nc, g.kb
    if not hasattr(g, "ybr_r"):
        g.ybr_r = Res()
    with ExitStack() as s:
        def sb(name, shape, dt):
            return s.enter_context(_sbt(nc, name, list(shape), dt))
        cp = sb("cp", [128, 12, 5], F32)
        cp_r = Res()
        with ExitStack() as s2:
            rows = [g.ssd_conv_w[l, j:j + 1, :] for j in range(4)] + [g.ssd_conv_b[l:l + 1, :]]
            cpT, cp_r0 = load_rows_T(g, s2, "cpar", rows, 12)
            kb.op("dve", lambda: nc.vector.tensor_copy(out=cp[:], in_=cpT[:]), reads=[cp_r0], writes=[cp_r])
            kb.barrier()
        pc = sb("pcC", [128, 64], F32)
        pc_r = Res()
        kb.dma("sp", pc[:, 0:16], g.ssd_dt_bias[l:l + 1, :].partition_broadcast(128), writes=[pc_r])
        kb.dma("sp", pc[:, 16:32], g.ssd_a_log[l:l + 1, :].partition_broadcast(128), writes=[pc_r])
        for par in range(2):
            src = bass.AP(g.ssd_d.tensor, l * 16 + par, [[0, 64], [2, 8]])
            kb.dma("sp", pc[par * 64:(par + 1) * 64, 32:40], src, writes=[pc_r], allow_slow_non_contiguous=True)
        kb.dma("sp", pc[:, 40:48], g.ssd_norm_w[l:l + 1, :].rearrange("o (t p) -> p (o t)", p=128), writes=[pc_r], allow_slow_non_contiguous=True)
        kb.op("act", lambda: nc.scalar.activation(out=pc[:, 16:32], in_=pc[:, 16:32], func=AF.Exp), reads=[pc_r], writes=[pc_r])
        kb.op("dve", lambda: nc.vector.tensor_scalar_mul(out=pc[:, 16:32], in0=pc[:, 16:32], scalar1=-1.0), reads=[pc_r], writes=[pc_r])
        blk = sb("blkC", [16, 2048], F32)
        blk_r = Res()
        kb.dma("sp", blk[:], g.cblk, writes=[blk_r])
        wp = WPool(g, s, "wC", 2, 16, 256)
        prot = Rot([0, 1])
        xbc = sb("xbc", [128, 12, T], BF16)
        xbc_r = [Res() for _ in range(12)]
        dtm = sb("dtm", [128, 256], F32)
        acol = sb("acol", [128, 256], F32)
        dst_ = sb("dstm", [128, 256], F32)
        cdb = sb("cdb", [128, 256], F32)
        dt_r, acol_r, dst_r, cdb_r = Res(), Res(), Res(), Res()
        with ExitStack() as s3:
            def sb3(name, shape, dt):
                return s3.enter_context(_sbt(nc, name, list(shape), dt))
            rawp = [sb3(f"rawp{i}", [128, 3 + T], BF16) for i in range(2)]
            rawp_r = [Res(), Res()]
            for i in range(2):
                kb.op("dve", lambda i=i: nc.vector.memset(rawp[i][:, 0:3], 0.0), writes=[rawp_r[i]])
            dg4 = [sb3(f"dg4{i}", [128, 4, 128], BF16) for i in range(2)]
            dg4_r = [Res(), Res()]
            dAm = sb3("dAm", [128, 256], F32)
            dA_r = Res()
            for ti in range(12):
                b = ti % 2
                if ti % 2 == 0:
                    wbuf, wres = wp.load([(win(g, l, OFF_CX + ti * 128, 256), 256)])
                for j in range(4):
                    kb.op("dve", lambda j=j, ti=ti, b=b: nc.vector.tensor_scalar_mul(out=dg4[b][:, j, :], in0=g.cf[:, C_ID:C_ID + 128], scalar1=cp[:, ti, j:j + 1]),
                          reads=[g.cf_r, cp_r], writes=[dg4_r[b]])
                for tg in range(4):
                    pi = prot.next()
                    proj_fm(g, wbuf, wres, (ti % 2) * 128, tg, pi)
                    kb.op("act", lambda pi=pi, b=b, tg=tg: nc.scalar.copy(out=rawp[b][:, 3 + tg * 512:3 + (tg + 1) * 512], in_=g.PS[pi][:, :]),
                          reads=[g.PSr[pi]], writes=[rawp_r[b]])
                for tg in range(4):
                    ci = 2 + (ti * 4 + tg) % 2

                    def cmm(ci=ci, b=b, tg=tg):
                        ins = None
                        for j in range(4):
                            ins = nc.tensor.matmul(g.PS[ci][:, :], dg4[b][:, j, :], rawp[b][:, tg * 512 + j:tg * 512 + j + 512], start=(j == 0), stop=(j == 3))
                        return ins
                    kb.op("pe", cmm, reads=[dg4_r[b], rawp_r[b]], writes=[g.PSr[ci]])
                    kb.op("act", lambda ci=ci, ti=ti, tg=tg: nc.scalar.activation(out=xbc[:, ti, tg * 512:(tg + 1) * 512], in_=g.PS[ci][:, :], func=AF.Silu,
                                                                                bias=cp[:, ti, 4:5]), reads=[g.PSr[ci], cp_r], writes=[xbc_r[ti]])
            wbuf, wres = wp.load([(win(g, l, OFF_CDT, 16), 16)])
            pdt = 4

            def dtmm():
                ins = None
                for tt in range(16):
                    for kt in range(16):
                        ins = nc.tensor.matmul(g.PS[pdt][:, tt * 16:(tt + 1) * 16], g.hT[:, kt, tt * 128:(tt + 1) * 128], wbuf[:, kt, 0:16], start=(kt == 0), stop=(kt == 15))
                return ins
            kb.op("pe", dtmm, reads=[wres] + g.hT_r, writes=[g.PSr[pdt]])
            kb.op("dve", lambda: nc.vector.tensor_tensor(out=dtm[:].rearrange("p (c e) -> p c e", e=16), in0=g.PS[pdt][:, 0:256].rearrange("p (c e) -> p c e", e=16),
                                                         in1=pc[:, 0:16].unsqueeze(1).to_broadcast([128, 16, 16]), op=ALU.add), reads=[g.PSr[pdt], pc_r], writes=[dt_r])
            kb.op("act", lambda: nc.scalar.activation(out=dtm[:], in_=dtm[:], func=AF.Exp), reads=[dt_r], writes=[dt_r])
            kb.op("act", lambda: nc.scalar.activation(out=dtm[:], in_=dtm[:], func=AF.Ln, bias=1.0), reads=[dt_r], writes=[dt_r])
            kb.op("dve", lambda: nc.vector.tensor_tensor(out=dAm[:].rearrange("p (c e) -> p c e", e=16), in0=dtm[:].rearrange("p (c e) -> p c e", e=16),
                                                         in1=pc[:, 16:32].unsqueeze(1).to_broadcast([128, 16, 16]), op=ALU.mult), reads=[dt_r, pc_r], writes=[dA_r])
            for (cst, dstt, dres, doexp) in ((C_TRI, acol, acol_r, False), (C_TRIU, dst_, dst_r, True), (C_ONES, cdb, cdb_r, True)):
                pi = prot.next()
                kb.op("pe", lambda pi=pi, cst=cst: nc.tensor.matmul(g.PS[pi][:, 0:256], g.cf[:, cst:cst + 128], dAm[:], start=True, stop=True),
                      reads=[dA_r, g.cf_r], writes=[g.PSr[pi]])
                if doexp:
                    kb.op("act", lambda pi=pi, dstt=dstt: nc.scalar.activation(out=dstt[:], in_=g.PS[pi][:, 0:256], func=AF.Exp), reads=[g.PSr[pi]], writes=[dres])
                else:
                    kb.op("dve", lambda pi=pi, dstt=dstt: nc.vector.tensor_copy(out=dstt[:], in_=g.PS[pi][:, 0:256]), reads=[g.PSr[pi]], writes=[dres])
            kb.barrier()
        hst = sb("hst", [128, 1024], F32)
        prevb = sb("prevb", [128, 1024], BF16)
        hst_r, prevb_r = Res(), Res()
        acTc = [sb(f"acTc{i}", [16, 128], F32) for i in range(2)]
        acTc_r = [Res(), Res()]
        tsegs = [sb(f"tseg{i}", [128, 512], F32) for i in range(2)]
        tsegs_r = [Res(), Res()]
        Ed = [sb(f"Ed{i}", [128, 512], BF16) for i in range(2)]
        Ed_r = [Res(), Res()]
        Ea = [sb(f"Ea{i}", [128, 512], BF16) for i in range(2)]
        Ea_r = [Res(), Res()]
        MT = [sb(f"MT{i}", [128, 4, 128], BF16) for i in range(2)]
        MT_r = [Res(), Res()]
        CsT = [sb(f"CsT{i}", [128, 4, 128], BF16) for i in range(2)]
        CsT_r = [Res(), Res()]
        cbm = [sb(f"cbm{i}", [128, 2, 128], BF16) for i in range(2)]
        cbm_r = [Res(), Res()]
        xdt = sb("xdt", [128, 1024], BF16)
        xdt_r = Res()
        xdd = sb("xdd", [128, 1024], BF16)
        xdd_r = Res()
        Btm = sb("Btm", [128, 256], BF16)
        Btm_r = Res()
        siz = sb("siz", [128, 8, 512], BF16)
        siz_r = Res()
        yg = sb("ygC", [128, 8, 128], F32)
        yg_r = Res()
        sqc = sb("sqC", [128, 8, 128], BF16)
        sqc_r = Res()
        rsc = sb("rsC", [128, 2, 128], F32)
        rsc_r = Res()
        ycb = [sb(f"ycb{i}", [128, 8, 128], BF16) for i in range(2)]
        ycb_r = [Res(), Res()]
        tmpc = sb("tmpC", [128, 1024], F32)
        tmpc_r = Res()
        mg_host(g, s, wp, prot)
        for c in range(16):
            cb2 = c % 2
            cs = slice(c * 128, (c + 1) * 128)
            if c % 4 == 0:
                tg = c // 4
                for t in range(8):
                    if t % 2 == 0:
                        wbuf, wres = wp.load([(win(g, l, OFF_CZ + t * 128, 256), 256)])
                    pi = prot.next()
                    proj_fm(g, wbuf, wres, (t % 2) * 128, tg, pi)
                    kb.op("act", lambda pi=pi, t=t: nc.scalar.activation(out=siz[:, t, :], in_=g.PS[pi][:, :], func=AF.Silu), reads=[g.PSr[pi]], writes=[siz_r])
            pT = prot.next()
            kb.op("pe", lambda pT=pT, c=c: nc.tensor.transpose(g.PS[pT][0:16, 0:128], acol[:, c * 16:(c + 1) * 16], g.cf[:, C_ID:C_ID + 128]),
                  reads=[acol_r, g.cf_r], writes=[g.PSr[pT]])
            kb.op("dve", lambda pT=pT, cb2=cb2: nc.vector.tensor_copy(out=acTc[cb2][:], in_=g.PS[pT][0:16, 0:128]), reads=[g.PSr[pT]], writes=[acTc_r[cb2]])
            pX = 2
            pXb = g.PS[pX][:].bitcast(BF16)

            def trx(cs=cs, pXb=pXb):
                ins = None
                for t in range(8):
                    ins = nc.tensor.transpose(pXb[:, t * 128:(t + 1) * 128], xbc[:, t, cs], g.idb[:])
                return ins
            kb.op("pe", trx, reads=xbc_r[0:8] + [g.cb_r], writes=[g.PSr[pX]])
            kb.op("dve", lambda c=c, pXb=pXb: nc.vector.tensor_tensor(
                out=xdt[:].rearrange("p (e d) -> p e d", d=64), in0=pXb[:, 0:1024].rearrange("p (e d) -> p e d", d=64),
                in1=dtm[:, c * 16:(c + 1) * 16].unsqueeze(2).to_broadcast([128, 16, 64]), op=ALU.mult), reads=[g.PSr[pX], dt_r], writes=[xdt_r])
            kb.op("dve", lambda c=c: nc.vector.tensor_tensor(
                out=xdd[:].rearrange("p (e d) -> p e d", d=64), in0=xdt[:].rearrange("p (e d) -> p e d", d=64),
                in1=dst_[:, c * 16:(c + 1) * 16].unsqueeze(2).to_broadcast([128, 16, 64]), op=ALU.mult), reads=[xdt_r, dst_r], writes=[xdd_r])
            pB = 3
            pBb = g.PS[pB][:].bitcast(BF16)

            def trb(cs=cs, pBb=pBb):
                nc.tensor.transpose(pBb[:, 0:128], xbc[:, 8, cs], g.idb[:])
                return nc.tensor.transpose(pBb[:, 128:256], xbc[:, 9, cs], g.idb[:])
            kb.op("pe", trb, reads=[xbc_r[8], xbc_r[9], g.cb_r], writes=[g.PSr[pB]])
            kb.op("act", lambda pBb=pBb: nc.scalar.copy(out=Btm[:], in_=pBb[:, 0:256]), reads=[g.PSr[pB]], writes=[Btm_r])
            pCB = 3

            def cbmm(cs=cs):
                nc.tensor.matmul(g.PS[pCB][:, 256:384], xbc[:, 8, cs], xbc[:, 10, cs], start=True, stop=True)
                return nc.tensor.matmul(g.PS[pCB][:, 384:512], xbc[:, 9, cs], xbc[:, 11, cs], start=True, stop=True)
            kb.op("pe", cbmm, reads=xbc_r[8:12], writes=[g.PSr[pCB]])
            kb.op("dve", lambda cb2=cb2: nc.vector.tensor_tensor(out=cbm[cb2][:], in0=g.PS[pCB][:, 256:512].rearrange("p (g l) -> p g l", g=2),
                                                                in1=g.trib[:].unsqueeze(1).to_broadcast([128, 2, 128]), op=ALU.mult),
                  reads=[g.PSr[pCB], g.cb_r], writes=[cbm_r[cb2]])
            pY = [6, 7]
            def emit_bcm(hq, cb2=cb2):
                pa = 4 + hq % 2

                def bcm():
                    ins = None
                    for e4 in range(4):
                        e = hq * 4 + e4
                        ins = nc.tensor.matmul(g.PS[pa][:, e4 * 128:(e4 + 1) * 128], blk[0:16, e * 128:(e + 1) * 128], acTc[cb2][:, :], start=True, stop=True)
                    return ins
                kb.op("pe", bcm, reads=[acTc_r[cb2], blk_r], writes=[g.PSr[pa]])
            emit_bcm(0)
            for hq in range(4):
                pa = 4 + hq % 2
                if hq < 3:
                    emit_bcm(hq + 1)
                b2 = hq % 2
                tseg, tseg_r = tsegs[b2], tsegs_r[b2]
                for e4 in range(4):
                    e = hq * 4 + e4
                    kb.op("dve", lambda pa=pa, e4=e4, e=e, c=c, tseg=tseg: nc.vector.tensor_scalar(
                        out=tseg[:, e4 * 128:(e4 + 1) * 128], in0=g.PS[pa][:, e4 * 128:(e4 + 1) * 128], scalar1=acol[:, c * 16 + e:c * 16 + e + 1],
                        scalar2=g.kc[:, 0:1], op0=ALU.subtract, op1=ALU.min), reads=[g.PSr[pa], acol_r, g.kc_r], writes=[tseg_r])
                kb.op("act", lambda b2=b2, tseg=tseg: nc.scalar.activation(out=Ed[b2][:], in_=tseg[:], func=AF.Exp), reads=[tseg_r], writes=[Ed_r[b2]])
                kb.op("act", lambda b2=b2, pa=pa: nc.scalar.activation(out=Ea[b2][:], in_=g.PS[pa][:, :], func=AF.Exp), reads=[g.PSr[pa]], writes=[Ea_r[b2]])
                grp = hq // 2
                kb.op("dve", lambda b2=b2, cb2=cb2, grp=grp: nc.vector.tensor_tensor(out=MT[b2][:], in0=Ed[b2][:].rearrange("p (e l) -> p e l", e=4),
                                                                                    in1=cbm[cb2][:, grp, :].unsqueeze(1).to_broadcast([128, 4, 128]), op=ALU.mult),
                      reads=[Ed_r[b2], cbm_r[cb2]], writes=[MT_r[b2]])
                kb.op("dve", lambda b2=b2, grp=grp, cs=cs: nc.vector.tensor_tensor(out=CsT[b2][:], in0=Ea[b2][:].rearrange("p (e l) -> p e l", e=4),
                                                                                  in1=xbc[:, 10 + grp, cs].unsqueeze(1).to_broadcast([128, 4, 128]), op=ALU.mult),
                      reads=[Ea_r[b2], xbc_r[10 + grp]], writes=[CsT_r[b2]])
                py = pY[hq // 2]
                mg_run(g, 4)

                def ymm(py=py, b2=b2, hq=hq, c=c):
                    ins = None
                    for e4 in range(4):
                        e = hq * 4 + e4
                        lo = (e % 2) * 64
                        cc = ((e // 2) % 4) * 128
                        ins = nc.tensor.matmul(g.PS[py][lo:lo + 64, cc:cc + 128], xdt[:, e * 64:(e + 1) * 64], MT[b2][:, e4, :], start=True, stop=(c == 0))
                        if c > 0:
                            ins = nc.tensor.matmul(g.PS[py][lo:lo + 64, cc:cc + 128], prevb[:, e * 64:(e + 1) * 64], CsT[b2][:, e4, :], start=False, stop=True)
                    return ins
                kb.op("pe", ymm, reads=[xdt_r, MT_r[b2], prevb_r, CsT_r[b2]], writes=[g.PSr[py]])
            kb.op("dve", lambda cs=cs: nc.vector.tensor_tensor(out=tmpc[:].rearrange("p (t l) -> p t l", t=8), in0=xbc[:, 0:8, cs],
                                                              in1=pc[:, 32:40].unsqueeze(2).to_broadcast([128, 8, 128]), op=ALU.mult),
                  reads=xbc_r[0:8] + [pc_r], writes=[tmpc_r])
            for gi in range(2):
                kb.op("dve", lambda gi=gi: nc.vector.tensor_tensor(out=yg[:, gi * 4:(gi + 1) * 4, :].rearrange("p t l -> p (t l)"), in0=g.PS[pY[gi]][:, :],
                                                                  in1=tmpc[:, gi * 512:(gi + 1) * 512], op=ALU.add), reads=[g.PSr[pY[gi]], tmpc_r], writes=[yg_r])
            zc = slice((c % 4) * 128, (c % 4 + 1) * 128)
            kb.op("dve", lambda zc=zc: nc.vector.tensor_tensor(out=yg[:], in0=yg[:], in1=siz[:, :, zc], op=ALU.mult), reads=[yg_r, siz_r], writes=[yg_r])
            kb.op("act", lambda: nc.scalar.activation(out=sqc[:], in_=yg[:], func=AF.Square), reads=[yg_r], writes=[sqc_r])
            pR = 2

            def rmm():
                ins = None
                for gi in range(2):
                    for t4 in range(4):
                        ins = nc.tensor.matmul(g.PS[pR][:, gi * 128:(gi + 1) * 128], g.onesb[:], sqc[:, gi * 4 + t4, :], start=(t4 == 0), stop=(t4 == 3))
                return ins
            kb.op("pe", rmm, reads=[sqc_r, g.cb_r], writes=[g.PSr[pR]])
            kb.op("act", lambda: nc.scalar.activation(out=rsc[:].rearrange("p g l -> p (g l)"), in_=g.PS[pR][:, 0:256], func=AF.Sqrt, scale=1.0 / 512, bias=g.kc[:, 2:3]),
                  reads=[g.PSr[pR], g.kc_r], writes=[rsc_r])
            kb.op("dve", lambda: nc.vector.reciprocal(out=rsc[:], in_=rsc[:]), reads=[rsc_r], writes=[rsc_r])
            for gi in range(2):
                kb.op("dve", lambda gi=gi: nc.vector.tensor_tensor(out=yg[:, gi * 4:(gi + 1) * 4, :], in0=yg[:, gi * 4:(gi + 1) * 4, :],
                                                                  in1=rsc[:, gi, :].unsqueeze(1).to_broadcast([128, 4, 128]), op=ALU.mult), reads=[yg_r, rsc_r], writes=[yg_r])
            kb.op("dve", lambda cb2=cb2: nc.vector.tensor_tensor(out=ycb[cb2][:], in0=yg[:], in1=pc[:, 40:48].unsqueeze(2).to_broadcast([128, 8, 128]), op=ALU.mult),
                  reads=[yg_r, pc_r], writes=[ycb_r[cb2]])
            kb.dma("sp", g.ybr[2].rearrange("(t p) l -> p t l", p=128)[:, :, cs], ycb[cb2][:], reads=[ycb_r[cb2]], writes=[g.ybr_r])
            if c < 15:
                pS2 = [4, 5]
                for gi in range(2):
                    kb.op("pe", lambda gi=gi: nc.tensor.matmul(g.PS[pS2[gi]][:, :], Btm[:, gi * 128:(gi + 1) * 128], xdd[:, gi * 512:(gi + 1) * 512], start=True, stop=True),
                          reads=[Btm_r, xdd_r], writes=[g.PSr[pS2[gi]]])
                if c == 0:
                    for gi in range(2):
                        kb.op("dve", lambda gi=gi: nc.vector.tensor_copy(out=hst[:, gi * 512:(gi + 1) * 512], in_=g.PS[pS2[gi]][:, :]), reads=[g.PSr[pS2[gi]]], writes=[hst_r])
                else:
                    kb.op("dve", lambda c=c: nc.vector.tensor_tensor(out=hst[:].rearrange("p (e d) -> p e d", d=64), in0=hst[:].rearrange("p (e d) -> p e d", d=64),
                                                                    in1=cdb[:, c * 16:(c + 1) * 16].unsqueeze(2).to_broadcast([128, 16, 64]), op=ALU.mult),
                          reads=[hst_r, cdb_r], writes=[hst_r])
                    for gi in range(2):
                        kb.op("dve", lambda gi=gi: nc.vector.tensor_tensor(out=hst[:, gi * 512:(gi + 1) * 512], in0=hst[:, gi * 512:(gi + 1) * 512], in1=g.PS[pS2[gi]][:, :], op=ALU.add),
                              reads=[hst_r, g.PSr[pS2[gi]]], writes=[hst_r])
                kb.op("act", lambda: nc.scalar.copy(out=prevb[:], in_=hst[:]), reads=[hst_r], writes=[prevb_r])


def phase3(g, l, src, dst):
    nc, kb = g.nc, g.kb
    g.xres_r_new = Res()
    with ExitStack() as s:
        def sb(name, shape, dt):
            return s.enter_context(_sbt(nc, name, list(shape), dt))
        yh = sb("yh", [128, 32, 1024], BF16)
        mT = sb("mT", [128, 16, 1024], BF16)
        yh_r = Res()
        mT_r = [Res(), Res()]
        wbr = WPool(g, s, "wbr", 3, 8, 128)
        wo = WPool(g, s, "wo", 2, 16, 256)
        acc = [sb(f"acc{i}", [128, 512], F32) for i in range(2)]
        acc_r = [Res(), Res()]
        NG = 6
        gt = [sb(f"gt3{i}", [128, 512], BF16) for i in range(NG)]
        gt_r = [Res() for _ in range(NG)]
        pr = [sb(f"pr3{i}", [128, 512], F32) for i in range(2)]
        pr_r = [Res(), Res()]
        xt = [sb(f"xt3{i}", [128, 256], F32) for i in range(4)]
        xt_r = [Res() for _ in range(4)]
        brot = Rot([0, 1, 2, 3, 4, 5])
        orot = Rot([6, 7])
        k = 0
        for hf in range(2):
            ts = slice(hf * 1024, (hf + 1) * 1024)
            for i in range(4):
                kb.dma("sp", yh[:, i * 8:(i + 1) * 8, :], g.ybr[i].rearrange("(t p) l -> p t l", p=128)[:, :, ts], reads=[g.ybr_r], writes=[yh_r])
            for dt_ in range(16):
                for i in range(4):
                    wb_, wb_r = wbr.load([(g.w_branch[l, i, :, dt_ * 128:(dt_ + 1) * 128], 128)])
                    for tg in range(2):
                        pb = brot.next()
                        k += 1
                        gi = k % NG
                        si = k % 2
                        r0 = i * 2048 + dt_ * 128
                        c0 = hf * 1024 + tg * 512
                        kb.dma("sp", gt[gi][:], g.sgd[r0:r0 + 128, c0:c0 + 512], reads=[g.sgd_r], writes=[gt_r[gi]])

                        def bmm(pb=pb, wb_=wb_, tg=tg, i=i):
                            ins = None
                            for rt in range(8):
                                ins = nc.tensor.matmul(g.PS[pb][:, :], wb_[:, rt, :], yh[:, i * 8 + rt, tg * 512:(tg + 1) * 512], start=(rt == 0), stop=(rt == 7))
                            return ins
                        kb.op("pe", bmm, reads=[wb_r, yh_r], writes=[g.PSr[pb]])
                        if i == 0:
                            kb.op("dve", lambda gi=gi, pb=pb, tg=tg: nc.vector.tensor_tensor(out=acc[tg][:], in0=g.PS[pb][:, :], in1=gt[gi][:], op=ALU.mult),
                                  reads=[g.PSr[pb], gt_r[gi]], writes=[acc_r[tg]])
                        else:
                            kb.op("dve", lambda gi=gi, si=si, pb=pb: nc.vector.tensor_tensor(out=pr[si][:], in0=g.PS[pb][:, :], in1=gt[gi][:], op=ALU.mult),
                                  reads=[g.PSr[pb], gt_r[gi]], writes=[pr_r[si]])
                            if i < 3:
                                kb.op("dve", lambda si=si, tg=tg: nc.vector.tensor_tensor(out=acc[tg][:], in0=acc[tg][:], in1=pr[si][:], op=ALU.add),
                                      reads=[acc_r[tg], pr_r[si]], writes=[acc_r[tg]])
                            else:
                                kb.op("dve", lambda si=si, tg=tg, dt_=dt_: nc.vector.tensor_tensor(out=mT[:, dt_, tg * 512:(tg + 1) * 512], in0=acc[tg][:], in1=pr[si][:], op=ALU.add),
                                      reads=[acc_r[tg], pr_r[si]], writes=[mT_r[tg]])
            iters = [(eg, tt) for eg in range(8) for tt in range(8)]

            def load_x(k):
                eg, tt = iters[k]
                xi = k % 4
                tok0 = hf * 1024 + tt * 128
                kb.dma("sp", xt[xi][:], src[tok0:tok0 + 128, eg * 256:(eg + 1) * 256], reads=[getattr(g, "xres_r", Res())], writes=[xt_r[xi]])
            load_x(0)
            load_x(1)
            for k, (eg, tt) in enumerate(iters):
                if tt == 0:
                    wo_, wo_r = wo.load([(g.w_out[l, :, eg * 256:(eg + 1) * 256], 256)])
                if k + 2 < len(iters):
                    load_x(k + 2)
                po = orot.next()
                tok0 = hf * 1024 + tt * 128
                xi = k % 4

                def omm(po=po, wo_=wo_, tt=tt):
                    ins = None
                    for dt_ in range(16):
                        ins = nc.tensor.matmul(g.PS[po][:, 0:256], mT[:, dt_, tt * 128:(tt + 1) * 128], wo_[:, dt_, :], start=(dt_ == 0), stop=(dt_ == 15))
                    return ins
                kb.op("pe", omm, reads=[wo_r, mT_r[tt // 4]], writes=[g.PSr[po]])
                kb.op("dve", lambda xi=xi, po=po: nc.vector.tensor_tensor(out=xt[xi][:], in0=xt[xi][:], in1=g.PS[po][:, 0:256], op=ALU.add),
                      reads=[xt_r[xi], g.PSr[po]], writes=[xt_r[xi]])
                kb.dma("sp", dst[tok0:tok0 + 128, eg * 256:(eg + 1) * 256], xt[xi][:], reads=[xt_r[xi]], writes=[g.xres_r_new])
    g.xres_r = g.xres_r_new


def phase_final(g, src):
    nc, kb = g.nc, g.kb
    with ExitStack() as s:
        wbc = s.enter_context(_sbt(nc, "wbcF", [128, D], F32))
        xt = [s.enter_context(_sbt(nc, f"xtF{i}", [128, D], F32)) for i in range(2)]
        ot = [s.enter_context(_sbt(nc, f"otF{i}", [128, D], F32)) for i in range(2)]
        junk = s.enter_context(_sbt(nc, "junkF", [128, D], BF16))
        st = [s.enter_context(_sbt(nc, f"stF{i}", [128, 4], F32)) for i in range(2)]
        wbc_r, junk_r = Res(), Res()
        xt_r, ot_r, st_r = [Res(), Res()], [Res(), Res()], [Res(), Res()]
        y_r = Res()
        kb.dma("sp", wbc[:], g.final_norm_w[0:1, :].partition_broadcast(128), writes=[wbc_r])
        for t in range(16):
            b = t % 2
            kb.dma("sp", xt[b][:], src[t * 128:(t + 1) * 128, :], reads=[g.xres_r], writes=[xt_r[b]])
            kb.op("act", lambda b=b: nc.scalar.activation(out=junk[:], in_=xt[b][:], func=AF.Square, accum_out=st[b][:, 0:1]), reads=[xt_r[b]], writes=[junk_r, st_r[b]])
            kb.op("act", lambda b=b: nc.scalar.activation(out=st[b][:, 1:2], in_=st[b][:, 0:1], func=AF.Sqrt, scale=1.0 / D, bias=g.kc[:, 2:3]),
                  reads=[st_r[b], g.kc_r], writes=[st_r[b]])
            kb.op("dve", lambda b=b: nc.vector.reciprocal(out=st[b][:, 2:3], in_=st[b][:, 1:2]), reads=[st_r[b]], writes=[st_r[b]])
            kb.op("dve", lambda b=b: nc.vector.scalar_tensor_tensor(out=ot[b][:], in0=xt[b][:], scalar=st[b][:, 2:3], in1=wbc[:], op0=ALU.mult, op1=ALU.mult),
                  reads=[xt_r[b], st_r[b], wbc_r], writes=[ot_r[b]])
            kb.dma("sp", g.y[t * 128:(t + 1) * 128, :], ot[b][:], reads=[ot_r[b]], writes=[y_r])


_CACHE = {}


def kernel(**inputs):
    dbg = inputs.pop("_dbg", None)
    key = tuple(sorted(dbg)) if dbg else None
    if key not in _CACHE:
        _CACHE[key] = build(dbg)
    nc, g = _CACHE[key]
    consts, cblk = make_consts()
    x = np.ascontiguousarray(inputs["x"], dtype=np.float32)
    shared = {
        "norm_w": inputs["norm_w"], "w_in": inputs["w_in"],
        "diff_lambda": np.asarray(inputs["diff_lambda"]).reshape(DEPTH, 256),
        "diff_subln_w": inputs["diff_subln_w"], "swa_sinks": inputs["swa_sinks"],
        "ssd_conv_w": inputs["ssd_conv_w"], "ssd_conv_b": inputs["ssd_conv_b"], "ssd_dt_bias": inputs["ssd_dt_bias"],
        "ssd_a_log": inputs["ssd_a_log"], "ssd_d": inputs["ssd_d"], "ssd_norm_w": inputs["ssd_norm_w"],
        "conf_conv_w": inputs["conf_conv_w"], "conf_conv_b": inputs["conf_conv_b"], "conf_ln_w": inputs["conf_ln_w"],
        "conf_ln_b": inputs["conf_ln_b"], "w_branch": inputs["w_branch"], "w_out": inputs["w_out"],
        "rel_bias": inputs["rel_bias"], "final_norm_w": np.asarray(inputs["final_norm_w"]).reshape(1, D),
        "consts": consts, "cblk": cblk,
    }
    shared = {k: np.ascontiguousarray(v, dtype=np.float32) for k, v in shared.items()}
    ncores = 1 if dbg else 4
    in_maps = []
    for c in range(ncores):
        m = dict(shared)
        m["x"] = x[c % 4]
        in_maps.append(m)
    res = run_bass_kernel_spmd(nc, in_maps, core_ids=list(range(ncores)))
    if dbg:
        return res
    return np.stack([np.asarray(res.results[b]["y"]) for b in range(4)], axis=0).astype(np.float32)
```

```python
import math
from contextlib import ExitStack
import numpy as np
import concourse.bass as bass
import concourse.mybir as mybir
from concourse.bass_utils import run_bass_kernel_spmd

F32 = mybir.dt.float32
BF16 = mybir.dt.bfloat16
AF = mybir.ActivationFunctionType
ALU = mybir.AluOpType
AX = mybir.AxisListType

T = 2048
D = 2048
NIN = 20496
DEPTH = 2
EPS = 1e-6
NEG = -30000.0
OFF_AQ, OFF_AK, OFF_AV, OFF_AG = 0, 1024, 2048, 3072
OFF_BQ, OFF_BK, OFF_BV, OFF_BG = 4096, 5120, 5376, 5632
OFF_CX, OFF_CDT, OFF_CZ = 6656, 8192, 8208
OFF_DGLU, OFF_DG, OFF_MG = 9232, 11280, 12304

C_ID, C_AJ, C_TRI, C_TRIU, C_OH, C_MROW, C_SELA, C_OH31, C_ONES = 0, 128, 256, 384, 512, 896, 1280, 1281, 1409
NCONST = 1537


def t5_bucket_np(n):
    n = np.maximum(n, 0)
    nf = np.maximum(n, 1).astype(np.float32)
    large = 16 + (np.log(nf / np.float32(16)) / np.float32(math.log(128 / 16)) * np.float32(16)).astype(np.int32)
    large = np.minimum(large, 31)
    return np.where(n < 16, n, large)


def make_consts():
    c = np.zeros((128, NCONST), np.float32)
    p = np.arange(128)
    c[:, C_ID:C_ID + 128] = np.eye(128)
    c[p, C_AJ + 127 - p] = 1.0
    c[:, C_TRI:C_TRI + 128] = (p[:, None] <= p[None, :])
    c[:, C_TRIU:C_TRIU + 128] = (p[:, None] > p[None, :])
    m = np.arange(384)
    dist = m - 128
    bk = t5_bucket_np(dist)
    for j in range(384):
        if dist[j] >= 0:
            c[bk[j], C_OH + j] = 1.0
    c[0:8, C_MROW:C_MROW + 128] = NEG
    c[8:24, C_MROW:C_MROW + 128] = NEG
    c[8:24, C_MROW + 256:C_MROW + 384] = NEG
    c[0:8, C_SELA] = 1.0
    c[31, C_OH31:C_OH31 + 128] = 1.0
    c[:, C_ONES:C_ONES + 128] = 1.0
    blk = np.zeros((16, 2048), np.float32)
    for e in range(16):
        blk[e, e * 128:(e + 1) * 128] = 1.0
    return c, blk


_UNIQ = [0]


def _sbt(nc, name, shape, dt):
    _UNIQ[0] += 1
    return nc.sbuf_tensor(f"{name}_u{_UNIQ[0]}", list(shape), dt)


class Res:
    __slots__ = ("w", "r", "name")

    def __init__(self, name=""):
        self.w = None
        self.r = {}
        self.name = name


class KB:
    def __init__(self, nc):
        self.nc = nc
        self.eng = {"pe": nc.tensor, "act": nc.scalar, "dve": nc.vector, "pool": nc.gpsimd, "sp": nc.sync}
        self.sems = {}
        self.cnt = {}
        self.seen = {}
        for n in self.eng:
            self.sems[n] = nc.semaphore("e_" + n).__enter__()
            self.cnt[n] = 0
            self.seen[n] = {}
        self.dq = {}
        for q, issuer, ns in (("sp", "sp", 28), ("pool", "pool", 12), ("act", "act", 6)):
            sl = [nc.semaphore(f"d_{q}{i}").__enter__() for i in range(ns)]
            for i, s in enumerate(sl):
                self.sems[("d", q, i)] = s
            self.dq[q] = dict(issuer=issuer, n=ns, uses=[0] * ns, nxt=0)
        self.ninst = 0

    def _deps(self, reads, writes):
        d = {}
        for r in reads:
            if r.w is not None and r.w[1] > d.get(r.w[0], 0):
                d[r.w[0]] = r.w[1]
        for w in writes:
            if w.w is not None and w.w[1] > d.get(w.w[0], 0):
                d[w.w[0]] = w.w[1]
            for k, v in w.r.items():
                if v > d.get(k, 0):
                    d[k] = v
        return d

    def _wait(self, issuer, deps, skip_self=False):
        e = self.eng[issuer]
        seen = self.seen[issuer]
        for k, v in deps.items():
            if skip_self and k == issuer:
                continue
            if seen.get(k, 0) >= v:
                continue
            e.wait_ge(self.sems[k], v)
            seen[k] = v
            self.ninst += 1

    def _mark(self, key, val, reads, writes):
        for r in reads:
            if r.r.get(key, 0) < val:
                r.r[key] = val
        for w in writes:
            w.w = (key, val)
            w.r = {}

    def op(self, e, fn, reads=(), writes=()):
        self._wait(e, self._deps(reads, writes), skip_self=(e == "pe"))
        ins = fn()
        self.cnt[e] += 1
        ins.then_inc(self.sems[e], 1)
        self._mark(e, self.cnt[e], reads, writes)
        self.ninst += 1

    def dma(self, q, out, in_, reads=(), writes=(), **kw):
        Q = self.dq[q]
        issuer = Q["issuer"]
        deps = self._deps(reads, writes)
        slot = Q["nxt"]
        Q["nxt"] = (slot + 1) % Q["n"]
        key = ("d", q, slot)
        if Q["uses"][slot] > 0:
            deps[key] = max(deps.get(key, 0), 16 * Q["uses"][slot])
        self._wait(issuer, deps)
        Q["uses"][slot] += 1
        val = 16 * Q["uses"][slot]
        self.eng[issuer].dma_start(out=out, in_=in_, **kw).then_inc(self.sems[key], 16)
        self._mark(key, val, reads, writes)
        self.ninst += 1

    def barrier(self):
        tot = {}
        for n in self.eng:
            if self.cnt[n] > 0:
                tot[n] = self.cnt[n]
        for q, Q in self.dq.items():
            for i in range(Q["n"]):
                if Q["uses"][i] > 0:
                    tot[("d", q, i)] = 16 * Q["uses"][i]
        for n in self.eng:
            self._wait(n, tot)


class Ctx:
    pass


def build(dbg=None):
    nc = bass.Bass("TRN2", target_bir_lowering=False)
    kb = KB(nc)
    g = Ctx()
    g.nc, g.kb, g.dbg = nc, kb, dbg

    def din(name, shape, dt=F32):
        return nc.dram_tensor(name, list(shape), dt, kind="ExternalInput").ap()

    g.x = din("x", [T, D])
    g.norm_w = din("norm_w", [DEPTH, D])
    g.w_in = din("w_in", [DEPTH, D, NIN])
    g.diff_lambda = din("diff_lambda", [DEPTH, 256])
    g.diff_subln_w = din("diff_subln_w", [DEPTH, 128])
    g.swa_sinks = din("swa_sinks", [DEPTH, 16])
    g.ssd_conv_w = din("ssd_conv_w", [DEPTH, 4, 1536])
    g.ssd_conv_b = din("ssd_conv_b", [DEPTH, 1536])
    g.ssd_dt_bias = din("ssd_dt_bias", [DEPTH, 16])
    g.ssd_a_log = din("ssd_a_log", [DEPTH, 16])
    g.ssd_d = din("ssd_d", [DEPTH, 16])
    g.ssd_norm_w = din("ssd_norm_w", [DEPTH, 1024])
    g.conf_conv_w = din("conf_conv_w", [DEPTH, 31, 1024])
    g.conf_conv_b = din("conf_conv_b", [DEPTH, 1024])
    g.conf_ln_w = din("conf_ln_w", [DEPTH, 1024])
    g.conf_ln_b = din("conf_ln_b", [DEPTH, 1024])
    g.w_branch = din("w_branch", [DEPTH, 4, 1024, D])
    g.w_out = din("w_out", [DEPTH, D, D])
    g.rel_bias = din("rel_bias", [32, 24])
    g.final_norm_w = din("final_norm_w", [1, D])
    g.consts = din("consts", [128, NCONST])
    g.cblk = din("cblk", [16, 2048])
    g.y = nc.dram_tensor("y", [T, D], F32, kind="ExternalOutput").ap()
    g.xres = [nc.dram_tensor(f"xres{i}", [T, D], F32).ap() for i in range(2)]
    g.ybr = nc.dram_tensor("ybr", [4, 1024, T], BF16).ap()
    g.sgd = nc.dram_tensor("sgd", [8192, T], BF16).ap()
    g.mgk = [0]
    g.tdram = nc.dram_tensor("tdram", [24, 384], F32).ap()
    g.dbg_out = {}
    if dbg:
        if "hT" in dbg:
            g.dbg_out["hT"] = nc.dram_tensor("dbg_hT", [128, 16, T], BF16, kind="ExternalOutput").ap()
        if "ybr" in dbg:
            g.dbg_out["ybr"] = nc.dram_tensor("dbg_ybr", [4, 1024, T], BF16, kind="ExternalOutput").ap()
        if "x0" in dbg:
            g.dbg_out["x0"] = nc.dram_tensor("dbg_x0", [T, D], F32, kind="ExternalOutput").ap()

    st = ExitStack()

    def sb(name, shape, dt):
        return st.enter_context(_sbt(nc, name, list(shape), dt))

    g.PS = [st.enter_context(nc.psum_tensor(f"ps{i}", [128, 512], F32)) for i in range(8)]
    g.PSr = [Res(f"ps{i}") for i in range(8)]

    g.cf = sb("cf", [128, NCONST], F32)
    g.cf_r = Res()
    kb.dma("sp", g.cf[:], g.consts, writes=[g.cf_r])
    g.idb = sb("idb", [128, 128], BF16)
    g.onesb = sb("onesb", [128, 128], BF16)
    g.trib = sb("trib", [128, 128], BF16)
    g.cb_r = Res()
    kb.op("dve", lambda: nc.vector.tensor_copy(out=g.idb[:], in_=g.cf[:, C_ID:C_ID + 128]), reads=[g.cf_r], writes=[g.cb_r])
    kb.op("dve", lambda: nc.vector.tensor_copy(out=g.onesb[:], in_=g.cf[:, C_ONES:C_ONES + 128]), reads=[g.cf_r], writes=[g.cb_r])
    kb.op("dve", lambda: nc.vector.tensor_copy(out=g.trib[:], in_=g.cf[:, C_TRI:C_TRI + 128]), reads=[g.cf_r], writes=[g.cb_r])
    g.kc = sb("kc", [128, 8], F32)
    g.kc_r = Res()
    for i, v in enumerate([0.0, 8.0, EPS, 1.0 / 1024, -1.0, 1.0 / 512]):
        kb.op("dve", lambda i=i, v=v: nc.vector.memset(g.kc[:, i:i + 1], v), writes=[g.kc_r])

    srcs = [g.x, g.xres[0], g.xres[1]]
    for l in range(DEPTH):
        with ExitStack() as ls:
            g.ls = ls
            g.hT = ls.enter_context(_sbt(nc, f"hT{l}", [128, 16, T], BF16))
            g.hT_r = [Res(f"hT{tg}") for tg in range(4)]
            phase_norm(g, l, srcs[l])
            g.sgd_r = Res()
            g.mgjobs = mg_job_list(g, l)
            if dbg and "hT" in dbg and l == 0:
                kb.dma("sp", g.dbg_out["hT"], g.hT[:], reads=g.hT_r)
            kb.barrier()
            with ExitStack() as bs:
                g.bt = bs.enter_context(_sbt(nc, f"bt{l}", [128, 2, 24, 128], BF16))
                g.bt_r = Res()
                g.c31 = bs.enter_context(_sbt(nc, f"c31{l}", [128, 24], F32))
                g.c31_r = Res()
                setup_bias(g)
                if not (dbg and "skipA" in dbg):
                    mixer_A(g, l)
                    kb.barrier()
                if not (dbg and "skipB" in dbg):
                    mixer_B(g, l)
                    kb.barrier()
            if not (dbg and "skipC" in dbg):
                mixer_C(g, l)
                kb.barrier()
            if not (dbg and "skipD" in dbg):
                mixer_D(g, l)
                kb.barrier()
        if dbg and "ybr" in dbg and l == 0:
            kb.dma("sp", g.dbg_out["ybr"], g.ybr, reads=[g.ybr_r])
            kb.barrier()
        if dbg and "stop_mix" in dbg:
            break
        phase3(g, l, srcs[l], srcs[l + 1])
        kb.barrier()
        if dbg and "x0" in dbg and l == 0:
            kb.dma("sp", g.dbg_out["x0"], g.xres[0], reads=[g.xres_r])
            kb.barrier()
    if not (dbg and "stop_mix" in dbg):
        phase_final(g, srcs[DEPTH])
    kb.barrier()
    return nc, g


def setup_bias(g):
    nc, kb = g.nc, g.kb
    with ExitStack() as s:
        tab = s.enter_context(_sbt(nc, "tab", [32, 24], F32))
        tt = s.enter_context(_sbt(nc, "ttab", [24, 384], F32))
        csh = s.enter_context(_sbt(nc, "csh", [24, 1], F32))
        hk = s.enter_context(_sbt(nc, "hk", [128, 2, 24, 128], F32))
        tab_r, tt_r, csh_r, hk_r, td_r = Res(), Res(), Res(), Res(), Res()
        kb.dma("sp", tab[:], g.rel_bias, writes=[tab_r])
        ps = g.PS[0]
        kb.op("pe", lambda: nc.tensor.matmul(ps[0:24, 0:384], tab[0:32, 0:24], g.cf[0:32, C_OH:C_OH + 384], start=True, stop=True),
              reads=[tab_r, g.cf_r], writes=[g.PSr[0]])
        kb.op("dve", lambda: nc.vector.tensor_tensor(out=csh[:], in0=ps[0:24, 383:384], in1=g.cf[0:24, C_SELA:C_SELA + 1], op=ALU.mult),
              reads=[g.PSr[0], g.cf_r], writes=[csh_r])
        kb.op("dve", lambda: nc.vector.tensor_scalar(out=tt[:], in0=ps[0:24, 0:384], scalar1=csh[:, 0:1], scalar2=g.kc[0:24, 1:2],
                                                     op0=ALU.subtract, op1=ALU.mult), reads=[g.PSr[0], csh_r, g.kc_r], writes=[tt_r])
        kb.op("dve", lambda: nc.vector.tensor_tensor(out=tt[:], in0=tt[:], in1=g.cf[0:24, C_MROW:C_MROW + 384], op=ALU.add),
              reads=[tt_r, g.cf_r], writes=[tt_r])
        kb.dma("sp", g.tdram, tt[:], reads=[tt_r], writes=[td_r])
        for kind, off in ((0, 1), (1, 129)):
            src = bass.AP(g.tdram.tensor, off, [[1, 128], [384, 24], [1, 128]])
            kb.dma("sp", hk[:, kind], src, reads=[td_r], writes=[hk_r])
        for kind in range(2):
            for hg in range(6):
                pi = 1 + (kind * 6 + hg) % 4
                p = g.PS[pi]
                kb.op("pe", lambda p=p, kind=kind, hg=hg: nc.tensor.matmul(
                    p[:, :], g.cf[:, C_AJ:C_AJ + 128], hk[:, kind, hg * 4:(hg + 1) * 4, :].rearrange("p h q -> p (h q)"),
                    start=True, stop=True), reads=[hk_r, g.cf_r], writes=[g.PSr[pi]])
                kb.op("dve", lambda p=p, kind=kind, hg=hg: nc.vector.tensor_copy(
                    out=g.bt[:, kind, hg * 4:(hg + 1) * 4, :].rearrange("p h q -> p (h q)"), in_=p[:, :]),
                    reads=[g.PSr[pi]], writes=[g.bt_r])
        p = g.PS[5]
        kb.op("pe", lambda: nc.tensor.matmul(p[:, 0:24], g.cf[0:32, C_OH31:C_OH31 + 128], tab[0:32, 0:24], start=True, stop=True),
              reads=[tab_r, g.cf_r], writes=[g.PSr[5]])
        kb.op("dve", lambda: nc.vector.tensor_copy(out=g.c31[:], in_=p[:, 0:24]), reads=[g.PSr[5]], writes=[g.c31_r])
        kb.barrier()


def phase_norm(g, l, src):
    nc, kb = g.nc, g.kb
    with ExitStack() as s:
        wbc = s.enter_context(_sbt(nc, "wbc", [128, D], F32))
        xt = [s.enter_context(_sbt(nc, f"xt{i}", [128, D], F32)) for i in range(2)]
        hb = [s.enter_context(_sbt(nc, f"hb{i}", [128, D], BF16)) for i in range(2)]
        junk = s.enter_context(_sbt(nc, "junk", [128, D], BF16))
        st = [s.enter_context(_sbt(nc, f"st{i}", [128, 4], F32)) for i in range(2)]
        wbc_r, junk_r = Res(), Res()
        xt_r = [Res(), Res()]
        hb_r = [Res(), Res()]
        st_r = [Res(), Res()]
        kb.dma("sp", wbc[:], g.norm_w[l:l + 1, :].partition_broadcast(128), writes=[wbc_r])
        for t in range(16):
            b = t % 2
            kb.dma("sp", xt[b][:], src[t * 128:(t + 1) * 128, :], reads=[getattr(g, "xres_r", Res())], writes=[xt_r[b]])
            kb.op("act", lambda b=b: nc.scalar.activation(out=junk[:], in_=xt[b][:], func=AF.Square, accum_out=st[b][:, 0:1]),
                  reads=[xt_r[b]], writes=[junk_r, st_r[b]])
            kb.op("act", lambda b=b: nc.scalar.activation(out=st[b][:, 1:2], in_=st[b][:, 0:1], func=AF.Sqrt, scale=1.0 / D, bias=g.kc[:, 2:3]),
                  reads=[st_r[b], g.kc_r], writes=[st_r[b]])
            kb.op("dve", lambda b=b: nc.vector.reciprocal(out=st[b][:, 2:3], in_=st[b][:, 1:2]), reads=[st_r[b]], writes=[st_r[b]])
            kb.op("dve", lambda b=b: nc.vector.scalar_tensor_tensor(out=hb[b][:], in0=xt[b][:], scalar=st[b][:, 2:3], in1=wbc[:],
                                                                    op0=ALU.mult, op1=ALU.mult),
                  reads=[xt_r[b], st_r[b], wbc_r], writes=[hb_r[b]])
            for q4 in range(4):
                pi = (t * 4 + q4) % 8
                pb = g.PS[pi][:].bitcast(BF16)

                def tr(pb=pb, b=b, q4=q4):
                    ins = None
                    for j in range(4):
                        kt = q4 * 4 + j
                        ins = nc.tensor.transpose(pb[:, j * 128:(j + 1) * 128], hb[b][:, kt * 128:(kt + 1) * 128], g.idb[:])
                    return ins
                kb.op("pe", tr, reads=[hb_r[b], g.cb_r], writes=[g.PSr[pi]])
                dst = g.hT[:, q4 * 4:(q4 + 1) * 4, t * 128:(t + 1) * 128]
                srcp = pb[:, 0:512].rearrange("p (j q) -> p j q", j=4)
                if q4 % 2 == 0:
                    kb.op("act", lambda dst=dst, srcp=srcp: nc.scalar.copy(out=dst, in_=srcp), reads=[g.PSr[pi]], writes=[g.hT_r[t // 4]])
                else:
                    kb.op("dve", lambda dst=dst, srcp=srcp: nc.vector.tensor_copy(out=dst, in_=srcp), reads=[g.PSr[pi]], writes=[g.hT_r[t // 4]])


class WPool:
    def __init__(self, g, s, name, n, kt, cols):
        self.g = g
        self.bufs = [s.enter_context(_sbt(g.nc, f"{name}{i}", [128, kt, cols], BF16)) for i in range(n)]
        self.res = [Res(f"{name}{i}") for i in range(n)]
        self.i = 0
        self.kt = kt

    def load(self, pieces):
        i = self.i
        self.i = (i + 1) % len(self.bufs)
        c = 0
        for ap, ncols in pieces:
            self.g.kb.dma("pool", self.bufs[i][:, :, c:c + ncols], ap.rearrange("(kt p) n -> p kt n", p=128), writes=[self.res[i]])
            c += ncols
        return self.bufs[i], self.res[i]


def win(g, l, c0, n):
    return g.w_in[l, :, c0:c0 + n]


def proj_fm(g, wbuf, wres, coff, tg, pi, M=128):
    nc = g.nc
    ps = g.PS[pi]

    def f():
        ins = None
        for kt in range(16):
            ins = nc.tensor.matmul(ps[0:M, :], wbuf[:, kt, coff:coff + M], g.hT[:, kt, tg * 512:(tg + 1) * 512],
                                   start=(kt == 0), stop=(kt == 15))
        return ins
    g.kb.op("pe", f, reads=[wres, g.hT_r[tg]], writes=[g.PSr[pi]])


class Rot:
    def __init__(self, items):
        self.items = list(items)
        self.i = 0

    def next(self):
        v = self.items[self.i]
        self.i = (self.i + 1) % len(self.items)
        return v


def mg_job_list(g, l):
    nc, kb = g.nc, g.kb
    jobs = []
    for i in range(4):
        for dtp in range(8):
            stt = {}
            for dj in range(2):
                for tg in range(4):
                    def job(i=i, dtp=dtp, dj=dj, tg=tg, stt=stt, first=(dj == 0 and tg == 0)):
                        if first:
                            stt["w"] = g.mgpool.load([(win(g, l, OFF_MG + i * 2048 + dtp * 256, 256), 256)])
                        wbuf, wres = stt["w"]
                        pi = g.mgrot.next()
                        proj_fm(g, wbuf, wres, dj * 128, tg, pi)
                        k = g.mgk[0]
                        g.mgk[0] += 1
                        sgi = k % len(g.mgst)
                        kb.op("act", lambda: nc.scalar.activation(out=g.mgst[sgi][:], in_=g.PS[pi][:, :], func=AF.Sigmoid), reads=[g.PSr[pi]], writes=[g.mgst_r[sgi]])
                        r0 = i * 2048 + dtp * 256 + dj * 128
                        kb.dma("sp", g.sgd[r0:r0 + 128, tg * 512:(tg + 1) * 512], g.mgst[sgi][:], reads=[g.mgst_r[sgi]], writes=[g.sgd_r])
                    jobs.append(job)
    return jobs


def mg_host(g, s, pool, rot):
    g.mgpool = pool
    g.mgrot = rot
    g.mgst = [s.enter_context(_sbt(g.nc, f"mgst{i}", [128, 512], BF16)) for i in range(2)]
    g.mgst_r = [Res(), Res()]


def mg_run(g, n):
    for _ in range(n):
        if g.mgjobs:
            g.mgjobs.pop(0)()


def mixer_A(g, l):
    nc, kb = g.nc, g.kb
    lam_init = 0.8 - 0.6 * math.exp(-0.3 * l)
    if not hasattr(g, "ybr_r"):
        g.ybr_r = Res()
    with ExitStack() as s:
        def sb(name, shape, dt):
            return s.enter_context(_sbt(nc, name, list(shape), dt))
        wp = WPool(g, s, "wA", 2, 16, 512)
        dl = sb("dl", [128, 256], F32)
        sc = sb("scA", [128, 8], F32)
        dl_r, sc_r = Res(), Res()
        kb.dma("sp", dl[:], g.diff_lambda[l:l + 1, :].partition_broadcast(128), writes=[dl_r])
        kb.op("dve", lambda: nc.vector.tensor_tensor(out=dl[:, 0:64], in0=dl[:, 0:64], in1=dl[:, 64:128], op=ALU.mult), reads=[dl_r], writes=[dl_r])
        kb.op("dve", lambda: nc.vector.tensor_tensor(out=dl[:, 128:192], in0=dl[:, 128:192], in1=dl[:, 192:256], op=ALU.mult), reads=[dl_r], writes=[dl_r])
        kb.op("dve", lambda: nc.vector.reduce_sum(out=sc[:, 0:1], in_=dl[:, 0:64], axis=AX.X), reads=[dl_r], writes=[sc_r])
        kb.op("dve", lambda: nc.vector.reduce_sum(out=sc[:, 1:2], in_=dl[:, 128:192], axis=AX.X), reads=[dl_r], writes=[sc_r])
        kb.op("act", lambda: nc.scalar.activation(out=sc[:, 2:4], in_=sc[:, 0:2], func=AF.Exp), reads=[sc_r], writes=[sc_r])
        kb.op("dve", lambda: nc.vector.tensor_tensor(out=sc[:, 4:5], in0=sc[:, 3:4], in1=sc[:, 2:3], op=ALU.subtract), reads=[sc_r], writes=[sc_r])
        kb.op("dve", lambda: nc.vector.tensor_scalar_add(out=sc[:, 5:6], in0=sc[:, 4:5], scalar1=-lam_init), reads=[sc_r], writes=[sc_r])
        kb.dma("sp", sc[:, 6:7], g.diff_subln_w[l:l + 1, :].rearrange("o e -> e o"), writes=[sc_r], allow_slow_non_contiguous=True)
        kb.op("dve", lambda: nc.vector.tensor_scalar_mul(out=sc[:, 7:8], in0=sc[:, 6:7], scalar1=(1.0 - lam_init)), reads=[sc_r], writes=[sc_r])
        neglam = sc[:, 5:6]
        swcol = sc[:, 7:8]

        NB = 2
        qT = [sb(f"qT{i}", [128, T], BF16) for i in range(NB)]
        kT = [sb(f"kT{i}", [128, T], BF16) for i in range(NB)]
        vT = [sb(f"vT{i}", [128, T], BF16) for i in range(NB)]
        gT = [sb(f"gT{i}", [128, T], BF16) for i in range(NB)]
        Vt = [sb(f"Vt{i}", [128, 16, 128], BF16) for i in range(NB)]
        qT_r = [Res() for _ in range(NB)]
        kT_r = [Res() for _ in range(NB)]
        vT_r = [Res() for _ in range(NB)]
        gT_r = [Res() for _ in range(NB)]
        Vt_r = [Res() for _ in range(NB)]
        NE = 5
        Eb = [sb(f"Eb{i}", [128, 512], BF16) for i in range(NE)]
        Eb_r = [Res() for _ in range(NE)]
        erot = Rot(range(NE))
        f1 = [sb(f"fA{i}", [128, 512], F32) for i in range(6)]
        f1_r = [Res() for _ in range(6)]
        sqb = sb("sqb", [128, 512], BF16)
        sqb_r = Res()
        yb = [sb(f"ybA{i}", [128, 512], BF16) for i in range(2)]
        yb_r = [Res(), Res()]
        prot = Rot([0, 1, 2, 3])
        srot = prot
        PO = [4, 6]
        PSUMS = [5, 7]
        def inproj_closures(h):
            hb = h % NB
            stt = {}
            out = []

            def ld():
                stt["w"] = wp.load([(win(g, l, OFF_AQ + h * 128, 128), 128), (win(g, l, OFF_AK + h * 128, 128), 128),
                                    (win(g, l, OFF_AV + h * 128, 128), 128), (win(g, l, OFF_AG + h * 128, 128), 128)])
            out.append(ld)
            dsts = [(qT[hb], qT_r[hb]), (kT[hb], kT_r[hb]), (vT[hb], vT_r[hb]), (gT[hb], gT_r[hb])]
            for ti in (2, 1, 0, 3):
                for tg in range(4):
                    def grp(ti=ti, tg=tg):
                        wbuf, wres = stt["w"]
                        pi = prot.next()
                        proj_fm(g, wbuf, wres, ti * 128, tg, pi)
                        dst, dres = dsts[ti]
                        o = dst[:, tg * 512:(tg + 1) * 512]
                        if ti == 3:
                            kb.op("act", lambda: nc.scalar.activation(out=o, in_=g.PS[pi][:, :], func=AF.Silu), reads=[g.PSr[pi]], writes=[dres])
                        else:
                            kb.op("dve", lambda: nc.vector.tensor_copy(out=o, in_=g.PS[pi][:, :]), reads=[g.PSr[pi]], writes=[dres])
                    out.append(grp)
                if ti == 2:
                    for q4 in range(4):
                        def trv(q4=q4):
                            pi = prot.next()
                            pb = g.PS[pi][:].bitcast(BF16)

                            def tr():
                                ins = None
                                for j in range(4):
                                    tt = q4 * 4 + j
                                    ins = nc.tensor.transpose(pb[:, j * 128:(j + 1) * 128], vT[hb][:, tt * 128:(tt + 1) * 128], g.idb[:])
                                return ins
                            kb.op("pe", tr, reads=[vT_r[hb], g.cb_r], writes=[g.PSr[pi]])
                            kb.op("dve", lambda: nc.vector.tensor_copy(out=Vt[hb][:, q4 * 4:(q4 + 1) * 4, :], in_=pb[:, 0:512].rearrange("p (j q) -> p j q", j=4)),
                                  reads=[g.PSr[pi]], writes=[Vt_r[hb]])
                        out.append(trv)
            return out

        for c_ in inproj_closures(0):
            c_()
        for h in range(8):
            hb = h % NB
            nxt = inproj_closures(h + 1) if h + 1 < 8 else []
            def make_step(G, m, j, hb=hb, h=h):
                po, psm = PO[m], PSUMS[m]
                last = 4 * G + 3
                c0 = max(j - 4 * G, 0) * 128
                st = {}

                def emit_S():
                    si = srot.next()
                    pS = g.PS[si]

                    def smm():
                        nb = []
                        if j >= 4 * G:
                            nb.append((0, (j - 4 * G) * 128))
                        if 4 * G <= j + 1 <= 4 * G + 3:
                            nb.append((1, (j + 1 - 4 * G) * 128))
                        ins = nc.tensor.matmul(pS[:, c0:512], kT[hb][m * 64:(m + 1) * 64, j * 128:(j + 1) * 128],
                                               qT[hb][m * 64:(m + 1) * 64, G * 512 + c0:(G + 1) * 512], start=True, stop=(len(nb) == 0))
                        for bi, (kind, cc) in enumerate(nb):
                            ins = nc.tensor.matmul(pS[:, cc:cc + 128], g.idb[:], g.bt[:, kind, h, :], start=False, stop=(bi == len(nb) - 1))
                        return ins
                    kb.op("pe", smm, reads=[kT_r[hb], qT_r[hb], g.bt_r, g.cb_r], writes=[g.PSr[si]])
                    ei = erot.next()
                    st["ei"] = ei
                    kb.op("act", lambda: nc.scalar.activation(out=Eb[ei][:, c0:512], in_=pS[:, c0:512], func=AF.Exp, scale=0.125, bias=g.c31[:, h:h + 1]),
                          reads=[g.PSr[si], g.c31_r], writes=[Eb_r[ei]])

                def emit_PV():
                    ei = st["ei"]

                    def pv():
                        nc.tensor.matmul(g.PS[po][:, c0:512], Vt[hb][:, j, :], Eb[ei][:, c0:512], start=(j == 0), stop=(j == last))
                        return nc.tensor.matmul(g.PS[psm][:, c0:512], g.onesb[:], Eb[ei][:, c0:512], start=(j == 0), stop=(j == last))
                    kb.op("pe", pv, reads=[Vt_r[hb], Eb_r[ei], g.cb_r], writes=[g.PSr[po], g.PSr[psm]])
                    if m == 1 and j == last:
                        norm_G(G)
                return emit_S, emit_PV

            def norm_G(G, hb=hb, h=h):
                r1, o1, o2, o, rs, y1 = f1
                kb.op("dve", lambda: nc.vector.tensor_copy(out=r1[:], in_=g.PS[PSUMS[0]][:, :]), reads=[g.PSr[PSUMS[0]]], writes=[f1_r[0]])
                kb.op("dve", lambda: nc.vector.tensor_copy(out=o1[:], in_=g.PS[PO[0]][:, :]), reads=[g.PSr[PO[0]]], writes=[f1_r[1]])
                kb.op("dve", lambda: nc.vector.tensor_copy(out=rs[:], in_=g.PS[PSUMS[1]][:, :]), reads=[g.PSr[PSUMS[1]]], writes=[f1_r[4]])
                kb.op("dve", lambda: nc.vector.tensor_copy(out=o2[:], in_=g.PS[PO[1]][:, :]), reads=[g.PSr[PO[1]]], writes=[f1_r[2]])
                kb.op("dve", lambda: nc.vector.reciprocal(out=r1[:], in_=r1[:]), reads=[f1_r[0]], writes=[f1_r[0]])
                kb.op("dve", lambda: nc.vector.tensor_tensor(out=o1[:], in0=o1[:], in1=r1[:], op=ALU.mult), reads=[f1_r[1], f1_r[0]], writes=[f1_r[1]])
                kb.op("dve", lambda: nc.vector.reciprocal(out=rs[:], in_=rs[:]), reads=[f1_r[4]], writes=[f1_r[4]])
                kb.op("dve", lambda: nc.vector.tensor_tensor(out=o2[:], in0=o2[:], in1=rs[:], op=ALU.mult), reads=[f1_r[2], f1_r[4]], writes=[f1_r[2]])
                kb.op("dve", lambda: nc.vector.scalar_tensor_tensor(out=o[:], in0=o2[:], scalar=neglam, in1=o1[:], op0=ALU.mult, op1=ALU.add),
                      reads=[f1_r[1], f1_r[2], sc_r], writes=[f1_r[3]])
                kb.op("act", lambda: nc.scalar.activation(out=sqb[:], in_=o[:], func=AF.Square), reads=[f1_r[3]], writes=[sqb_r])
                pi = prot.next()
                kb.op("pe", lambda pi=pi: nc.tensor.matmul(g.PS[pi][:, :], g.onesb[:], sqb[:], start=True, stop=True), reads=[sqb_r, g.cb_r], writes=[g.PSr[pi]])
                kb.op("act", lambda pi=pi: nc.scalar.activation(out=rs[:], in_=g.PS[pi][:, :], func=AF.Sqrt, scale=1.0 / 128, bias=g.kc[:, 2:3]),
                      reads=[g.PSr[pi], g.kc_r], writes=[f1_r[4]])
                kb.op("dve", lambda: nc.vector.reciprocal(out=rs[:], in_=rs[:]), reads=[f1_r[4]], writes=[f1_r[4]])
                kb.op("dve", lambda: nc.vector.scalar_tensor_tensor(out=y1[:], in0=o[:], scalar=swcol, in1=gT[hb][:, G * 512:(G + 1) * 512],
                                                                    op0=ALU.mult, op1=ALU.mult), reads=[f1_r[3], sc_r, gT_r[hb]], writes=[f1_r[5]])
                yi = (h * 4 + G) % 2
                kb.op("dve", lambda: nc.vector.tensor_tensor(out=yb[yi][:], in0=y1[:], in1=rs[:], op=ALU.mult), reads=[f1_r[5], f1_r[4]], writes=[yb_r[yi]])
                kb.dma("sp", g.ybr[0, h * 128:(h + 1) * 128, G * 512:(G + 1) * 512], yb[yi][:], reads=[yb_r[yi]], writes=[g.ybr_r])

            steps = [make_step(G, m, j) for G in range(4) for m in range(2) for j in range(0, 4 * G + 4)]
            SKEW = 2
            pend = []
            stride = 3
            for si_, (eS, ePV) in enumerate(steps):
                eS()
                pend.append(ePV)
                if len(pend) > SKEW:
                    pend.pop(0)()
                if nxt and si_ % stride == stride - 1:
                    nxt.pop(0)()
            while pend:
                pend.pop(0)()
            while nxt:
                nxt.pop(0)()


def mixer_B(g, l):
    nc, kb = g.nc, g.kb
    if not hasattr(g, "ybr_r"):
        g.ybr_r = Res()
    with ExitStack() as s:
        def sb(name, shape, dt):
            return s.enter_context(_sbt(nc, name, list(shape), dt))
        wp = WPool(g, s, "wB", 2, 16, 512)
        mgp = WPool(g, s, "wBm", 2, 16, 256)
        sk = sb("skB", [128, 16], F32)
        sk_r = Res()
        for par in range(2):
            src = bass.AP(g.swa_sinks.tensor, l * 16 + par, [[0, 64], [2, 8]])
            kb.dma("sp", sk[par * 64:(par + 1) * 64, 0:8], src, writes=[sk_r], allow_slow_non_contiguous=True)
        kb.op("act", lambda: nc.scalar.activation(out=sk[:, 8:16], in_=sk[:, 0:8], func=AF.Exp), reads=[sk_r], writes=[sk_r])
        kd = [sb(f"kd{i}", [128, T], BF16) for i in range(4)]
        kd_r = [Res() for _ in range(4)]
        Vb = sb("Vb", [128, 16, 256], BF16)
        Vb_r = Res()
        prot = Rot([0, 1])
        srot = Rot([2, 3, 4, 5])
        PO, PSM = 6, 7
        wbuf, wres = wp.load([(win(g, l, OFF_BK + (i // 2) * 64, 64), 64) for i in range(8)])
        ev = 0
        for kv in range(4):
            for tg in range(4):
                pi = prot.next()
                proj_fm(g, wbuf, wres, kv * 128, tg, pi)
                o = kd[kv][:, tg * 512:(tg + 1) * 512]
                ev += 1
                if ev % 2 == 0:
                    kb.op("act", lambda o=o, pi=pi: nc.scalar.copy(out=o, in_=g.PS[pi][:, :]), reads=[g.PSr[pi]], writes=[kd_r[kv]])
                else:
                    kb.op("dve", lambda o=o, pi=pi: nc.vector.tensor_copy(out=o, in_=g.PS[pi][:, :]), reads=[g.PSr[pi]], writes=[kd_r[kv]])
        wbuf, wres = wp.load([(win(g, l, OFF_BV, 256), 256)])
        for tt in range(16):
            pi = prot.next()

            def vmm(pi=pi, tt=tt, wbuf=wbuf):
                ins = None
                for kt in range(16):
                    ins = nc.tensor.matmul(g.PS[pi][:, 0:256], g.hT[:, kt, tt * 128:(tt + 1) * 128], wbuf[:, kt, 0:256], start=(kt == 0), stop=(kt == 15))
                return ins
            kb.op("pe", vmm, reads=[wres, g.hT_r[tt // 4]], writes=[g.PSr[pi]])
            kb.op("dve", lambda pi=pi, tt=tt: nc.vector.tensor_copy(out=Vb[:, tt, :], in_=g.PS[pi][:, 0:256]), reads=[g.PSr[pi]], writes=[Vb_r])
        NB = 2
        qT = [sb(f"qB{i}", [128, T], BF16) for i in range(NB)]
        gT = [sb(f"gB{i}", [128, T], BF16) for i in range(NB)]
        qT_r = [Res() for _ in range(NB)]
        gT_r = [Res() for _ in range(NB)]
        NE = 5
        Eb = [sb(f"EbB{i}", [128, 256], BF16) for i in range(NE)]
        Eb_r = [Res() for _ in range(NE)]
        erot = Rot(range(NE))
        f1 = [sb(f"fB{i}", [128, 512], F32) for i in range(2)]
        f1_r = [Res() for _ in range(2)]
        yb = [sb(f"ybB{i}", [128, 512], BF16) for i in range(2)]
        yb_r = [Res(), Res()]
        def inprojB(t):
            hb = t % NB
            stt = {}
            out = []

            def ld():
                stt["w"] = wp.load([(win(g, l, OFF_BQ + t * 128, 128), 128), (win(g, l, OFF_BG + t * 128, 128), 128)])
            out.append(ld)
            for ti in range(2):
                for tg in range(4):
                    def grp(ti=ti, tg=tg):
                        wbuf, wres = stt["w"]
                        pi = prot.next()
                        proj_fm(g, wbuf, wres, ti * 128, tg, pi)
                        if ti == 0:
                            o = qT[hb][:, tg * 512:(tg + 1) * 512]
                            kb.op("dve", lambda: nc.vector.tensor_copy(out=o, in_=g.PS[pi][:, :]), reads=[g.PSr[pi]], writes=[qT_r[hb]])
                        else:
                            o = gT[hb][:, tg * 512:(tg + 1) * 512]
                            kb.op("act", lambda: nc.scalar.activation(out=o, in_=g.PS[pi][:, :], func=AF.Silu), reads=[g.PSr[pi]], writes=[gT_r[hb]])
                    out.append(grp)
            return out

        mg_host(g, s, mgp, prot)
        for c_ in inprojB(0):
            c_()
        for t in range(8):
            hb = t % NB
            kv = t // 2
            nxt = inprojB(t + 1) if t + 1 < 8 else []

            def make_stepB(G, par, j, t=t, hb=hb, kv=kv):
                hq = 2 * t + par
                lo, hi = par * 64, (par + 1) * 64
                blocks = [i for i in (j, j + 1) if 4 * G <= i <= 4 * G + 3]
                cA = (blocks[0] - 4 * G) * 128
                ncol = 128 * len(blocks)
                st = {}

                def emit_S():
                    si = srot.next()
                    pS = g.PS[si]

                    def smm():
                        nc.tensor.matmul(pS[:, 0:ncol], kd[kv][lo:hi, j * 128:(j + 1) * 128],
                                         qT[hb][lo:hi, G * 512 + cA:G * 512 + cA + ncol], start=True, stop=False)
                        ins = None
                        for bi, i in enumerate(blocks):
                            kind = 0 if i == j else 1
                            ins = nc.tensor.matmul(pS[:, bi * 128:(bi + 1) * 128], g.idb[:], g.bt[:, kind, 8 + hq, :], start=False, stop=(bi == len(blocks) - 1))
                        return ins
                    kb.op("pe", smm, reads=[kd_r[kv], qT_r[hb], g.bt_r, g.cb_r], writes=[g.PSr[si]])
                    ei = erot.next()
                    st["ei"] = ei
                    kb.op("act", lambda: nc.scalar.activation(out=Eb[ei][:, 0:ncol], in_=pS[:, 0:ncol], func=AF.Exp, scale=0.125),
                          reads=[g.PSr[si]], writes=[Eb_r[ei]])

                def emit_PV():
                    ei = st["ei"]

                    def pv():
                        ins = None
                        for bi, i in enumerate(blocks):
                            cc = (i - 4 * G) * 128
                            first = (j == i - 1) or (i == 0)
                            lastk = (j == i)
                            nc.tensor.matmul(g.PS[PO][lo:hi, cc:cc + 128], Vb[:, j, kv * 64:(kv + 1) * 64], Eb[ei][:, bi * 128:(bi + 1) * 128], start=first, stop=lastk)
                            ins = nc.tensor.matmul(g.PS[PSM][lo:hi, cc:cc + 128], g.onesb[:, 0:64], Eb[ei][:, bi * 128:(bi + 1) * 128], start=first, stop=lastk)
                        return ins
                    kb.op("pe", pv, reads=[Vb_r, Eb_r[ei], g.cb_r], writes=[g.PSr[PO], g.PSr[PSM]])
                    if par == 1 and j == 4 * G + 3:
                        fin_G(G)
                return emit_S, emit_PV

            def fin_G(G, t=t, hb=hb):
                den, y1 = f1
                kb.op("dve", lambda: nc.vector.tensor_scalar_add(out=den[:], in0=g.PS[PSM][:, :], scalar1=sk[:, 8 + t:9 + t]), reads=[g.PSr[PSM], sk_r], writes=[f1_r[0]])
                kb.op("dve", lambda: nc.vector.tensor_copy(out=y1[:], in_=g.PS[PO][:, :]), reads=[g.PSr[PO]], writes=[f1_r[1]])
                kb.op("dve", lambda: nc.vector.reciprocal(out=den[:], in_=den[:]), reads=[f1_r[0]], writes=[f1_r[0]])
                kb.op("dve", lambda: nc.vector.tensor_tensor(out=y1[:], in0=y1[:], in1=den[:], op=ALU.mult), reads=[f1_r[1], f1_r[0]], writes=[f1_r[1]])
                yi = (t * 4 + G) % 2
                kb.op("dve", lambda: nc.vector.tensor_tensor(out=yb[yi][:], in0=y1[:], in1=gT[hb][:, G * 512:(G + 1) * 512], op=ALU.mult),
                      reads=[f1_r[1], gT_r[hb]], writes=[yb_r[yi]])
                kb.dma("sp", g.ybr[1, t * 128:(t + 1) * 128, G * 512:(G + 1) * 512], yb[yi][:], reads=[yb_r[yi]], writes=[g.ybr_r])

            steps = [make_stepB(G, par, j) for G in range(4) for par in range(2) for j in range(max(4 * G - 1, 0), 4 * G + 4)]
            SKEW = 2
            pend = []
            stride = 4
            for si_, (eS, ePV) in enumerate(steps):
                eS()
                pend.append(ePV)
                if len(pend) > SKEW:
                    pend.pop(0)()
                if nxt and si_ % stride == stride - 1:
                    nxt.pop(0)()
            while pend:
                pend.pop(0)()
            while nxt:
                nxt.pop(0)()


def load_rows_T(g, s, name, rows_aps, nt):
    nc, kb = g.nc, g.kb
    R = len(rows_aps)
    rows = s.enter_context(_sbt(nc, name + "_rows", [R, nt * 128], F32))
    outT = s.enter_context(_sbt(nc, name + "_T", [128, nt, R], F32))
    rows_r, out_r = Res(), Res()
    for r, ap in enumerate(rows_aps):
        kb.dma("sp", rows[r:r + 1, :], ap, writes=[rows_r])
    pi = 0
    ps = g.PS[pi]

    def f():
        ins = None
        for t in range(nt):
            ins = nc.tensor.transpose(ps[:, t * R:(t + 1) * R], rows[0:R, t * 128:(t + 1) * 128], g.cf[0:R, C_ID:C_ID + R])
        return ins
    kb.op("pe", f, reads=[rows_r, g.cf_r], writes=[g.PSr[pi]])
    kb.op("dve", lambda: nc.vector.tensor_copy(out=outT[:].rearrange("p t r -> p (t r)"), in_=ps[:, 0:nt * R]), reads=[g.PSr[pi]], writes=[out_r])
    return outT, out_r


def mixer_D(g, l):
    nc, kb = g.nc, g.kb
    if not hasattr(g, "ybr_r"):
        g.ybr_r = Res()
    with ExitStack() as s:
        def sb(name, shape, dt):
            return s.enter_context(_sbt(nc, name, list(shape), dt))
        rows = [g.conf_conv_w[l, j:j + 1, :] for j in range(31)] + [g.conf_conv_b[l:l + 1, :], g.conf_ln_w[l:l + 1, :], g.conf_ln_b[l:l + 1, :]]
        dp = sb("dp", [128, 8, 34], F32)
        dp_r = Res()
        with ExitStack() as s2:
            dpT, dp_r0 = load_rows_T(g, s2, "dpar", rows, 8)
            kb.op("dve", lambda: nc.vector.tensor_copy(out=dp[:], in_=dpT[:]), reads=[dp_r0], writes=[dp_r])
            kb.barrier()
        wp = WPool(g, s, "wD", 2, 16, 256)
        conv = sb("convD", [128, 8, T], F32)
        conv_r = [Res() for _ in range(4)]
        s3 = ExitStack()

        def sb3(name, shape, dt):
            return s3.enter_context(_sbt(nc, name, list(shape), dt))
        hpad = [sb3(f"hpad{i}", [128, 30 + T], BF16) for i in range(2)]
        hpad_r = [Res(), Res()]
        for i in range(2):
            kb.op("dve", lambda i=i: nc.vector.memset(hpad[i][:, 0:30], 0.0), writes=[hpad_r[i]])
        dgl = [sb3(f"dgl{i}", [128, 31, 128], BF16) for i in range(2)]
        dgl_r = [Res(), Res()]
        sg = [sb3(f"sgD{i}", [128, 512], F32) for i in range(2)]
        sg_r = [Res(), Res()]
        prot = Rot([0, 1, 2, 3])
        crot = Rot([4, 5, 6, 7])
        for t in range(8):
            b = t % 2
            wbuf, wres = wp.load([(win(g, l, OFF_DGLU + t * 128, 128), 128), (win(g, l, OFF_DGLU + 1024 + t * 128, 128), 128)])
            for j in range(31):
                kb.op("dve", lambda j=j, t=t, b=b: nc.vector.tensor_scalar_mul(out=dgl[b][:, j, :], in0=g.cf[:, C_ID:C_ID + 128], scalar1=dp[:, t, j:j + 1]),
                      reads=[g.cf_r, dp_r], writes=[dgl_r[b]])
            for tg in range(4):
                pv_, pg_ = prot.next(), prot.next()
                proj_fm(g, wbuf, wres, 0, tg, pv_)
                proj_fm(g, wbuf, wres, 128, tg, pg_)
                si = (t * 4 + tg) % 2
                kb.op("act", lambda si=si, pg_=pg_: nc.scalar.activation(out=sg[si][:], in_=g.PS[pg_][:, :], func=AF.Sigmoid), reads=[g.PSr[pg_]], writes=[sg_r[si]])
                kb.op("dve", lambda si=si, pv_=pv_, b=b, tg=tg: nc.vector.tensor_tensor(
                    out=hpad[b][:, 30 + tg * 512:30 + (tg + 1) * 512], in0=g.PS[pv_][:, :], in1=sg[si][:], op=ALU.mult),
                    reads=[g.PSr[pv_], sg_r[si]], writes=[hpad_r[b]])
            for tg in range(4):
                ci = crot.next()

                def cmm(ci=ci, b=b, tg=tg):
                    ins = None
                    for j in range(31):
                        ins = nc.tensor.matmul(g.PS[ci][:, :], dgl[b][:, j, :], hpad[b][:, tg * 512 + j:tg * 512 + j + 512], start=(j == 0), stop=(j == 30))
                    return ins
                kb.op("pe", cmm, reads=[dgl_r[b], hpad_r[b]], writes=[g.PSr[ci]])
                kb.op("act", lambda ci=ci, t=t, tg=tg: nc.scalar.activation(out=conv[:, t, tg * 512:(tg + 1) * 512], in_=g.PS[ci][:, :], func=AF.Identity,
                                                                           bias=dp[:, t, 31:32]), reads=[g.PSr[ci], dp_r], writes=[conv_r[tg]])
        kb.barrier()
        s3.close()
        sq = [sb(f"sqD{i}", [128, 512], F32) for i in range(2)]
        sq_r = [Res(), Res()]
        mu = sb("muD", [128, 512], F32)
        rs = sb("rsD", [128, 512], F32)
        tmp = [sb(f"tmD{i}", [128, 512], F32) for i in range(2)]
        tmp_r = [Res(), Res()]
        gD = [sb(f"gD{i}", [128, 512], F32) for i in range(2)]
        gD_r = [Res(), Res()]
        mu_r, rs_r = Res(), Res()
        yb = [sb(f"ybD{i}", [128, 512], BF16) for i in range(2)]
        yb_r = [Res(), Res()]
        onesf = g.cf[:, C_ONES:C_ONES + 128]
        wg = [None] * 8
        for tg in range(4):
            pA, pB = 0, 1

            def amm(tg=tg):
                ins = None
                for t in range(8):
                    ins = nc.tensor.matmul(g.PS[pA][:, :], onesf, conv[:, t, tg * 512:(tg + 1) * 512], start=(t == 0), stop=(t == 7))
                return ins
            kb.op("pe", amm, reads=[conv_r[tg], g.cf_r], writes=[g.PSr[pA]])
            for t in range(8):
                si = t % 2
                kb.op("act", lambda si=si, t=t, tg=tg: nc.scalar.activation(out=sq[si][:], in_=conv[:, t, tg * 512:(tg + 1) * 512], func=AF.Square),
                      reads=[conv_r[tg]], writes=[sq_r[si]])
                kb.op("pe", lambda si=si, t=t: nc.tensor.matmul(g.PS[pB][:, :], onesf, sq[si][:], start=(t == 0), stop=(t == 7)),
                      reads=[sq_r[si], g.cf_r], writes=[g.PSr[pB]])
            kb.op("act", lambda: nc.scalar.mul(out=mu[:], in_=g.PS[pA][:, :], mul=1.0 / 1024), reads=[g.PSr[pA]], writes=[mu_r])
            kb.op("dve", lambda: nc.vector.tensor_tensor(out=rs[:], in0=mu[:], in1=mu[:], op=ALU.mult), reads=[mu_r], writes=[rs_r])
            kb.op("dve", lambda: nc.vector.scalar_tensor_tensor(out=rs[:], in0=g.PS[pB][:, :], scalar=g.kc[:, 3:4], in1=rs[:], op0=ALU.mult, op1=ALU.subtract),
                  reads=[g.PSr[pB], g.kc_r, rs_r], writes=[rs_r])
            kb.op("act", lambda: nc.scalar.activation(out=rs[:], in_=rs[:], func=AF.Sqrt, bias=g.kc[:, 2:3]), reads=[rs_r, g.kc_r], writes=[rs_r])
            kb.op("dve", lambda: nc.vector.reciprocal(out=rs[:], in_=rs[:]), reads=[rs_r], writes=[rs_r])
            for t in range(8):
                if t % 2 == 0:
                    wbuf, wres = wp.load([(win(g, l, OFF_DG + t * 128, 256), 256)])
                pg_ = 2 + (t % 4)
                proj_fm(g, wbuf, wres, (t % 2) * 128, tg, pg_)
                si = t % 2
                kb.op("act", lambda si=si, pg_=pg_: nc.scalar.activation(out=gD[si][:], in_=g.PS[pg_][:, :], func=AF.Silu), reads=[g.PSr[pg_]], writes=[gD_r[si]])
                kb.op("dve", lambda si=si, t=t, tg=tg: nc.vector.tensor_tensor(out=tmp[si][:], in0=conv[:, t, tg * 512:(tg + 1) * 512], in1=mu[:], op=ALU.subtract),
                      reads=[conv_r[tg], mu_r], writes=[tmp_r[si]])
                kb.op("dve", lambda si=si: nc.vector.tensor_tensor(out=tmp[si][:], in0=tmp[si][:], in1=rs[:], op=ALU.mult), reads=[tmp_r[si], rs_r], writes=[tmp_r[si]])
                kb.op("act", lambda si=si, t=t: nc.scalar.activation(out=tmp[si][:], in_=tmp[si][:], func=AF.Silu, scale=dp[:, t, 32:33], bias=dp[:, t, 33:34]),
                      reads=[tmp_r[si], dp_r], writes=[tmp_r[si]])
                kb.op("dve", lambda si=si: nc.vector.tensor_tensor(out=yb[si][:], in0=tmp[si][:], in1=gD[si][:], op=ALU.mult), reads=[tmp_r[si], gD_r[si]], writes=[yb_r[si]])
                kb.dma("sp", g.ybr[3, t * 128:(t + 1) * 128, tg * 512:(tg + 1) * 512], yb[si][:], reads=[yb_r[si]], writes=[g.ybr_r])
        if g.mgjobs:
            mg_host(g, s, wp, Rot([6, 7]))
            mg_run(g, 10 ** 6)


def mixer_C(g, l):
    nc, kb = g.nc, g.kb
    if not hasattr(g, "ybr_r"):
        g.ybr_r = Res()
    with ExitStack() as s:
        def sb(name, shape, dt):
            return s.enter_context(_sbt(nc, name, list(shape), dt))
        cp = sb("cp", [128, 12, 5], F32)
        cp_r = Res()
        with ExitStack() as s2:
            rows = [g.ssd_conv_w[l, j:j + 1, :] for j in range(4)] + [g.ssd_conv_b[l:l + 1, :]]
            cpT, cp_r0 = load_rows_T(g, s2, "cpar", rows, 12)
            kb.op("dve", lambda: nc.vector.tensor_copy(out=cp[:], in_=cpT[:]), reads=[cp_r0], writes=[cp_r])
            kb.barrier()
        pc = sb("pcC", [128, 64], F32)
        pc_r = Res()
        kb.dma("sp", pc[:, 0:16], g.ssd_dt_bias[l:l + 1, :].partition_broadcast(128), writes=[pc_r])
        kb.dma("sp", pc[:, 16:32], g.ssd_a_log[l:l + 1, :].partition_broadcast(128), writes=[pc_r])
        for par in range(2):
            src = bass.AP(g.ssd_d.tensor, l * 16 + par, [[0, 64], [2, 8]])
            kb.dma("sp", pc[par * 64:(par + 1) * 64, 32:40], src, writes=[pc_r], allow_slow_non_contiguous=True)
        kb.dma("sp", pc[:, 40:48], g.ssd_norm_w[l:l + 1, :].rearrange("o (t p) -> p (o t)", p=128), writes=[pc_r], allow_slow_non_contiguous=True)
        kb.op("act", lambda: nc.scalar.activation(out=pc[:, 16:32], in_=pc[:, 16:32], func=AF.Exp), reads=[pc_r], writes=[pc_r])
        kb.op("dve", lambda: nc.vector.tensor_scalar_mul(out=pc[:, 16:32], in0=pc[:, 16:32], scalar1=-1.0), reads=[pc_r], writes=[pc_r])
        blk = sb("blkC", [16, 2048], F32)
        blk_r = Res()
        kb.dma("sp", blk[:], g.cblk, writes=[blk_r])
        wp = WPool(g, s, "wC", 2, 16, 256)
        prot = Rot([0, 1])
        xbc = sb("xbc", [128, 12, T], BF16)
        xbc_r = [Res() for _ in range(12)]
        dtm = sb("dtm", [128, 256], F32)
        acol = sb("acol", [128, 256], F32)
        dst_ = sb("dstm", [128, 256], F32)
        cdb = sb("cdb", [128, 256], F32)
        dt_r, acol_r, dst_r, cdb_r = Res(), Res(), Res(), Res()
        with ExitStack() as s3:
            def sb3(name, shape, dt):
                return s3.enter_context(_sbt(nc, name, list(shape), dt))
            rawp = [sb3(f"rawp{i}", [128, 3 + T], BF16) for i in range(2)]
            rawp_r = [Res(), Res()]
            for i in range(2):
                kb.op("dve", lambda i=i: nc.vector.memset(rawp[i][:, 0:3], 0.0), writes=[rawp_r[i]])
            dg4 = [sb3(f"dg4{i}", [128, 4, 128], BF16) for i in range(2)]
            dg4_r = [Res(), Res()]
            dAm = sb3("dAm", [128, 256], F32)
            dA_r = Res()
            for ti in range(12):
                b = ti % 2
                if ti % 2 == 0:
                    wbuf, wres = wp.load([(win(g, l, OFF_CX + ti * 128, 256), 256)])
                for j in range(4):
                    kb.op("dve", lambda j=j, ti=ti, b=b: nc.vector.tensor_scalar_mul(out=dg4[b][:, j, :], in0=g.cf[:, C_ID:C_ID + 128], scalar1=cp[:, ti, j:j + 1]),
                          reads=[g.cf_r, cp_r], writes=[dg4_r[b]])
                for tg in range(4):
                    pi = prot.next()
                    proj_fm(g, wbuf, wres, (ti % 2) * 128, tg, pi)
                    kb.op("act", lambda pi=pi, b=b, tg=tg: nc.scalar.copy(out=rawp[b][:, 3 + tg * 512:3 + (tg + 1) * 512], in_=g.PS[pi][:, :]),
                          reads=[g.PSr[pi]], writes=[rawp_r[b]])
                for tg in range(4):
                    ci = 2 + (ti * 4 + tg) % 2

                    def cmm(ci=ci, b=b, tg=tg):
                        ins = None
                        for j in range(4):
                            ins = nc.tensor.matmul(g.PS[ci][:, :], dg4[b][:, j, :], rawp[b][:, tg * 512 + j:tg * 512 + j + 512], start=(j == 0), stop=(j == 3))
                        return ins
                    kb.op("pe", cmm, reads=[dg4_r[b], rawp_r[b]], writes=[g.PSr[ci]])
                    kb.op("act", lambda ci=ci, ti=ti, tg=tg: nc.scalar.activation(out=xbc[:, ti, tg * 512:(tg + 1) * 512], in_=g.PS[ci][:, :], func=AF.Silu,
                                                                                bias=cp[:, ti, 4:5]), reads=[g.PSr[ci], cp_r], writes=[xbc_r[ti]])
            wbuf, wres = wp.load([(win(g, l, OFF_CDT, 16), 16)])
            pdt = 4

            def dtmm():
                ins = None
                for tt in range(16):
                    for kt in range(16):
                        ins = nc.tensor.matmul(g.PS[pdt][:, tt * 16:(tt + 1) * 16], g.hT[:, kt, tt * 128:(tt + 1) * 128], wbuf[:, kt, 0:16], start=(kt == 0), stop=(kt == 15))
                return ins
            kb.op("pe", dtmm, reads=[wres] + g.hT_r, writes=[g.PSr[pdt]])
            kb.op("dve", lambda: nc.vector.tensor_tensor(out=dtm[:].rearrange("p (c e) -> p c e", e=16), in0=g.PS[pdt][:, 0:256].rearrange("p (c e) -> p c e", e=16),
                                                         in1=pc[:, 0:16].unsqueeze(1).to_broadcast([128, 16, 16]), op=ALU.add), reads=[g.PSr[pdt], pc_r], writes=[dt_r])
            kb.op("act", lambda: nc.scalar.activation(out=dtm[:], in_=dtm[:], func=AF.Exp), reads=[dt_r], writes=[dt_r])
            kb.op("act", lambda: nc.scalar.activation(out=dtm[:], in_=dtm[:], func=AF.Ln, bias=1.0), reads=[dt_r], writes=[dt_r])
            kb.op("dve", lambda: nc.vector.tensor_tensor(out=dAm[:].rearrange("p (c e) -> p c e", e=16), in0=dtm[:].rearrange("p (c e) -> p c e", e=16),
                                                         in1=pc[:, 16:32].unsqueeze(1).to_broadcast([128, 16, 16]), op=ALU.mult), reads=[dt_r, pc_r], writes=[dA_r])
            for (cst, dstt, dres, doexp) in ((C_TRI, acol, acol_r, False), (C_TRIU, dst_, dst_r, True), (C_ONES, cdb, cdb_r, True)):
                pi = prot.next()
                kb.op("pe", lambda pi=pi, cst=cst: nc.tensor.matmul(g.PS[pi][:, 0:256], g.cf[:, cst:cst + 128], dAm[:], start=True, stop=True),
                      reads=[dA_r, g.cf_r], writes=[g.PSr[pi]])
                if doexp:
                    kb.op("act", lambda pi=pi, dstt=dstt: nc.scalar.activation(out=dstt[:], in_=g.PS[pi][:, 0:256], func=AF.Exp), reads=[g.PSr[pi]], writes=[dres])
                else:
                    kb.op("dve", lambda pi=pi, dstt=dstt: nc.vector.tensor_copy(out=dstt[:], in_=g.PS[pi][:, 0:256]), reads=[g.PSr[pi]], writes=[dres])
            kb.barrier()
        hst = sb("hst", [128, 1024], F32)
        prevb = sb("prevb", [128, 1024], BF16)
        hst_r, prevb_r = Res(), Res()
        acTc = [sb(f"acTc{i}", [16, 128], F32) for i in range(2)]
        acTc_r = [Res(), Res()]
        tsegs = [sb(f"tseg{i}", [128, 512], F32) for i in range(2)]
        tsegs_r = [Res(), Res()]
        Ed = [sb(f"Ed{i}", [128, 512], BF16) for i in range(2)]
        Ed_r = [Res(), Res()]
        Ea = [sb(f"Ea{i}", [128, 512], BF16) for i in range(2)]
        Ea_r = [Res(), Res()]
        MT = [sb(f"MT{i}", [128, 4, 128], BF16) for i in range(2)]
        MT_r = [Res(), Res()]
        CsT = [sb(f"CsT{i}", [128, 4, 128], BF16) for i in range(2)]
        CsT_r = [Res(), Res()]
        cbm = [sb(f"cbm{i}", [128, 2, 128], BF16) for i in range(2)]
        cbm_r = [Res(), Res()]
        xdt = sb("xdt", [128, 1024], BF16)
        xdt_r = Res()
        xdd = sb("xdd", [128, 1024], BF16)
        xdd_r = Res()
        Btm = sb("Btm", [128, 256], BF16)
        Btm_r = Res()
        siz = sb("siz", [128, 8, 512], BF16)
        siz_r = Res()
        yg = sb("ygC", [128, 8, 128], F32)
        yg_r = Res()
        sqc = sb("sqC", [128, 8, 128], BF16)
        sqc_r = Res()
        rsc = sb("rsC", [128, 2, 128], F32)
        rsc_r = Res()
        ycb = [sb(f"ycb{i}", [128, 8, 128], BF16) for i in range(2)]
        ycb_r = [Res(), Res()]
        tmpc = sb("tmpC", [128, 1024], F32)
        tmpc_r = Res()
        mg_host(g, s, wp, prot)
        for c in range(16):
            cb2 = c % 2
            cs = slice(c * 128, (c + 1) * 128)
            if c % 4 == 0:
                tg = c // 4
                for t in range(8):
                    if t % 2 == 0:
                        wbuf, wres = wp.load([(win(g, l, OFF_CZ + t * 128, 256), 256)])
                    pi = prot.next()
                    proj_fm(g, wbuf, wres, (t % 2) * 128, tg, pi)
                    kb.op("act", lambda pi=pi, t=t: nc.scalar.activation(out=siz[:, t, :], in_=g.PS[pi][:, :], func=AF.Silu), reads=[g.PSr[pi]], writes=[siz_r])
            pT = prot.next()
            kb.op("pe", lambda pT=pT, c=c: nc.tensor.transpose(g.PS[pT][0:16, 0:128], acol[:, c * 16:(c + 1) * 16], g.cf[:, C_ID:C_ID + 128]),
                  reads=[acol_r, g.cf_r], writes=[g.PSr[pT]])
            kb.op("dve", lambda pT=pT, cb2=cb2: nc.vector.tensor_copy(out=acTc[cb2][:], in_=g.PS[pT][0:16, 0:128]), reads=[g.PSr[pT]], writes=[acTc_r[cb2]])
            pX = 2
            pXb = g.PS[pX][:].bitcast(BF16)

            def trx(cs=cs, pXb=pXb):
                ins = None
                for t in range(8):
                    ins = nc.tensor.transpose(pXb[:, t * 128:(t + 1) * 128], xbc[:, t, cs], g.idb[:])
                return ins
            kb.op("pe", trx, reads=xbc_r[0:8] + [g.cb_r], writes=[g.PSr[pX]])
            kb.op("dve", lambda c=c, pXb=pXb: nc.vector.tensor_tensor(
                out=xdt[:].rearrange("p (e d) -> p e d", d=64), in0=pXb[:, 0:1024].rearrange("p (e d) -> p e d", d=64),
                in1=dtm[:, c * 16:(c + 1) * 16].unsqueeze(2).to_broadcast([128, 16, 64]), op=ALU.mult), reads=[g.PSr[pX], dt_r], writes=[xdt_r])
            kb.op("dve", lambda c=c: nc.vector.tensor_tensor(
                out=xdd[:].rearrange("p (e d) -> p e d", d=64), in0=xdt[:].rearrange("p (e d) -> p e d", d=64),
                in1=dst_[:, c * 16:(c + 1) * 16].unsqueeze(2).to_broadcast([128, 16, 64]), op=ALU.mult), reads=[xdt_r, dst_r], writes=[xdd_r])
            pB = 3
            pBb = g.PS[pB][:].bitcast(BF16)

            def trb(cs=cs, pBb=pBb):
                nc.tensor.transpose(pBb[:, 0:128], xbc[:, 8, cs], g.idb[:])
                return nc.tensor.transpose(pBb[:, 128:256], xbc[:, 9, cs], g.idb[:])
            kb.op("pe", trb, reads=[xbc_r[8], xbc_r[9], g.cb_r], writes=[g.PSr[pB]])
            kb.op("act", lambda pBb=pBb: nc.scalar.copy(out=Btm[:], in_=pBb[:, 0:256]), reads=[g.PSr[pB]], writes=[Btm_r])
            pCB = 3

            def cbmm(cs=cs):
                nc.tensor.matmul(g.PS[pCB][:, 256:384], xbc[:, 8, cs], xbc[:, 10, cs], start=True, stop=True)
                return nc.tensor.matmul(g.PS[pCB][:, 384:512], xbc[:, 9, cs], xbc[:, 11, cs], start=True, stop=True)
            kb.op("pe", cbmm, reads=xbc_r[8:12], writes=[g.PSr[pCB]])
            kb.op("dve", lambda cb2=cb2: nc.vector.tensor_tensor(out=cbm[cb2][:], in0=g.PS[pCB][:, 256:512].rearrange("p (g l) -> p g l", g=2),
                                                                in1=g.trib[:].unsqueeze(1).to_broadcast([128, 2, 128]), op=ALU.mult),
                  reads=[g.PSr[pCB], g.cb_r], writes=[cbm_r[cb2]])
            pY = [6, 7]
            def emit_bcm(hq, cb2=cb2):
                pa = 4 + hq % 2

                def bcm():
                    ins = None
                    for e4 in range(4):
                        e = hq * 4 + e4
                        ins = nc.tensor.matmul(g.PS[pa][:, e4 * 128:(e4 + 1) * 128], blk[0:16, e * 128:(e + 1) * 128], acTc[cb2][:, :], start=True, stop=True)
                    return ins
                kb.op("pe", bcm, reads=[acTc_r[cb2], blk_r], writes=[g.PSr[pa]])
            emit_bcm(0)
            for hq in range(4):
                pa = 4 + hq % 2
                if hq < 3:
                    emit_bcm(hq + 1)
                b2 = hq % 2
                tseg, tseg_r = tsegs[b2], tsegs_r[b2]
                for e4 in range(4):
                    e = hq * 4 + e4
                    kb.op("dve", lambda pa=pa, e4=e4, e=e, c=c, tseg=tseg: nc.vector.tensor_scalar(
                        out=tseg[:, e4 * 128:(e4 + 1) * 128], in0=g.PS[pa][:, e4 * 128:(e4 + 1) * 128], scalar1=acol[:, c * 16 + e:c * 16 + e + 1],
                        scalar2=g.kc[:, 0:1], op0=ALU.subtract, op1=ALU.min), reads=[g.PSr[pa], acol_r, g.kc_r], writes=[tseg_r])
                kb.op("act", lambda b2=b2, tseg=tseg: nc.scalar.activation(out=Ed[b2][:], in_=tseg[:], func=AF.Exp), reads=[tseg_r], writes=[Ed_r[b2]])
                kb.op("act", lambda b2=b2, pa=pa: nc.scalar.activation(out=Ea[b2][:], in_=g.PS[pa][:, :], func=AF.Exp), reads=[g.PSr[pa]], writes=[Ea_r[b2]])
                grp = hq // 2
                kb.op("dve", lambda b2=b2, cb2=cb2, grp=grp: nc.vector.tensor_tensor(out=MT[b2][:], in0=Ed[b2][:].rearrange("p (e l) -> p e l", e=4),
                                                                                    in1=cbm[cb2][:, grp, :].unsqueeze(1).to_broadcast([128, 4, 128]), op=ALU.mult),
                      reads=[Ed_r[b2], cbm_r[cb2]], writes=[MT_r[b2]])
                kb.op("dve", lambda b2=b2, grp=grp, cs=cs: nc.vector.tensor_tensor(out=CsT[b2][:], in0=Ea[b2][:].rearrange("p (e l) -> p e l", e=4),
                                                                                  in1=xbc[:, 10 + grp, cs].unsqueeze(1).to_broadcast([128, 4, 128]), op=ALU.mult),
                      reads=[Ea_r[b2], xbc_r[10 + grp]], writes=[CsT_r[b2]])
                py = pY[hq // 2]
                mg_run(g, 4)

                def ymm(py=py, b2=b2, hq=hq, c=c):
                    ins = None
                    for e4 in range(4):
                        e = hq * 4 + e4
                        lo = (e % 2) * 64
                        cc = ((e // 2) % 4) * 128
                        ins = nc.tensor.matmul(g.PS[py][lo:lo + 64, cc:cc + 128], xdt[:, e * 64:(e + 1) * 64], MT[b2][:, e4, :], start=True, stop=(c == 0))
                        if c > 0:
                            ins = nc.tensor.matmul(g.PS[py][lo:lo + 64, cc:cc + 128], prevb[:, e * 64:(e + 1) * 64], CsT[b2][:, e4, :], start=False, stop=True)
                    return ins
                kb.op("pe", ymm, reads=[xdt_r, MT_r[b2], prevb_r, CsT_r[b2]], writes=[g.PSr[py]])
            kb.op("dve", lambda cs=cs: nc.vector.tensor_tensor(out=tmpc[:].rearrange("p (t l) -> p t l", t=8), in0=xbc[:, 0:8, cs],
                                                              in1=pc[:, 32:40].unsqueeze(2).to_broadcast([128, 8, 128]), op=ALU.mult),
                  reads=xbc_r[0:8] + [pc_r], writes=[tmpc_r])
            for gi in range(2):
                kb.op("dve", lambda gi=gi: nc.vector.tensor_tensor(out=yg[:, gi * 4:(gi + 1) * 4, :].rearrange("p t l -> p (t l)"), in0=g.PS[pY[gi]][:, :],
                                                                  in1=tmpc[:, gi * 512:(gi + 1) * 512], op=ALU.add), reads=[g.PSr[pY[gi]], tmpc_r], writes=[yg_r])
            zc = slice((c % 4) * 128, (c % 4 + 1) * 128)
            kb.op("dve", lambda zc=zc: nc.vector.tensor_tensor(out=yg[:], in0=yg[:], in1=siz[:, :, zc], op=ALU.mult), reads=[yg_r, siz_r], writes=[yg_r])
            kb.op("act", lambda: nc.scalar.activation(out=sqc[:], in_=yg[:], func=AF.Square), reads=[yg_r], writes=[sqc_r])
            pR = 2

            def rmm():
                ins = None
                for gi in range(2):
                    for t4 in range(4):
                        ins = nc.tensor.matmul(g.PS[pR][:, gi * 128:(gi + 1) * 128], g.onesb[:], sqc[:, gi * 4 + t4, :], start=(t4 == 0), stop=(t4 == 3))
                return ins
            kb.op("pe", rmm, reads=[sqc_r, g.cb_r], writes=[g.PSr[pR]])
            kb.op("act", lambda: nc.scalar.activation(out=rsc[:].rearrange("p g l -> p (g l)"), in_=g.PS[pR][:, 0:256], func=AF.Sqrt, scale=1.0 / 512, bias=g.kc[:, 2:3]),
                  reads=[g.PSr[pR], g.kc_r], writes=[rsc_r])
            kb.op("dve", lambda: nc.vector.reciprocal(out=rsc[:], in_=rsc[:]), reads=[rsc_r], writes=[rsc_r])
            for gi in range(2):
                kb.op("dve", lambda gi=gi: nc.vector.tensor_tensor(out=yg[:, gi * 4:(gi + 1) * 4, :], in0=yg[:, gi * 4:(gi + 1) * 4, :],
                                                                  in1=rsc[:, gi, :].unsqueeze(1).to_broadcast([128, 4, 128]), op=ALU.mult), reads=[yg_r, rsc_r], writes=[yg_r])
            kb.op("dve", lambda cb2=cb2: nc.vector.tensor_tensor(out=ycb[cb2][:], in0=yg[:], in1=pc[:, 40:48].unsqueeze(2).to_broadcast([128, 8, 128]), op=ALU.mult),
                  reads=[yg_r, pc_r], writes=[ycb_r[cb2]])
            kb.dma("sp", g.ybr[2].rearrange("(t p) l -> p t l", p=128)[:, :, cs], ycb[cb2][:], reads=[ycb_r[cb2]], writes=[g.ybr_r])
            if c < 15:
                pS2 = [4, 5]
                for gi in range(2):
                    kb.op("pe", lambda gi=gi: nc.tensor.matmul(g.PS[pS2[gi]][:, :], Btm[:, gi * 128:(gi + 1) * 128], xdd[:, gi * 512:(gi + 1) * 512], start=True, stop=True),
                          reads=[Btm_r, xdd_r], writes=[g.PSr[pS2[gi]]])
                if c == 0:
                    for gi in range(2):
                        kb.op("dve", lambda gi=gi: nc.vector.tensor_copy(out=hst[:, gi * 512:(gi + 1) * 512], in_=g.PS[pS2[gi]][:, :]), reads=[g.PSr[pS2[gi]]], writes=[hst_r])
                else:
                    kb.op("dve", lambda c=c: nc.vector.tensor_tensor(out=hst[:].rearrange("p (e d) -> p e d", d=64), in0=hst[:].rearrange("p (e d) -> p e d", d=64),
                                                                    in1=cdb[:, c * 16:(c + 1) * 16].unsqueeze(2).to_broadcast([128, 16, 64]), op=ALU.mult),
                          reads=[hst_r, cdb_r], writes=[hst_r])
                    for gi in range(2):
                        kb.op("dve", lambda gi=gi: nc.vector.tensor_tensor(out=hst[:, gi * 512:(gi + 1) * 512], in0=hst[:, gi * 512:(gi + 1) * 512], in1=g.PS[pS2[gi]][:, :], op=ALU.add),
                              reads=[hst_r, g.PSr[pS2[gi]]], writes=[hst_r])
                kb.op("act", lambda: nc.scalar.copy(out=prevb[:], in_=hst[:]), reads=[hst_r], writes=[prevb_r])


def phase3(g, l, src, dst):
    nc, kb = g.nc, g.kb
    g.xres_r_new = Res()
    with ExitStack() as s:
        def sb(name, shape, dt):
            return s.enter_context(_sbt(nc, name, list(shape), dt))
        yh = sb("yh", [128, 32, 1024], BF16)
        mT = sb("mT", [128, 16, 1024], BF16)
        yh_r = Res()
        mT_r = [Res(), Res()]
        wbr = WPool(g, s, "wbr", 3, 8, 128)
        wo = WPool(g, s, "wo", 2, 16, 256)
        acc = [sb(f"acc{i}", [128, 512], F32) for i in range(2)]
        acc_r = [Res(), Res()]
        NG = 6
        gt = [sb(f"gt3{i}", [128, 512], BF16) for i in range(NG)]
        gt_r = [Res() for _ in range(NG)]
        pr = [sb(f"pr3{i}", [128, 512], F32) for i in range(2)]
        pr_r = [Res(), Res()]
        xt = [sb(f"xt3{i}", [128, 256], F32) for i in range(4)]
        xt_r = [Res() for _ in range(4)]
        brot = Rot([0, 1, 2, 3, 4, 5])
        orot = Rot([6, 7])
        k = 0
        for hf in range(2):
            ts = slice(hf * 1024, (hf + 1) * 1024)
            for i in range(4):
                kb.dma("sp", yh[:, i * 8:(i + 1) * 8, :], g.ybr[i].rearrange("(t p) l -> p t l", p=128)[:, :, ts], reads=[g.ybr_r], writes=[yh_r])
            for dt_ in range(16):
                for i in range(4):
                    wb_, wb_r = wbr.load([(g.w_branch[l, i, :, dt_ * 128:(dt_ + 1) * 128], 128)])
                    for tg in range(2):
                        pb = brot.next()
                        k += 1
                        gi = k % NG
                        si = k % 2
                        r0 = i * 2048 + dt_ * 128
                        c0 = hf * 1024 + tg * 512
                        kb.dma("sp", gt[gi][:], g.sgd[r0:r0 + 128, c0:c0 + 512], reads=[g.sgd_r], writes=[gt_r[gi]])

                        def bmm(pb=pb, wb_=wb_, tg=tg, i=i):
                            ins = None
                            for rt in range(8):
                                ins = nc.tensor.matmul(g.PS[pb][:, :], wb_[:, rt, :], yh[:, i * 8 + rt, tg * 512:(tg + 1) * 512], start=(rt == 0), stop=(rt == 7))
                            return ins
                        kb.op("pe", bmm, reads=[wb_r, yh_r], writes=[g.PSr[pb]])
                        if i == 0:
                            kb.op("dve", lambda gi=gi, pb=pb, tg=tg: nc.vector.tensor_tensor(out=acc[tg][:], in0=g.PS[pb][:, :], in1=gt[gi][:], op=ALU.mult),
                                  reads=[g.PSr[pb], gt_r[gi]], writes=[acc_r[tg]])
                        else:
                            kb.op("dve", lambda gi=gi, si=si, pb=pb: nc.vector.tensor_tensor(out=pr[si][:], in0=g.PS[pb][:, :], in1=gt[gi][:], op=ALU.mult),
                                  reads=[g.PSr[pb], gt_r[gi]], writes=[pr_r[si]])
                            if i < 3:
                                kb.op("dve", lambda si=si, tg=tg: nc.vector.tensor_tensor(out=acc[tg][:], in0=acc[tg][:], in1=pr[si][:], op=ALU.add),
                                      reads=[acc_r[tg], pr_r[si]], writes=[acc_r[tg]])
                            else:
                                kb.op("dve", lambda si=si, tg=tg, dt_=dt_: nc.vector.tensor_tensor(out=mT[:, dt_, tg * 512:(tg + 1) * 512], in0=acc[tg][:], in1=pr[si][:], op=ALU.add),
                                      reads=[acc_r[tg], pr_r[si]], writes=[mT_r[tg]])
            iters = [(eg, tt) for eg in range(8) for tt in range(8)]

            def load_x(k):
                eg, tt = iters[k]
                xi = k % 4
                tok0 = hf * 1024 + tt * 128
                kb.dma("sp", xt[xi][:], src[tok0:tok0 + 128, eg * 256:(eg + 1) * 256], reads=[getattr(g, "xres_r", Res())], writes=[xt_r[xi]])
            load_x(0)
            load_x(1)
            for k, (eg, tt) in enumerate(iters):
                if tt == 0:
                    wo_, wo_r = wo.load([(g.w_out[l, :, eg * 256:(eg + 1) * 256], 256)])
                if k + 2 < len(iters):
                    load_x(k + 2)
                po = orot.next()
                tok0 = hf * 1024 + tt * 128
                xi = k % 4

                def omm(po=po, wo_=wo_, tt=tt):
                    ins = None
                    for dt_ in range(16):
                        ins = nc.tensor.matmul(g.PS[po][:, 0:256], mT[:, dt_, tt * 128:(tt + 1) * 128], wo_[:, dt_, :], start=(dt_ == 0), stop=(dt_ == 15))
                    return ins
                kb.op("pe", omm, reads=[wo_r, mT_r[tt // 4]], writes=[g.PSr[po]])
                kb.op("dve", lambda xi=xi, po=po: nc.vector.tensor_tensor(out=xt[xi][:], in0=xt[xi][:], in1=g.PS[po][:, 0:256], op=ALU.add),
                      reads=[xt_r[xi], g.PSr[po]], writes=[xt_r[xi]])
                kb.dma("sp", dst[tok0:tok0 + 128, eg * 256:(eg + 1) * 256], xt[xi][:], reads=[xt_r[xi]], writes=[g.xres_r_new])
    g.xres_r = g.xres_r_new


def phase_final(g, src):
    nc, kb = g.nc, g.kb
    with ExitStack() as s:
        wbc = s.enter_context(_sbt(nc, "wbcF", [128, D], F32))
        xt = [s.enter_context(_sbt(nc, f"xtF{i}", [128, D], F32)) for i in range(2)]
        ot = [s.enter_context(_sbt(nc, f"otF{i}", [128, D], F32)) for i in range(2)]
        junk = s.enter_context(_sbt(nc, "junkF", [128, D], BF16))
        st = [s.enter_context(_sbt(nc, f"stF{i}", [128, 4], F32)) for i in range(2)]
        wbc_r, junk_r = Res(), Res()
        xt_r, ot_r, st_r = [Res(), Res()], [Res(), Res()], [Res(), Res()]
        y_r = Res()
        kb.dma("sp", wbc[:], g.final_norm_w[0:1, :].partition_broadcast(128), writes=[wbc_r])
        for t in range(16):
            b = t % 2
            kb.dma("sp", xt[b][:], src[t * 128:(t + 1) * 128, :], reads=[g.xres_r], writes=[xt_r[b]])
            kb.op("act", lambda b=b: nc.scalar.activation(out=junk[:], in_=xt[b][:], func=AF.Square, accum_out=st[b][:, 0:1]), reads=[xt_r[b]], writes=[junk_r, st_r[b]])
            kb.op("act", lambda b=b: nc.scalar.activation(out=st[b][:, 1:2], in_=st[b][:, 0:1], func=AF.Sqrt, scale=1.0 / D, bias=g.kc[:, 2:3]),
                  reads=[st_r[b], g.kc_r], writes=[st_r[b]])
            kb.op("dve", lambda b=b: nc.vector.reciprocal(out=st[b][:, 2:3], in_=st[b][:, 1:2]), reads=[st_r[b]], writes=[st_r[b]])
            kb.op("dve", lambda b=b: nc.vector.scalar_tensor_tensor(out=ot[b][:], in0=xt[b][:], scalar=st[b][:, 2:3], in1=wbc[:], op0=ALU.mult, op1=ALU.mult),
                  reads=[xt_r[b], st_r[b], wbc_r], writes=[ot_r[b]])
            kb.dma("sp", g.y[t * 128:(t + 1) * 128, :], ot[b][:], reads=[ot_r[b]], writes=[y_r])


_CACHE = {}


def kernel(**inputs):
    dbg = inputs.pop("_dbg", None)
    key = tuple(sorted(dbg)) if dbg else None
    if key not in _CACHE:
        _CACHE[key] = build(dbg)
    nc, g = _CACHE[key]
    consts, cblk = make_consts()
    x = np.ascontiguousarray(inputs["x"], dtype=np.float32)
    shared = {
        "norm_w": inputs["norm_w"], "w_in": inputs["w_in"],
        "diff_lambda": np.asarray(inputs["diff_lambda"]).reshape(DEPTH, 256),
        "diff_subln_w": inputs["diff_subln_w"], "swa_sinks": inputs["swa_sinks"],
        "ssd_conv_w": inputs["ssd_conv_w"], "ssd_conv_b": inputs["ssd_conv_b"], "ssd_dt_bias": inputs["ssd_dt_bias"],
        "ssd_a_log": inputs["ssd_a_log"], "ssd_d": inputs["ssd_d"], "ssd_norm_w": inputs["ssd_norm_w"],
        "conf_conv_w": inputs["conf_conv_w"], "conf_conv_b": inputs["conf_conv_b"], "conf_ln_w": inputs["conf_ln_w"],
        "conf_ln_b": inputs["conf_ln_b"], "w_branch": inputs["w_branch"], "w_out": inputs["w_out"],
        "rel_bias": inputs["rel_bias"], "final_norm_w": np.asarray(inputs["final_norm_w"]).reshape(1, D),
        "consts": consts, "cblk": cblk,
    }
    shared = {k: np.ascontiguousarray(v, dtype=np.float32) for k, v in shared.items()}
    ncores = 1 if dbg else 4
    in_maps = []
    for c in range(ncores):
        m = dict(shared)
        m["x"] = x[c % 4]
        in_maps.append(m)
    res = run_bass_kernel_spmd(nc, in_maps, core_ids=list(range(ncores)))
    if dbg:
        return res
    return np.stack([np.asarray(res.results[b]["y"]) for b in range(4)], axis=0).astype(np.float32)
```
